# Optimizing a Trainium2 kernel written in Bass

```python
import math
import jax
import jax.numpy as jnp
from jax import lax
import numpy as np

D_MODEL = 2048
BATCH = 2
SEQ = 4096
DEPTH = 4

GRID_W = 64
CTX_LEN = 256
N_MIXERS = 3
RMS_EPS = 1e-6
SSM_EXPAND = 2
SSM_INNER = SSM_EXPAND * D_MODEL
SSM_HEAD_DIM = 64
SSM_HEADS = SSM_INNER // SSM_HEAD_DIM
SSM_STATE = 128
SSM_GROUPS = 8
SSM_CONV_W = 3
SSM_CHUNK = 128
SSM_CONV_DIM = SSM_INNER + 2 * SSM_GROUPS * SSM_STATE
SSM_IN_DIM = SSM_INNER + SSM_CONV_DIM + 2 * SSM_HEADS
HG_KEY_DIM = 128
HG_HEADS = D_MODEL // HG_KEY_DIM
HG_VAL_DIM = D_MODEL // HG_HEADS
HG_KDIM = HG_HEADS * HG_KEY_DIM
HG_IN_DIM = 3 * HG_KDIM + 2 * D_MODEL
HG_CHUNK = 64
NA_HEAD_DIM = 128
NA_HEADS = D_MODEL // NA_HEAD_DIM
NA_WIN_R = 8
NA_WIN_C = 16
ROPE_BASE = 10000.0
FFN_HIDDEN = 5632
FFN_CONV_W = 3

kernel_name = 'hybrid_ssd_hgrn2_natten_convffn_trunk'


def rms_norm(x, g):
    xf = x.astype(jnp.float32)
    y = xf * lax.rsqrt(jnp.mean(xf * xf, axis=-1, keepdims=True) + RMS_EPS)
    return (y * g).astype(x.dtype)


def dwconv_centred(x, w, b):
    k_w = w.shape[0]
    left = k_w // 2
    length = x.shape[1]
    xp = jnp.pad(x, ((0, 0), (left, k_w - 1 - left), (0, 0)))
    y = b
    for j in range(k_w):
        y = y + xp[:, j:j + length] * w[j]
    return y


def _rev(t, rev):
    return jnp.flip(t, axis=1) if rev else t


def _adaln(cvec, w, b):
    return jnp.split(jax.nn.silu(cvec) @ w + b, 6, axis=-1)


def ssd_chunked(x, dt, a, b_in, c_in, s0):
    bsz, length, n_h, p_dim = x.shape
    n_g, n_s = b_in.shape[2], b_in.shape[3]
    hg, nc, q_len = n_h // n_g, length // SSM_CHUNK, SSM_CHUNK
    xc = x.reshape(bsz, nc, q_len, n_g, hg, p_dim)
    dtc = dt.reshape(bsz, nc, q_len, n_g, hg)
    bc = b_in.reshape(bsz, nc, q_len, n_g, n_s)
    cc = c_in.reshape(bsz, nc, q_len, n_g, n_s)
    acum = jnp.cumsum(dtc * a.reshape(n_g, hg), axis=2)
    lower = jnp.tril(jnp.ones((q_len, q_len), dtype=bool))
    seg = acum[:, :, :, None] - acum[:, :, None]
    decay = jnp.exp(jnp.where(lower[:, :, None, None], seg, -jnp.inf))
    cb = jnp.einsum('bcign,bcjgn->bcijg', cc, bc)
    scores = cb[..., None] * decay * dtc[:, :, None]
    y_intra = jnp.einsum('bcijgh,bcjghp->bcighp', scores, xc)
    to_end = jnp.exp(acum[:, :, -1:] - acum) * dtc
    chunk_states = jnp.einsum('bcqgn,bcqgh,bcqghp->bcghpn', bc, to_end, xc)
    chunk_decay = jnp.exp(acum[:, :, -1])

    def step(s, inp):
        dec, st = inp
        return (dec[..., None, None] * s + st).astype(s.dtype), s

    s_fin, s_in = lax.scan(step, s0, (jnp.moveaxis(chunk_decay, 1, 0), jnp.moveaxis(chunk_states, 1, 0)))
    s_in = jnp.moveaxis(s_in, 0, 1)
    y_inter = jnp.einsum('bcqgn,bcghpn->bcqghp', cc, s_in) * jnp.exp(acum)[..., None]
    return (y_intra + y_inter).reshape(bsz, length, n_h, p_dim), s_fin


def _ssd_inputs(h, w_in, conv_w, conv_b):
    bsz, length, _ = h.shape
    proj = h @ w_in
    z = proj[..., :SSM_INNER]
    xbc = proj[..., SSM_INNER:SSM_INNER + SSM_CONV_DIM]
    dt_raw = proj[..., SSM_INNER + SSM_CONV_DIM:]
    xbc = jax.nn.silu(dwconv_centred(xbc, conv_w, conv_b))
    gn = SSM_GROUPS * SSM_STATE
    xs = xbc[..., :SSM_INNER].reshape(bsz, length, SSM_HEADS, SSM_HEAD_DIM)
    b_in = xbc[..., SSM_INNER:SSM_INNER + gn].reshape(bsz, length, SSM_GROUPS, SSM_STATE)
    c_in = xbc[..., SSM_INNER + gn:].reshape(bsz, length, SSM_GROUPS, SSM_STATE)
    return z, xs, b_in, c_in, dt_raw


def mamba2_mixer(h_ctx, h_lat, w_in, conv_w, conv_b, dt_bias, a_log, d_skip, norm_g, w_out, ctx_out):
    zc, xc, bc, cc, dtc_raw = _ssd_inputs(h_ctx, w_in, conv_w, conv_b)
    zl, xl, bl, cl, dtl_raw = _ssd_inputs(h_lat, w_in, conv_w, conv_b)
    bsz = h_lat.shape[0]
    yc = jnp.zeros_like(xc)
    yl = jnp.zeros_like(xl)
    for d in range(2):
        rev = d == 1
        a = -jnp.exp(a_log[d])
        cols = slice(d * SSM_HEADS, (d + 1) * SSM_HEADS)
        s0 = jnp.zeros((bsz, SSM_GROUPS, SSM_HEADS // SSM_GROUPS, SSM_HEAD_DIM, SSM_STATE), xl.dtype)
        dt_c = jax.nn.softplus(dtc_raw[..., cols] + dt_bias[d])
        y_d, s_ctx = ssd_chunked(_rev(xc, rev), _rev(dt_c, rev), a, _rev(bc, rev), _rev(cc, rev), s0)
        yc = yc + _rev(y_d, rev) + d_skip[d][:, None] * xc
        dt_l = jax.nn.softplus(dtl_raw[..., cols] + dt_bias[d])
        y_d, _ = ssd_chunked(_rev(xl, rev), _rev(dt_l, rev), a, _rev(bl, rev), _rev(cl, rev), s_ctx)
        yl = yl + _rev(y_d, rev) + d_skip[d][:, None] * xl

    def gate_out(y, z):
        b_, length = y.shape[:2]
        return rms_norm(y.reshape(b_, length, SSM_INNER) * jax.nn.silu(z), norm_g) @ w_out

    return (gate_out(yc, zc) if ctx_out else None), gate_out(yl, zl)


def hgrn2_lower_bounds(lb_param):
    cs = jnp.cumsum(jax.nn.softmax(lb_param.astype(jnp.float32), axis=1), axis=1)
    return cs - cs[:, :1]


def gla_chunked(q, k, v, logf, s0):
    bsz, length, n_h, _ = q.shape
    nc, q_len = length // HG_CHUNK, HG_CHUNK

    def chunks(t):
        return t.reshape(bsz, nc, q_len, n_h, t.shape[-1]).transpose(1, 0, 3, 2, 4)

    lower = jnp.tril(jnp.ones((q_len, q_len), dtype=bool))

    def step(s, inp):
        qc, kc, vc, gc = inp
        bcum = jnp.cumsum(gc, axis=2)
        seg = bcum[:, :, :, None, :] - bcum[:, :, None, :, :]
        decay = jnp.exp(jnp.where(lower[:, :, None], seg, -jnp.inf))
        attn = jnp.einsum('bhic,bhjc,bhijc->bhij', qc, kc, decay)
        o = jnp.einsum('bhij,bhjv->bhiv', attn, vc) + jnp.einsum('bhic,bhcv->bhiv', qc * jnp.exp(bcum), s)
        blast = bcum[:, :, -1:]
        s_new = jnp.exp(blast[:, :, 0, :, None]) * s + jnp.einsum('bhjc,bhjv->bhcv', kc * jnp.exp(blast - bcum), vc)
        return s_new.astype(s.dtype), o

    s_fin, o = lax.scan(step, s0, (chunks(q), chunks(k), chunks(v), chunks(logf)))
    return o.transpose(1, 0, 3, 2, 4).reshape(bsz, length, n_h, v.shape[-1]), s_fin


def _log_forget(f, lb):
    return jnp.logaddexp(jnp.log(lb), jnp.log1p(-lb) + jax.nn.log_sigmoid(f.astype(jnp.float32)))


def _hgrn2_inputs(h, w_in, lb):
    bsz, length, _ = h.shape
    q, v, f_fw, f_bw, g = jnp.split(
        h @ w_in, [HG_KDIM, HG_KDIM + D_MODEL, 2 * HG_KDIM + D_MODEL, 3 * HG_KDIM + D_MODEL], axis=-1)

    def heads(t):
        return t.reshape(bsz, length, HG_HEADS, -1)

    logf = [heads(_log_forget(f, lb[d]).astype(h.dtype)) for d, f in enumerate((f_fw, f_bw))]
    return heads(jax.nn.silu(q)), heads(v), logf, g


def hgrn2_mixer(h_ctx, h_lat, w_in, lb, norm_g, w_out, ctx_out):
    qc, vc, logf_c, gc = _hgrn2_inputs(h_ctx, w_in, lb)
    ql, vl, logf_l, gl = _hgrn2_inputs(h_lat, w_in, lb)
    bsz = h_lat.shape[0]
    oc = jnp.zeros_like(vc)
    ol = jnp.zeros_like(vl)
    for d in range(2):
        rev = d == 1
        s0 = jnp.zeros((bsz, HG_HEADS, HG_KEY_DIM, HG_VAL_DIM), h_lat.dtype)
        kc = -jnp.expm1(logf_c[d])
        o_d, s_ctx = gla_chunked(_rev(qc, rev), _rev(kc, rev), _rev(vc, rev), _rev(logf_c[d], rev), s0)
        oc = oc + _rev(o_d, rev)
        kl = -jnp.expm1(logf_l[d])
        o_d, _ = gla_chunked(_rev(ql, rev), _rev(kl, rev), _rev(vl, rev), _rev(logf_l[d], rev), s_ctx)
        ol = ol + _rev(o_d, rev)

    def readout(o, g):
        b_, length = o.shape[:2]
        o = rms_norm(o, norm_g.reshape(HG_HEADS, HG_VAL_DIM)).reshape(b_, length, D_MODEL)
        return (o * jax.nn.silu(g)) @ w_out

    return (readout(oc, gc) if ctx_out else None), readout(ol, gl)


def _rope_1d(v, pos):
    half = v.shape[-1] // 2
    inv = ROPE_BASE ** (-jnp.arange(half, dtype=jnp.float32) / half)
    ang = pos.astype(jnp.float32)[:, None] * inv[None]
    cos, sin = jnp.cos(ang)[:, None, :], jnp.sin(ang)[:, None, :]
    v1, v2 = v[..., :half], v[..., half:]
    return jnp.concatenate([v1 * cos - v2 * sin, v2 * cos + v1 * sin], axis=-1).astype(v.dtype)


def axial_rope(v, row, col):
    half = v.shape[-1] // 2
    return jnp.concatenate([_rope_1d(v[..., :half], row), _rope_1d(v[..., half:], col)], axis=-1)


def na_mixer(h_ctx, h_lat, w_qkv, rpb, w_out, ctx_out):
    bsz, length, _ = h_lat.shape
    rows = length // GRID_W
    win_r = min(NA_WIN_R, rows)
    scale = NA_HEAD_DIM ** -0.5

    def heads(t):
        return t.reshape(t.shape[0], t.shape[1], NA_HEADS, NA_HEAD_DIM)

    q, k, v = [heads(t) for t in jnp.split(h_lat @ w_qkv, 3, axis=-1)]
    qc, kc, vc = [heads(t) for t in jnp.split(h_ctx @ w_qkv, 3, axis=-1)]
    t_idx = jnp.arange(length)
    row, col = t_idx // GRID_W, t_idx % GRID_W
    q = axial_rope(q, row, col)
    k = axial_rope(k, row, col)

    qg = q.reshape(bsz, rows, GRID_W, NA_HEADS, NA_HEAD_DIM)
    kg = k.reshape(bsz, rows, GRID_W, NA_HEADS, NA_HEAD_DIM)
    vg = v.reshape(bsz, rows, GRID_W, NA_HEADS, NA_HEAD_DIM)
    r_idx = jnp.arange(rows)
    r0 = jnp.clip(r_idx - win_r // 2, 0, rows - win_r)
    key_rows = r0[:, None] + jnp.arange(win_r)[None]
    k_win = kg[:, key_rows]
    v_win = vg[:, key_rows]
    c_idx = jnp.arange(GRID_W)
    c0 = jnp.clip(c_idx - NA_WIN_C // 2, 0, GRID_W - NA_WIN_C)
    col_in = (c_idx[None] >= c0[:, None]) & (c_idx[None] < c0[:, None] + NA_WIN_C)
    dr = key_rows - r_idx[:, None] + NA_WIN_R - 1
    dc = jnp.clip(c_idx[None] - c_idx[:, None] + NA_WIN_C - 1, 0, 2 * NA_WIN_C - 2)
    bias = rpb[:, dr[:, None, :, None], dc[None, :, None, :]]
    bias = jnp.where(col_in[None, None, :, None, :], bias, -jnp.inf)

    s_loc = jnp.einsum('brqhd,brwkhd->bhrqwk', qg, k_win) * scale + bias[None]
    s_ctx = jnp.einsum('brqhd,bkhd->bhrqk', qg, kc) * scale
    n_loc = win_r * GRID_W
    s_all = jnp.concatenate([s_loc.reshape(bsz, NA_HEADS, rows, GRID_W, n_loc), s_ctx], axis=-1)
    p = jax.nn.softmax(s_all.astype(jnp.float32), axis=-1).astype(v.dtype)
    p_loc = p[..., :n_loc].reshape(bsz, NA_HEADS, rows, GRID_W, win_r, GRID_W)
    p_ctx = p[..., n_loc:]
    o = jnp.einsum('bhrqwk,brwkhd->brqhd', p_loc, v_win) + jnp.einsum('bhrqk,bkhd->brqhd', p_ctx, vc)
    y_lat = o.reshape(bsz, length, D_MODEL) @ w_out

    y_ctx = None
    if ctx_out:
        sc = jnp.einsum('bqhd,bkhd->bhqk', qc, kc) * scale
        pc = jax.nn.softmax(sc.astype(jnp.float32), axis=-1).astype(vc.dtype)
        oc = jnp.einsum('bhqk,bkhd->bqhd', pc, vc)
        y_ctx = oc.reshape(bsz, oc.shape[1], D_MODEL) @ w_out
    return y_ctx, y_lat


def conv_ffn(h, w_up, conv_w, conv_b, w_down):
    u = dwconv_centred(h @ w_up, conv_w, conv_b)
    a, v = jnp.split(u, 2, axis=-1)
    return (jax.nn.silu(a) * v) @ w_down


def setup_inputs(seed: int = 0) -> dict:
    key = jax.random.key(seed)
    ks = iter(jax.random.split(key, 32))
    f32 = jnp.float32
    dm = D_MODEL

    def nrm(shape, scale):
        return jax.random.normal(next(ks), shape, f32) * scale

    def gain(shape):
        return 1.0 + nrm(shape, 0.05)

    n_a = len(range(0, DEPTH, N_MIXERS))
    n_b = len(range(1, DEPTH, N_MIXERS))
    n_c = len(range(2, DEPTH, N_MIXERS))
    dt0 = jnp.exp(jax.random.uniform(next(ks), (n_a, 2, SSM_HEADS), f32, math.log(1e-3), math.log(1e-1)))
    dt_bias = dt0 + jnp.log(-jnp.expm1(-dt0))
    a_log = jnp.log(jax.random.uniform(next(ks), (n_a, 2, SSM_HEADS), f32, 1.0, 16.0))
    return {
        'x': nrm((BATCH, SEQ, dm), 1.0),
        'c': nrm((BATCH, dm), 1.0),
        'ctx': nrm((BATCH, CTX_LEN, dm), 1.0),
        'c_ctx': nrm((dm,), 1.0),
        'w_mod': nrm((DEPTH, dm, 6 * dm), 0.5 * dm ** -0.5),
        'b_mod': nrm((DEPTH, 6 * dm), 0.02),
        'norm_g': gain((DEPTH, 4, dm)),
        'ffn_w_up': nrm((DEPTH, dm, 2 * FFN_HIDDEN), dm ** -0.5),
        'ffn_conv_w': nrm((DEPTH, FFN_CONV_W, 2 * FFN_HIDDEN), FFN_CONV_W ** -0.5),
        'ffn_conv_b': nrm((DEPTH, 2 * FFN_HIDDEN), 0.02),
        'ffn_w_down': nrm((DEPTH, FFN_HIDDEN, dm), FFN_HIDDEN ** -0.5),
        'ssm_w_in': nrm((n_a, dm, SSM_IN_DIM), dm ** -0.5),
        'ssm_conv_w': nrm((n_a, SSM_CONV_W, SSM_CONV_DIM), SSM_CONV_W ** -0.5),
        'ssm_conv_b': nrm((n_a, SSM_CONV_DIM), 0.02),
        'ssm_dt_bias': dt_bias,
        'ssm_a_log': a_log,
        'ssm_d': 1.0 + nrm((n_a, 2, SSM_HEADS), 0.1),
        'ssm_norm_g': gain((n_a, SSM_INNER)),
        'ssm_w_out': nrm((n_a, SSM_INNER, dm), SSM_INNER ** -0.5),
        'hg_w_in': nrm((n_b, dm, HG_IN_DIM), dm ** -0.5),
        'hg_lb': 1.0 + nrm((2, DEPTH, HG_KDIM), 0.1),
        'hg_norm_g': gain((n_b, dm)),
        'hg_w_out': nrm((n_b, dm, dm), dm ** -0.5),
        'na_w_qkv': nrm((n_c, dm, 3 * dm), dm ** -0.5),
        'na_rpb': nrm((n_c, NA_HEADS, 2 * NA_WIN_R - 1, 2 * NA_WIN_C - 1), 0.02),
        'na_w_out': nrm((n_c, dm, dm), dm ** -0.5),
    }


def reference(x, c, ctx, c_ctx, w_mod, b_mod, norm_g, ffn_w_up, ffn_conv_w, ffn_conv_b, ffn_w_down,
              ssm_w_in, ssm_conv_w, ssm_conv_b, ssm_dt_bias, ssm_a_log, ssm_d, ssm_norm_g, ssm_w_out,
              hg_w_in, hg_lb, hg_norm_g, hg_w_out, na_w_qkv, na_rpb, na_w_out):
    lower_bounds = hgrn2_lower_bounds(hg_lb)
    x_lat, x_ctx = x, ctx
    for i in range(DEPTH):
        last = i == DEPTH - 1
        kind, slot = i % N_MIXERS, i // N_MIXERS
        sh, sc, gt, fsh, fsc, fgt = [m[:, None, :] for m in _adaln(c, w_mod[i], b_mod[i])]
        csh, csc, cgt, cfsh, cfsc, cfgt = _adaln(c_ctx, w_mod[i], b_mod[i])
        h_lat = rms_norm(x_lat, norm_g[i, 0]) * (1.0 + sc) + sh
        h_ctx = rms_norm(x_ctx, norm_g[i, 0]) * (1.0 + csc) + csh
        if kind == 0:
            y_ctx, y_lat = mamba2_mixer(h_ctx, h_lat, ssm_w_in[slot], ssm_conv_w[slot], ssm_conv_b[slot],
                                        ssm_dt_bias[slot], ssm_a_log[slot], ssm_d[slot], ssm_norm_g[slot],
                                        ssm_w_out[slot], not last)
        elif kind == 1:
            y_ctx, y_lat = hgrn2_mixer(h_ctx, h_lat, hg_w_in[slot], lower_bounds[:, i], hg_norm_g[slot],
                                       hg_w_out[slot], not last)
        else:
            y_ctx, y_lat = na_mixer(h_ctx, h_lat, na_w_qkv[slot], na_rpb[slot], na_w_out[slot], not last)
        x_lat = x_lat + gt * rms_norm(y_lat, norm_g[i, 1])
        h_lat = rms_norm(x_lat, norm_g[i, 2]) * (1.0 + fsc) + fsh
        x_lat = x_lat + fgt * rms_norm(conv_ffn(h_lat, ffn_w_up[i], ffn_conv_w[i], ffn_conv_b[i], ffn_w_down[i]),
                                       norm_g[i, 3])
        if not last:
            x_ctx = x_ctx + cgt * rms_norm(y_ctx, norm_g[i, 1])
            h_ctx = rms_norm(x_ctx, norm_g[i, 2]) * (1.0 + cfsc) + cfsh
            x_ctx = x_ctx + cfgt * rms_norm(conv_ffn(h_ctx, ffn_w_up[i], ffn_conv_w[i], ffn_conv_b[i],
                                                     ffn_w_down[i]), norm_g[i, 3])
    return x_lat
```

```python
import numpy as np
from contextlib import ExitStack
import concourse.bass as bass
import concourse.mybir as mybir
from concourse.bass_utils import run_bass_kernel_spmd

F32 = mybir.dt.float32
BF16 = mybir.dt.bfloat16
AF = mybir.ActivationFunctionType
ALU = mybir.AluOpType
AX = mybir.AxisListType

SAME_ENGINE_SYNC = True


class Prog:
    ENGINES = ("tensor", "vector", "scalar", "gpsimd", "sync")

    def __init__(self):
        self.nc = bass.Bass("TRN2", target_bir_lowering=False)
        self.ops = []
        self.stack = ExitStack()
        self.stack.enter_context(self.nc.allow_low_precision("bf16 matmul operands, fp32 accumulation"))
        self.n_names = 0

    def dram_in(self, name, shape, dtype=F32):
        return self.nc.dram_tensor(name, list(shape), dtype, kind="ExternalInput").ap()

    def dram_out(self, name, shape, dtype=F32):
        return self.nc.dram_tensor(name, list(shape), dtype, kind="ExternalOutput").ap()

    def sbuf(self, shape, dtype=F32, name=None):
        self.n_names += 1
        name = name or f"sb{self.n_names}"
        return self.stack.enter_context(self.nc.sbuf_tensor(name, list(shape), dtype))

    def psum(self, shape, dtype=F32, name=None):
        self.n_names += 1
        name = name or f"ps{self.n_names}"
        return self.stack.enter_context(self.nc.psum_tensor(name, list(shape), dtype))

    def op(self, engine, fn, reads=(), writes=(), signal=True):
        self.ops.append(dict(engine=engine, fn=fn, reads=list(reads), writes=list(writes),
                             kind="c", signal=signal))

    def dma(self, queue, out, in_, reads=(), writes=(), key=None, **kw):
        if key is None:
            key = ("dma", writes[0] if writes else reads[0])
        self.ops.append(dict(engine=queue, fn=lambda e: e.dma_start(out=out, in_=in_, **kw),
                             reads=list(reads), writes=list(writes), kind="d", key=key, signal=True))

    def build(self):
        nc = self.nc
        ops = self.ops
        eng_count = {e: 0 for e in self.ENGINES}
        dma_count = {}
        per_eng = {e: [] for e in self.ENGINES}
        for i, o in enumerate(ops):
            o["idx"] = i
            per_eng[o["engine"]].append(o)
        for e, lst in per_eng.items():
            c = 0
            pend = []
            for o in lst:
                if o["kind"] == "d":
                    k = o["key"]
                    dma_count[k] = dma_count.get(k, 0) + 16
                    o["token"] = (("D", k), dma_count[k])
                    o["tok_idx"] = o["idx"]
                else:
                    if o["signal"]:
                        c += 1
                        o["token"] = (("E", e), c)
                        o["tok_idx"] = o["idx"]
                        for p in pend:
                            p["token"] = (("E", e), c)
                            p["tok_idx"] = o["idx"]
                        pend = []
                    else:
                        pend.append(o)
            assert not pend, "last op on engine must signal"
        last_w = {}
        readers = {}
        for o in ops:
            deps = {}

            def add(src, same_ok):
                s, v = src["token"]
                if src["engine"] != o["engine"]:
                    assert src["tok_idx"] < o["idx"], ("forward dependency (deadlock)", src["idx"], o["idx"])
                deps[s] = max(deps.get(s, 0), v)

            for k in o["reads"]:
                if k in last_w:
                    add(last_w[k], False)
            for k in o["writes"]:
                if k in last_w:
                    add(last_w[k], True)
                for r in readers.get(k, ()):
                    add(r, True)
            o["deps"] = deps
            for k in o["reads"]:
                readers.setdefault(k, []).append(o)
            for k in o["writes"]:
                last_w[k] = o
                readers[k] = []
        sem_keys = []
        for o in ops:
            s = o["token"][0]
            if s not in sem_keys:
                sem_keys.append(s)
        self.n_sems = len(sem_keys)
        sems = {}
        for i, s in enumerate(sem_keys):
            sems[s] = self.stack.enter_context(nc.semaphore(f"s{i}"))
        block = self.stack.enter_context(nc.Block())
        final_dma = {}
        for o in ops:
            if o["kind"] == "d":
                final_dma.setdefault(o["engine"], {})[o["token"][0]] = o["token"][1]

        def make(e, lst):
            def body(eng):
                seen = {}
                own = ("E", e)
                for o in lst:
                    for s, v in o["deps"].items():
                        if s == own:
                            if not SAME_ENGINE_SYNC or e == "tensor":
                                continue
                            if v >= o["token"][1] and o["kind"] == "c":
                                continue
                        if seen.get(s, 0) >= v:
                            continue
                        eng.wait_ge(sems[s], v)
                        seen[s] = v
                    ins = o["fn"](eng)
                    if o["kind"] == "d":
                        ins.then_inc(sems[o["token"][0]], 16)
                    elif o["signal"]:
                        ins.then_inc(sems[o["token"][0]], 1)
                for s, v in final_dma.get(e, {}).items():
                    if seen.get(s, 0) < v:
                        eng.wait_ge(sems[s], v)
            return body

        for e in self.ENGINES:
            lst = per_eng[e]
            if not lst:
                continue
            getattr(block, e)(make(e, lst))
        self.stack.close()
        return nc


def run(prog_nc, in_maps, n=8, trace=False):
    return run_bass_kernel_spmd(prog_nc, in_maps, core_ids=list(range(n)), trace=trace)


D_MODEL = 2048
KC = D_MODEL // 128
NT = 1092
TTS = [(0, 364), (364, 728), (728, 1092)]
SEGS = [(0, 66), (66, 1092)]
HALO_COLS = [0, 65, 66, 1091]
RMS_EPS = 1e-6


class Rot:
    def __init__(self, P, n, shape, dtype, name):
        self.bufs = [P.sbuf(shape, dtype, name=f"{name}{i}") for i in range(n)]
        self.name = name
        self.i = 0

    def next(self):
        j = self.i % len(self.bufs)
        self.i += 1
        return self.bufs[j], (self.name, j)


class PsRot:
    def __init__(self, P, n, name="pt", cols=512):
        self.bufs = [P.psum([128, cols], F32, name=f"{name}{i}") for i in range(n)]
        self.name = name
        self.i = 0

    def next(self):
        j = self.i % len(self.bufs)
        self.i += 1
        return self.bufs[j], (self.name, j)


def load_small(P, name, shape, queue="sync"):
    d = P.dram_in(name, shape)
    t = P.sbuf(shape, F32, name=name + "_sb")
    P.dma(queue, t[:], d, writes=[name])
    return t


def load_w(P, W, col0, ncols, rot, queue="gpsimd"):
    wt, wk = rot.next()
    P.dma(queue, wt[:, :, 0:ncols], W[:, col0:col0 + ncols].rearrange("(kc p) n -> p kc n", p=128),
          writes=[wk])
    return wt, wk


def make_ones(P):
    ones = P.sbuf([128, 128], BF16, name="ones")
    P.op("vector", lambda e: e.memset(ones[:], 1.0), writes=["ones"])
    return ones


def rms_stats_begin(P, n=3, name="st"):
    return [P.psum([128, 512], F32, name=f"{name}{i}") for i in range(n)]


def rms_accum(P, ones, st, st_name, src, src_key, sqrot, first, last, ncols=NT, tts=TTS):
    sq, sqk = sqrot.next()
    P.op("scalar", lambda e: e.activation(sq[:, 0:ncols], src, AF.Square), reads=[src_key], writes=[sqk])
    for ti, (c0, c1) in enumerate(tts):
        P.op("tensor", lambda e, ti=ti, c0=c0, c1=c1: e.matmul(st[ti][:, 0:c1 - c0], ones[:], sq[:, c0:c1],
                                                                 start=first, stop=last),
             reads=[sqk, "ones"], writes=[(st_name, ti)], signal=True)


def rms_finish(P, st, st_name, rstd, rstd_key, dim, tts=TTS):
    for ti, (c0, c1) in enumerate(tts):
        P.op("vector", lambda e, ti=ti, c0=c0, c1=c1: e.tensor_scalar(
            rstd[:, c0:c1], st[ti][:, 0:c1 - c0], 1.0 / dim, RMS_EPS, ALU.mult, ALU.add),
            reads=[(st_name, ti)], writes=[(rstd_key, ti)])
        P.op("scalar", lambda e, c0=c0, c1=c1: e.activation(rstd[:, c0:c1], rstd[:, c0:c1], AF.Sqrt),
             reads=[(rstd_key, ti)], writes=[(rstd_key, ti)])
        P.op("vector", lambda e, c0=c0, c1=c1: e.reciprocal(rstd[:, c0:c1], rstd[:, c0:c1]),
             reads=[(rstd_key, ti)], writes=[(rstd_key, ti)])


def rstd_keys(rstd_key):
    return [(rstd_key, i) for i in range(3)]


def norm_mod(P, xT, ones, g_ap, sc_t, sh_t, hT, hm, sts, xrot, sqrot, tmprot, rstd):
    s1 = P.sbuf([128, 2, KC], F32, name="nm_s1%d" % P.n_names)
    for s in range(2):
        P.op("vector", lambda e, s=s: e.scalar_tensor_tensor(s1[:, s, :], sc_t[:, s, :], 1.0, g_ap,
                                                            ALU.add, ALU.mult),
             reads=["mod", "ng"], writes=[("s1", id(s1))])
    for kc in range(KC):
        xb, xk = xrot.next()
        P.dma("sync", xb[:], xT[kc * 128:(kc + 1) * 128, :], writes=[xk])
        rms_accum(P, ones, sts, "st", xb[:], xk, sqrot, kc == 0, kc == KC - 1)
    rms_finish(P, sts, "st", rstd, "rstd", D_MODEL)
    for kc in range(KC):
        xb, xk = xrot.next()
        P.dma("sync", xb[:], xT[kc * 128:(kc + 1) * 128, :], writes=[xk])
        tb, tk = tmprot.next()
        P.op("vector", lambda e, xb=xb, tb=tb: e.tensor_tensor(tb[:], xb[:], rstd[:], ALU.mult),
             reads=[xk] + rstd_keys("rstd"), writes=[tk])
        for s, (c0, c1) in enumerate(SEGS):
            P.op("scalar", lambda e, tb=tb, kc=kc, s=s, c0=c0, c1=c1: e.activation(
                hT[:, kc, c0:c1], tb[:, c0:c1], AF.Identity, bias=sh_t[:, s, kc:kc + 1],
                scale=s1[:, s, kc:kc + 1]),
                reads=[tk, ("s1", id(s1)), "mod"], writes=[("hT", kc)])
    for i, col in enumerate(HALO_COLS):
        P.op("vector", lambda e, i=i, col=col: e.tensor_scalar(
            hT[:, :, col:col + 1], hT[:, :, col:col + 1], hm[:, i:i + 1], None, ALU.mult),
            reads=[("hT", kc) for kc in range(KC)] + ["hm"], writes=[("hT", kc) for kc in range(KC)])


def resid_norm(P, fT_d, xT_d, outT_d, rstd2, gate_t, xrot, frot, orot):
    for kc in range(KC):
        xb, xk = xrot.next()
        P.dma("sync", xb[:], xT_d[kc * 128:(kc + 1) * 128, :], writes=[xk])
        fb, fk = frot.next()
        P.dma("sync", fb[:], fT_d[kc * 128:(kc + 1) * 128, :], reads=[("fT_d", kc)], writes=[fk])
        P.op("vector", lambda e, fb=fb: e.tensor_tensor(fb[:], fb[:], rstd2[:], ALU.mult),
             reads=[fk] + rstd_keys("rstd2"), writes=[fk])
        ob, ok = orot.next()
        for s, (c0, c1) in enumerate(SEGS):
            P.op("vector", lambda e, fb=fb, xb=xb, ob=ob, s=s, kc=kc, c0=c0, c1=c1: e.scalar_tensor_tensor(
                ob[:, c0:c1], fb[:, c0:c1], gate_t[:, s, kc:kc + 1], xb[:, c0:c1], ALU.mult, ALU.add),
                reads=[fk, xk, "gate"], writes=[ok])
        P.dma("sync", outT_d[kc * 128:(kc + 1) * 128, :], ob[:], reads=[ok], key=("dma_out", ok))


def make_gate(P, g_ap, gt_t, name):
    gate = P.sbuf([128, 2, KC], F32, name=name)
    for s in range(2):
        P.op("vector", lambda e, s=s: e.tensor_tensor(gate[:, s, :], gt_t[:, s, :], g_ap, ALU.mult),
             reads=["mod", "ng"], writes=["gate"])
    return gate


def load_mod(P):
    mod_d = P.dram_in("mod", [128, 6, 2, KC])
    mod = P.sbuf([128, 6, 2, KC], F32, name="mod_sb")
    P.dma("sync", mod[:], mod_d, writes=["mod"])
    ng = load_small(P, "ng", [128, 4, KC])
    hm = load_small(P, "hm", [128, 4])
    return mod, ng, hm


def build_ffn(FH=5632):
    NJ = FH // 128
    P = Prog()
    xT = P.dram_in("xT", [D_MODEL, NT])
    w_up = P.dram_in("w_up", [D_MODEL, 2 * FH])
    w_down = P.dram_in("w_down", [FH, D_MODEL])
    cw = load_small(P, "cw", [128, 3, 2 * NJ])
    cb = load_small(P, "cb", [128, 2 * NJ])
    mod, ng, hm = load_mod(P)
    outT = P.dram_out("outT", [D_MODEL, NT])
    fT_d = P.dram_out("fT", [D_MODEL, NT])
    ones = make_ones(P)
    hT = P.sbuf([128, KC, NT], BF16, name="hT")
    gT = P.sbuf([128, NJ, NT], BF16, name="gT")
    rstd = P.sbuf([128, NT], F32, name="rstd")
    rstd2 = P.sbuf([128, NT], F32, name="rstd2")
    xrot = Rot(P, 2, [128, NT], F32, "xb")
    frot = xrot
    sqrot = Rot(P, 2, [128, NT], BF16, "sq")
    wuprot = Rot(P, 2, [128, KC, 256], BF16, "wup")
    wdnrot = Rot(P, 2, [128, NJ, 128], BF16, "wdn")
    urot = Rot(P, 2, [128, NT], F32, "u")
    crot = Rot(P, 2, [128, NT], F32, "c")
    sts = rms_stats_begin(P)
    pts = PsRot(P, 5)

    norm_mod(P, xT, ones, ng[:, 2, :], mod[:, 4, :, :], mod[:, 3, :, :], hT, hm, sts, xrot, sqrot, frot, rstd)
    gate = make_gate(P, ng[:, 3, :], mod[:, 5, :, :], "gate3")
    hkeys = [("hT", kc) for kc in range(KC)]

    for j in range(NJ):
        wt, wk = wuprot.next()
        P.dma("gpsimd", wt[:, :, 0:128], w_up[:, j * 128:(j + 1) * 128].rearrange("(kc p) n -> p kc n", p=128),
              writes=[wk], key=("dmaw", wk))
        P.dma("gpsimd", wt[:, :, 128:256],
              w_up[:, FH + j * 128:FH + (j + 1) * 128].rearrange("(kc p) n -> p kc n", p=128),
              writes=[wk], key=("dmaw", wk))
        cs = []
        for half in range(2):
            jj = half * NJ + j
            ub, uk = urot.next()
            for ti, (c0, c1) in enumerate(TTS):
                pt, pk = pts.next()
                for kc in range(KC):
                    P.op("tensor", lambda e, pt=pt, wt=wt, kc=kc, half=half, c0=c0, c1=c1: e.matmul(
                        pt[:, 0:c1 - c0], wt[:, kc, half * 128:(half + 1) * 128], hT[:, kc, c0:c1],
                        start=(kc == 0), stop=(kc == KC - 1)),
                        reads=[wk] + hkeys, writes=[pk], signal=(kc == KC - 1))
                P.op("scalar", lambda e, pt=pt, ub=ub, c0=c0, c1=c1: e.copy(ub[:, c0:c1], pt[:, 0:c1 - c0]),
                     reads=[pk], writes=[(uk, ti)])
            uks = [(uk, ti) for ti in range(3)]
            cbuf, ck = crot.next()
            P.op("scalar", lambda e, ub=ub, cbuf=cbuf, jj=jj: e.activation(
                cbuf[:, 1:NT - 1], ub[:, 1:NT - 1], AF.Identity, bias=cb[:, jj:jj + 1], scale=cw[:, 1, jj:jj + 1]),
                reads=uks + ["cw", "cb"], writes=[ck])
            P.op("vector", lambda e, ub=ub, cbuf=cbuf, jj=jj: e.scalar_tensor_tensor(
                cbuf[:, 1:NT - 1], ub[:, 0:NT - 2], cw[:, 0, jj:jj + 1], cbuf[:, 1:NT - 1], ALU.mult, ALU.add),
                reads=uks + [ck, "cw"], writes=[ck])
            P.op("vector", lambda e, ub=ub, cbuf=cbuf, jj=jj: e.scalar_tensor_tensor(
                cbuf[:, 1:NT - 1], ub[:, 2:NT], cw[:, 2, jj:jj + 1], cbuf[:, 1:NT - 1], ALU.mult, ALU.add),
                reads=uks + [ck, "cw"], writes=[ck])
            cs.append((cbuf, ck))
        (ca, cka), (cv, ckv) = cs
        P.op("scalar", lambda e, ca=ca: e.activation(ca[:, 1:NT - 1], ca[:, 1:NT - 1], AF.Silu),
             reads=[cka], writes=[cka])
        P.op("vector", lambda e, ca=ca, cv=cv, j=j: e.tensor_tensor(
            gT[:, j, 1:NT - 1], ca[:, 1:NT - 1], cv[:, 1:NT - 1], ALU.mult),
            reads=[cka, ckv], writes=[("gT", j)])
    P.op("vector", lambda e: e.memset(gT[:, :, 0:1], 0.0), writes=[("gT", j) for j in range(NJ)],
         reads=[("gT", j) for j in range(NJ)])
    P.op("vector", lambda e: e.memset(gT[:, :, NT - 1:NT], 0.0), writes=[("gT", j) for j in range(NJ)],
         reads=[("gT", j) for j in range(NJ)])
    gkeys = [("gT", j) for j in range(NJ)]

    for d in range(KC):
        wt, wk = wdnrot.next()
        P.dma("gpsimd", wt[:], w_down[:, d * 128:(d + 1) * 128].rearrange("(j p) n -> p j n", p=128),
              writes=[wk], key=("dmaw", wk))
        fb, fk = frot.next()
        for ti, (c0, c1) in enumerate(TTS):
            pt, pk = pts.next()
            for j in range(NJ):
                P.op("tensor", lambda e, pt=pt, wt=wt, j=j, c0=c0, c1=c1: e.matmul(
                    pt[:, 0:c1 - c0], wt[:, j, :], gT[:, j, c0:c1], start=(j == 0), stop=(j == NJ - 1)),
                    reads=[wk] + gkeys, writes=[pk], signal=(j == NJ - 1))
            P.op("scalar", lambda e, pt=pt, fb=fb, c0=c0, c1=c1: e.copy(fb[:, c0:c1], pt[:, 0:c1 - c0]),
                 reads=[pk], writes=[fk])
        P.dma("sync", fT_d[d * 128:(d + 1) * 128, :], fb[:], reads=[fk], writes=[("fT_d", d)],
              key=("dma_out", fk))
        rms_accum(P, ones, sts, "st", fb[:], fk, sqrot, d == 0, d == KC - 1)
    rms_finish(P, sts, "st", rstd2, "rstd2", D_MODEL)
    resid_norm(P, fT_d, xT, outT, rstd2, gate, xrot, frot, urot)
    return P.build()


def build_mod():
    P = Prog()
    wm = P.dram_in("wm", [D_MODEL, 6144])
    bm = load_small(P, "bm", [128, 48])
    cT = load_small(P, "cT", [128, KC, 3])
    modo = P.dram_out("modo", [128, 48, 3])
    sc = P.sbuf([128, KC, 3], BF16, name="silu_c")
    P.op("scalar", lambda e: e.activation(sc[:], cT[:], AF.Silu), reads=["cT"], writes=["sc"])
    res = P.sbuf([128, 48, 3], F32, name="res")
    wrot = Rot(P, 3, [128, KC, 128], BF16, "w")
    pts = PsRot(P, 4)
    for j in range(48):
        wt, wk = load_w(P, wm, j * 128, 128, wrot)
        pt, pk = pts.next()
        for kc in range(KC):
            P.op("tensor", lambda e, pt=pt, wt=wt, kc=kc: e.matmul(pt[:, 0:3], wt[:, kc, :], sc[:, kc, :],
                                                                  start=(kc == 0), stop=(kc == KC - 1)),
                 reads=[wk, "sc"], writes=[pk], signal=(kc == KC - 1))
        P.op("scalar", lambda e, pt=pt, j=j: e.activation(res[:, j, :], pt[:, 0:3], AF.Identity,
                                                         bias=bm[:, j:j + 1]),
             reads=[pk, "bm"], writes=["res"])
    P.dma("sync", modo, res[:], reads=["res"])
    return P.build()


def build_ts1(kind, layer_idx=1):
    N = {"na": 6144, "hg": 10240, "ssd": 10368}[kind]
    NOUT = {"na": 6144, "hg": 10240, "ssd": 10368 + 128}[kind]
    P = Prog()
    xT = P.dram_in("xT", [D_MODEL, NT])
    W = P.dram_in("W", [D_MODEL, N])
    mod, ng, hm = load_mod(P)
    outT = P.dram_out("outT", [NOUT, NT])
    ones = make_ones(P)
    hT = P.sbuf([128, KC, NT], BF16, name="hT")
    rstd = P.sbuf([128, NT], F32, name="rstd")
    xrot = Rot(P, 3, [128, NT], F32, "xb")
    sqrot = Rot(P, 2, [128, NT], BF16, "sq")
    tmprot = Rot(P, 2, [128, NT], F32, "tmp")
    wrot = Rot(P, 3, [128, KC, 128], BF16, "w")
    srot = Rot(P, 4, [128, NT], F32, "stage")
    sts = rms_stats_begin(P)
    pts = PsRot(P, 5)
    if kind == "ssd":
        cw = load_small(P, "cw", [128, 3, 48])
        cb = load_small(P, "cb", [128, 48])
        dtb = load_small(P, "dtb", [128, 1])
        alog = load_small(P, "alog", [128, 1])
        negA = P.sbuf([128, 1], F32, name="negA")
        P.op("scalar", lambda e: e.activation(negA[:], alog[:], AF.Exp), reads=["alog"], writes=["negA"])
        P.op("vector", lambda e: e.tensor_scalar(negA[:], negA[:], -1.0, None, ALU.mult),
             reads=["negA"], writes=["negA"])
    if kind == "hg":
        lbp = load_small(P, "lbp", [128, 2, 4, KC])
        el = P.sbuf([128, 2, 4, KC], F32, name="el")
        P.op("scalar", lambda e: e.activation(el[:], lbp[:], AF.Exp), reads=["lbp"], writes=["el"])
        ssum = P.sbuf([128, 2, KC], F32, name="ssum")
        num = P.sbuf([128, 2, KC], F32, name="num")
        lb = P.sbuf([128, 2, KC], F32, name="lb")
        oml = P.sbuf([128, 2, KC], F32, name="oml")
        P.op("vector", lambda e: e.tensor_tensor(ssum[:], el[:, :, 0, :], el[:, :, 1, :], ALU.add),
             reads=["el"], writes=["ssum"])
        for l in (2, 3):
            P.op("vector", lambda e, l=l: e.tensor_tensor(ssum[:], ssum[:], el[:, :, l, :], ALU.add),
                 reads=["el", "ssum"], writes=["ssum"])
        P.op("vector", lambda e: e.tensor_copy(num[:], el[:, :, 1, :]), reads=["el"], writes=["num"])
        for l in range(2, layer_idx + 1):
            P.op("vector", lambda e, l=l: e.tensor_tensor(num[:], num[:], el[:, :, l, :], ALU.add),
                 reads=["el", "num"], writes=["num"])
        P.op("vector", lambda e: e.reciprocal(ssum[:], ssum[:]), reads=["ssum"], writes=["ssum"])
        P.op("vector", lambda e: e.tensor_tensor(lb[:], num[:], ssum[:], ALU.mult),
             reads=["num", "ssum"], writes=["lb"])
        P.op("vector", lambda e: e.tensor_scalar(oml[:], lb[:], -1.0, 1.0, ALU.mult, ALU.add),
             reads=["lb"], writes=["oml"])

    norm_mod(P, xT, ones, ng[:, 0, :], mod[:, 1, :, :], mod[:, 0, :, :], hT, hm, sts, xrot, sqrot, tmprot, rstd)
    hkeys = [("hT", kc) for kc in range(KC)]

    def out_dma(row0, sb, sk):
        P.dma("sync", outT[row0:row0 + 128, :], sb[:], reads=[sk], key=("dma_out", sk))

    for n in range(N // 128):
        wt, wk = load_w(P, W, n * 128, 128, wrot)
        pks = []
        for ti, (c0, c1) in enumerate(TTS):
            pt, pk = pts.next()
            for kc in range(KC):
                P.op("tensor", lambda e, pt=pt, wt=wt, kc=kc, c0=c0, c1=c1: e.matmul(
                    pt[:, 0:c1 - c0], wt[:, kc, :], hT[:, kc, c0:c1], start=(kc == 0), stop=(kc == KC - 1)),
                    reads=[wk] + hkeys, writes=[pk], signal=(kc == KC - 1))
            pks.append((pt, pk, c0, c1))
        if kind == "na":
            ep = "copy"
        elif kind == "hg":
            ep = "silu" if (n < 16 or n >= 64) else ("copy" if n < 32 else "hgf")
        else:
            ep = "silu" if n < 32 else ("conv" if n < 80 else "dt")
        sb, sk = srot.next()
        if ep in ("copy", "silu"):
            fn = AF.Copy if ep == "copy" else AF.Silu
            for (pt, pk, c0, c1) in pks:
                P.op("scalar", lambda e, pt=pt, sb=sb, c0=c0, c1=c1, fn=fn: e.activation(
                    sb[:, c0:c1], pt[:, 0:c1 - c0], fn), reads=[pk], writes=[sk])
            out_dma(n * 128, sb, sk)
        elif ep == "hgf":
            d, kc_f = (0, n - 32) if n < 48 else (1, n - 48)
            for (pt, pk, c0, c1) in pks:
                P.op("scalar", lambda e, pt=pt, sb=sb, c0=c0, c1=c1: e.activation(
                    sb[:, c0:c1], pt[:, 0:c1 - c0], AF.Sigmoid), reads=[pk], writes=[sk])
            P.op("vector", lambda e, sb=sb, d=d, kc_f=kc_f: e.tensor_scalar(
                sb[:], sb[:], oml[:, d, kc_f:kc_f + 1], lb[:, d, kc_f:kc_f + 1], ALU.mult, ALU.add),
                reads=[sk, "oml", "lb"], writes=[sk])
            P.op("scalar", lambda e, sb=sb: e.activation(sb[:], sb[:], AF.Ln), reads=[sk], writes=[sk])
            out_dma(n * 128, sb, sk)
        elif ep == "conv":
            jj = n - 32
            for (pt, pk, c0, c1) in pks:
                P.op("scalar", lambda e, pt=pt, sb=sb, c0=c0, c1=c1: e.copy(sb[:, c0:c1], pt[:, 0:c1 - c0]),
                     reads=[pk], writes=[sk])
            cbuf, ck = srot.next()
            P.op("scalar", lambda e, sb=sb, cbuf=cbuf, jj=jj: e.activation(
                cbuf[:, 1:NT - 1], sb[:, 1:NT - 1], AF.Identity, bias=cb[:, jj:jj + 1], scale=cw[:, 1, jj:jj + 1]),
                reads=[sk, "cw", "cb"], writes=[ck])
            P.op("vector", lambda e, sb=sb, cbuf=cbuf, jj=jj: e.scalar_tensor_tensor(
                cbuf[:, 1:NT - 1], sb[:, 0:NT - 2], cw[:, 0, jj:jj + 1], cbuf[:, 1:NT - 1], ALU.mult, ALU.add),
                reads=[sk, ck, "cw"], writes=[ck])
            P.op("vector", lambda e, sb=sb, cbuf=cbuf, jj=jj: e.scalar_tensor_tensor(
                cbuf[:, 1:NT - 1], sb[:, 2:NT], cw[:, 2, jj:jj + 1], cbuf[:, 1:NT - 1], ALU.mult, ALU.add),
                reads=[sk, ck, "cw"], writes=[ck])
            P.op("scalar", lambda e, cbuf=cbuf: e.activation(cbuf[:, 1:NT - 1], cbuf[:, 1:NT - 1], AF.Silu),
                 reads=[ck], writes=[ck])
            P.op("vector", lambda e, cbuf=cbuf: e.memset(cbuf[:, 0:1], 0.0), reads=[ck], writes=[ck])
            P.op("vector", lambda e, cbuf=cbuf: e.memset(cbuf[:, NT - 1:NT], 0.0), reads=[ck], writes=[ck])
            out_dma(n * 128, cbuf, ck)
        else:
            for (pt, pk, c0, c1) in pks:
                P.op("scalar", lambda e, pt=pt, sb=sb, c0=c0, c1=c1: e.activation(
                    sb[:, c0:c1], pt[:, 0:c1 - c0], AF.Exp, bias=dtb[:, 0:1]), reads=[pk, "dtb"], writes=[sk])
            P.op("scalar", lambda e, sb=sb: e.activation(sb[:], sb[:], AF.Ln, bias=1.0), reads=[sk], writes=[sk])
            out_dma(n * 128, sb, sk)
            ab, ak = srot.next()
            P.op("vector", lambda e, sb=sb, ab=ab: e.tensor_scalar(ab[:], sb[:], negA[:, 0:1], None, ALU.mult),
                 reads=[sk, "negA"], writes=[ak])
            out_dma(n * 128 + 128, ab, ak)
    return P.build()


def build_ts2(kind):
    KCI = 32 if kind == "ssd" else 16
    P = Prog()
    xT = P.dram_in("xT", [D_MODEL, NT])
    W = P.dram_in("W", [KCI * 128, D_MODEL])
    mod, ng, hm = load_mod(P)
    outT = P.dram_out("outT", [D_MODEL, NT])
    fT_d = P.nc.dram_tensor("fT_scratch", [D_MODEL, NT], F32).ap()
    ones = make_ones(P)
    actT = P.sbuf([128, KCI, NT], BF16, name="actT")
    rstd2 = P.sbuf([128, NT], F32, name="rstd2")
    arot = Rot(P, 4, [128, NT], F32, "a")
    xrot = Rot(P, 3, [128, NT], F32, "xb")
    sqrot = Rot(P, 2, [128, NT], BF16, "sq")
    wrot = Rot(P, 2, [128, KCI, 128], BF16, "w")
    sts = rms_stats_begin(P)
    pts = PsRot(P, 5)
    akeys = [("actT", kc) for kc in range(KCI)]
    rstd_s = None
    if kind == "na":
        oT = P.dram_in("oT", [D_MODEL, NT])
        for kc in range(KCI):
            ab, ak = arot.next()
            P.dma("sync", ab[:], oT[kc * 128:(kc + 1) * 128, :], writes=[ak])
            P.op("scalar", lambda e, ab=ab, kc=kc: e.copy(actT[:, kc, :], ab[:]), reads=[ak], writes=[("actT", kc)])
    elif kind == "hg":
        oF = P.dram_in("oF", [D_MODEL, NT])
        oB = P.dram_in("oB", [D_MODEL, NT])
        gsT = P.dram_in("gsT", [D_MODEL, NT])
        hgn = load_small(P, "hgn", [128, KC])
        rs = P.sbuf([128, NT], F32, name="rs_h")
        for kc in range(KCI):
            a1, k1 = arot.next()
            a2, k2 = arot.next()
            a3, k3 = arot.next()
            P.dma("sync", a1[:], oF[kc * 128:(kc + 1) * 128, :], writes=[k1])
            P.dma("sync", a2[:], oB[kc * 128:(kc + 1) * 128, :], writes=[k2])
            P.dma("sync", a3[:], gsT[kc * 128:(kc + 1) * 128, :], writes=[k3])
            P.op("vector", lambda e, a1=a1, a2=a2: e.tensor_tensor(a1[:], a1[:], a2[:], ALU.add),
                 reads=[k1, k2], writes=[k1])
            rms_accum(P, ones, sts, "st", a1[:], k1, sqrot, True, True)
            rms_finish(P, sts, "st", rs, "rs", 128)
            P.op("vector", lambda e, a1=a1: e.tensor_tensor(a1[:], a1[:], rs[:], ALU.mult),
                 reads=[k1] + rstd_keys("rs"), writes=[k1])
            P.op("vector", lambda e, a1=a1, a3=a3, kc=kc: e.scalar_tensor_tensor(
                actT[:, kc, :], a1[:], hgn[:, kc:kc + 1], a3[:], ALU.mult, ALU.mult),
                reads=[k1, k3, "hgn"], writes=[("actT", kc)])
    else:
        yF = P.dram_in("yF", [4096, NT])
        yB = P.dram_in("yB", [4096, NT])
        xcT = P.dram_in("xcT", [4096, NT])
        zsT = P.dram_in("zsT", [4096, NT])
        dsk = load_small(P, "dsk", [128, 2, 32])
        sng = load_small(P, "sng", [128, 32])
        dsum = P.sbuf([128, 32], F32, name="dsum")
        P.op("vector", lambda e: e.tensor_tensor(dsum[:], dsk[:, 0, :], dsk[:, 1, :], ALU.add),
             reads=["dsk"], writes=["dsum"])
        rstd_s = P.sbuf([128, NT], F32, name="rstd_s")
        for kc in range(KCI):
            a1, k1 = arot.next()
            a2, k2 = arot.next()
            a3, k3 = arot.next()
            a4, k4 = arot.next()
            P.dma("sync", a1[:], yF[kc * 128:(kc + 1) * 128, :], writes=[k1])
            P.dma("sync", a2[:], yB[kc * 128:(kc + 1) * 128, :], writes=[k2])
            P.dma("sync", a3[:], xcT[kc * 128:(kc + 1) * 128, :], writes=[k3])
            P.dma("sync", a4[:], zsT[kc * 128:(kc + 1) * 128, :], writes=[k4])
            P.op("vector", lambda e, a1=a1, a2=a2: e.tensor_tensor(a1[:], a1[:], a2[:], ALU.add),
                 reads=[k1, k2], writes=[k1])
            P.op("vector", lambda e, a1=a1, a3=a3, kc=kc: e.scalar_tensor_tensor(
                a1[:], a3[:], dsum[:, kc:kc + 1], a1[:], ALU.mult, ALU.add), reads=[k1, k3, "dsum"], writes=[k1])
            P.op("vector", lambda e, a1=a1, a4=a4: e.tensor_tensor(a1[:], a1[:], a4[:], ALU.mult),
                 reads=[k1, k4], writes=[k1])
            rms_accum(P, ones, sts, "st", a1[:], k1, sqrot, kc == 0, kc == KCI - 1)
            P.op("vector", lambda e, a1=a1, kc=kc: e.tensor_scalar(
                actT[:, kc, :], a1[:], sng[:, kc:kc + 1], None, ALU.mult), reads=[k1, "sng"], writes=[("actT", kc)])
        rms_finish(P, sts, "st", rstd_s, "rstd_s", 4096)

    gate = make_gate(P, ng[:, 1, :], mod[:, 2, :, :], "gate1")
    for d in range(KC):
        wt, wk = wrot.next()
        P.dma("gpsimd", wt[:], W[:, d * 128:(d + 1) * 128].rearrange("(j p) n -> p j n", p=128),
              writes=[wk], key=("dmaw", wk))
        fb, fk = xrot.next()
        for ti, (c0, c1) in enumerate(TTS):
            pt, pk = pts.next()
            for j in range(KCI):
                P.op("tensor", lambda e, pt=pt, wt=wt, j=j, c0=c0, c1=c1: e.matmul(
                    pt[:, 0:c1 - c0], wt[:, j, :], actT[:, j, c0:c1], start=(j == 0), stop=(j == KCI - 1)),
                    reads=[wk] + akeys, writes=[pk], signal=(j == KCI - 1))
            if rstd_s is None:
                P.op("scalar", lambda e, pt=pt, fb=fb, c0=c0, c1=c1: e.copy(fb[:, c0:c1], pt[:, 0:c1 - c0]),
                     reads=[pk], writes=[fk])
            else:
                P.op("vector", lambda e, pt=pt, fb=fb, c0=c0, c1=c1: e.tensor_tensor(
                    fb[:, c0:c1], pt[:, 0:c1 - c0], rstd_s[:, c0:c1], ALU.mult),
                    reads=[pk] + rstd_keys("rstd_s"), writes=[fk])
        P.dma("sync", fT_d[d * 128:(d + 1) * 128, :], fb[:], reads=[fk], writes=[("fT_d", d)],
              key=("dma_out", fk))
        rms_accum(P, ones, sts, "st", fb[:], fk, sqrot, d == 0, d == KC - 1)
    rms_finish(P, sts, "st", rstd2, "rstd2", D_MODEL)
    resid_norm(P, fT_d, xT, outT, rstd2, gate, xrot, arot, arot)
    return P.build()


NA_SCALE = 128 ** -0.5


def build_us_na(NU=4):
    P = Prog()
    qT_d = P.dram_in("qT", [NU, 128, 4096])
    kT_d = P.dram_in("kT", [NU, 128, 4096])
    qcT_d = P.dram_in("qcT", [NU, 128, 256])
    kcT_d = P.dram_in("kcT", [NU, 128, 256])
    v_d = P.dram_in("v", [NU, 4352, 128])
    bias_d = P.dram_in("biasT", [NU, 128, 8, 256])
    cos = load_small(P, "cos", [128, 4096])
    sin = load_small(P, "sin", [128, 4096])
    rot_d = P.dram_in("rotT", [128, 128])
    oT_d = P.dram_out("oT", [NU, 128, 4096])
    ocT_d = P.dram_out("ocT", [NU, 128, 256])
    rotb = P.sbuf([128, 128], BF16, name="rotb")
    P.dma("gpsimd", rotb[:], rot_d, writes=["rotb"])
    ones = make_ones(P)
    qrot = Rot(P, 2, [128, 4096], F32, "qf")
    qbrot = Rot(P, 2, [128, 512], BF16, "qb")
    t1rot = Rot(P, 2, [128, 512], F32, "t1")
    t2rot = Rot(P, 2, [128, 512], F32, "t2")
    qr = P.sbuf([128, 4096], BF16, name="qr")
    kr = P.sbuf([128, 4096], BF16, name="kr")
    qc = P.sbuf([128, 256], BF16, name="qc")
    kc = P.sbuf([128, 256], BF16, name="kc")
    ve = P.sbuf([128, 34, 128], BF16, name="ve")
    vo = P.sbuf([128, 32, 128], BF16, name="vo")
    bias = P.sbuf([128, 8, 256], F32, name="bias")
    oT = P.sbuf([128, 4096], F32, name="oT_sb")
    ocT = P.sbuf([128, 256], F32, name="ocT_sb")
    trot = Rot(P, 3, [128, 256], F32, "t")
    erot = Rot(P, 3, [128, 512], BF16, "e")
    rrot = Rot(P, 2, [128, 256], F32, "rec")
    ps_s = PsRot(P, 3, "pS")
    ps_nd = PsRot(P, 3, "pND")
    ps_r = PsRot(P, 2, "pR")

    for u in range(NU):
        P.dma("gpsimd", ve[:], v_d[u].rearrange("(t p) d -> p t d", p=128), writes=["ve"])
        P.dma("gpsimd", vo[:], v_d[u, 64:64 + 4096, :].rearrange("(t p) d -> p t d", p=128), writes=["vo"])
        P.dma("gpsimd", qc[:], qcT_d[u], writes=["qc"])
        P.dma("gpsimd", kc[:], kcT_d[u], writes=["kc"])
        P.dma("sync", bias[:], bias_d[u], writes=["bias"])
        for (src_d, dst, dk) in ((qT_d, qr, "qr"), (kT_d, kr, "kr")):
            qf, qk = qrot.next()
            P.dma("sync", qf[:], src_d[u], writes=[qk])
            for t in range(8):
                sl = slice(t * 512, (t + 1) * 512)
                qb, qbk = qbrot.next()
                P.op("scalar", lambda e, qb=qb, qf=qf, sl=sl: e.copy(qb[:], qf[:, sl]), reads=[qk], writes=[qbk])
                pt, pk = ps_r.next()
                P.op("tensor", lambda e, pt=pt, qb=qb: e.matmul(pt[:], rotb[:], qb[:], start=True, stop=True),
                     reads=[qbk, "rotb"], writes=[pk])
                t1, t1k = t1rot.next()
                t2, t2k = t2rot.next()
                P.op("vector", lambda e, t1=t1, qf=qf, sl=sl: e.tensor_tensor(t1[:], qf[:, sl], cos[:, sl], ALU.mult),
                     reads=[qk, "cos"], writes=[t1k])
                P.op("vector", lambda e, t2=t2, pt=pt, sl=sl: e.tensor_tensor(t2[:], pt[:], sin[:, sl], ALU.mult),
                     reads=[pk, "sin"], writes=[t2k])
                P.op("gpsimd", lambda e, t1=t1, t2=t2, dst=dst, sl=sl: e.tensor_tensor(dst[:, sl], t1[:], t2[:], ALU.add),
                     reads=[t1k, t2k], writes=[(dk, t)])
        qrk = [("qr", t) for t in range(8)]
        krk = [("kr", t) for t in range(8)]
        for r in range(64):
            r0 = min(max(r - 4, 0), 56)
            var = r - r0
            qs = slice(r * 64, (r + 1) * 64)
            ps, psk = ps_s.next()
            for kb in range(4):
                tok0 = (r0 + 2 * kb) * 64
                P.op("tensor", lambda e, ps=ps, kb=kb, tok0=tok0, qs=qs: e.matmul(
                    ps[:, kb * 64:(kb + 1) * 64], kr[:, tok0:tok0 + 128], qr[:, qs], start=True, stop=True),
                    reads=qrk + krk, writes=[psk], signal=False)
            for cb in range(2):
                P.op("tensor", lambda e, ps=ps, cb=cb, qs=qs: e.matmul(
                    ps[:, 256 + cb * 64:256 + (cb + 1) * 64], kc[:, cb * 128:(cb + 1) * 128], qr[:, qs],
                    start=True, stop=True), reads=qrk + ["kc"], writes=[psk], signal=(cb == 1))
            tb, tk = trot.next()
            P.op("vector", lambda e, tb=tb, ps=ps, var=var: e.scalar_tensor_tensor(
                tb[:], ps[:, 0:256], NA_SCALE, bias[:, var, :], ALU.mult, ALU.add),
                reads=[psk, "bias"], writes=[tk])
            eb, ek = erot.next()
            P.op("scalar", lambda e, eb=eb, tb=tb: e.activation(eb[:, 0:256], tb[:], AF.Exp),
                 reads=[tk], writes=[(ek, 0)])
            P.op("scalar", lambda e, eb=eb, ps=ps: e.activation(eb[:, 256:384], ps[:, 256:384], AF.Exp, scale=NA_SCALE),
                 reads=[psk], writes=[(ek, 1)])
            nd, ndk = ps_nd.next()
            for which in range(2):
                for blk in range(6):
                    if which == 0:
                        if blk < 4:
                            tok0 = (r0 + 2 * blk) * 64
                            lhs = ve[:, tok0 // 128, :] if tok0 % 128 == 0 else vo[:, (tok0 - 64) // 128, :]
                        else:
                            lhs = ve[:, 32 + blk - 4, :]
                    else:
                        lhs = ones[:]
                    P.op("tensor", lambda e, nd=nd, lhs=lhs, eb=eb, blk=blk, which=which: e.matmul(
                        nd[:, which * 64:(which + 1) * 64], lhs, eb[:, blk * 64:(blk + 1) * 64],
                        start=(blk == 0), stop=(blk == 5)),
                        reads=[(ek, 0), (ek, 1), "ve", "vo", "ones"], writes=[ndk],
                        signal=(which == 1 and blk == 5))
            rb, rk = rrot.next()
            P.op("vector", lambda e, rb=rb, nd=nd: e.reciprocal(rb[:, 0:64], nd[:, 64:128]), reads=[ndk], writes=[rk])
            P.op("vector", lambda e, rb=rb, nd=nd, qs=qs: e.tensor_tensor(oT[:, qs], nd[:, 0:64], rb[:, 0:64], ALU.mult),
                 reads=[ndk, rk], writes=["oT"])
        P.dma("sync", oT_d[u], oT[:], reads=["oT"], key=("dma_out", "oT"))
        ps, psk = ps_s.next()
        for cb in range(2):
            P.op("tensor", lambda e, ps=ps, cb=cb: e.matmul(
                ps[:, cb * 256:(cb + 1) * 256], kc[:, cb * 128:(cb + 1) * 128], qc[:], start=True, stop=True),
                reads=["qc", "kc"], writes=[psk], signal=(cb == 1))
        eb, ek = erot.next()
        P.op("scalar", lambda e, eb=eb, ps=ps: e.activation(eb[:], ps[:], AF.Exp, scale=NA_SCALE),
             reads=[psk], writes=[(ek, 0), (ek, 1)])
        nd, ndk = ps_nd.next()
        for which in range(2):
            for cb in range(2):
                lhs = ve[:, 32 + cb, :] if which == 0 else ones[:]
                P.op("tensor", lambda e, nd=nd, lhs=lhs, eb=eb, cb=cb, which=which: e.matmul(
                    nd[:, which * 256:(which + 1) * 256], lhs, eb[:, cb * 256:(cb + 1) * 256],
                    start=(cb == 0), stop=(cb == 1)),
                    reads=[(ek, 0), (ek, 1), "ve", "ones"], writes=[ndk], signal=(which == 1 and cb == 1))
        rb, rk = rrot.next()
        P.op("vector", lambda e, rb=rb, nd=nd: e.reciprocal(rb[:], nd[:, 256:512]), reads=[ndk], writes=[rk])
        P.op("vector", lambda e, rb=rb, nd=nd: e.tensor_tensor(ocT[:], nd[:, 0:256], rb[:], ALU.mult),
             reads=[ndk, rk], writes=["ocT"])
        P.dma("sync", ocT_d[u], ocT[:], reads=["ocT"], key=("dma_out", "ocT"))
    return P.build()


_PROGS = {}


def _prog(name, builder, *a):
    key = (name,) + a
    if key not in _PROGS:
        _PROGS[key] = builder(*a)
    return _PROGS[key]


_TRACE = False
_TIMES = []


def _launch(nc, in_maps):
    if _TRACE:
        res = run_bass_kernel_spmd(nc, in_maps, core_ids=list(range(8)), trace=True)
        _TIMES.append(res.exec_time_ns)
        print("LAUNCH exec_time_ns", res.exec_time_ns, flush=True)
    else:
        res = run_bass_kernel_spmd(nc, in_maps, core_ids=list(range(8)))
    return res.results


def _vec(v):
    v = np.asarray(v, np.float32)
    F = v.shape[-1]
    return np.ascontiguousarray(np.moveaxis(v.reshape(v.shape[:-1] + (F // 128, 128)), -1, 0))


def to_ts(lat, ctx):
    out = []
    F = lat.shape[-1]
    for c in range(8):
        b, q = c // 4, c % 4
        a = np.zeros((NT, F), np.float32)
        lo, hi = q * 64 - 1, q * 64 + 65
        s0, s1 = max(lo, 0), min(hi, 256)
        a[s0 - lo:s1 - lo] = ctx[b, s0:s1]
        lo, hi = q * 1024 - 1, q * 1024 + 1025
        s0, s1 = max(lo, 0), min(hi, 4096)
        a[66 + s0 - lo:66 + s1 - lo] = lat[b, s0:s1]
        out.append(np.ascontiguousarray(a.T))
    return out


def from_ts(outs, rows=None):
    F = outs[0].shape[0] if rows is None else rows[1] - rows[0]
    lat = np.empty((2, 4096, F), np.float32)
    ctx = np.empty((2, 256, F), np.float32)
    for c in range(8):
        b, q = c // 4, c % 4
        a = outs[c] if rows is None else outs[c][rows[0]:rows[1]]
        a = a.T
        ctx[b, q * 64:(q + 1) * 64] = a[1:65]
        lat[b, q * 1024:(q + 1) * 1024] = a[67:1091]
    return lat, ctx


def _hm(c):
    q = c % 4
    v = np.array([q > 0, q < 3, q > 0, q < 3], np.float32)
    return np.ascontiguousarray(np.tile(v, (128, 1)))


def run_mod(c, c_ctx, w_mod, b_mod):
    nc = _prog("mod", build_mod)
    cT = np.ascontiguousarray(np.transpose(_vec(np.stack([c[0], c[1], c_ctx])), (0, 2, 1)))
    maps = []
    for core in range(8):
        l, half = core // 2, core % 2
        maps.append(dict(wm=np.ascontiguousarray(w_mod[l][:, half * 6144:(half + 1) * 6144]),
                         bm=_vec(b_mod[l][half * 6144:(half + 1) * 6144]), cT=cT))
    res = _launch(nc, maps)
    mod_full = np.empty((4, 3, 12288), np.float32)
    for core in range(8):
        l, half = core // 2, core % 2
        mo = res[core]["modo"]
        mod_full[l][:, half * 6144:(half + 1) * 6144] = np.transpose(mo, (2, 1, 0)).reshape(3, 6144)
    return mod_full


def _mod_in(mod_full, i, core):
    b = core // 4
    m = mod_full[i].reshape(3, 6, 2048)
    arr = np.stack([m[2], m[b]], axis=1)
    return _vec(arr)


def _common(mod_full, norm_g, i):
    return [dict(mod=_mod_in(mod_full, i, c), ng=_vec(norm_g[i]), hm=_hm(c)) for c in range(8)]


def run_ffn(x_lat, x_ctx, com, w_up, conv_w, conv_b, w_down):
    nc = _prog("ffn", build_ffn)
    xs = to_ts(x_lat, x_ctx)
    cw, cb = _vec(conv_w), _vec(conv_b)
    maps = [dict(xT=xs[c], w_up=w_up, w_down=w_down, cw=cw, cb=cb, **com[c]) for c in range(8)]
    res = _launch(nc, maps)
    return from_ts([r["outT"] for r in res])


def _na_tables():
    t = np.arange(4096)
    row, col = (t // 64).astype(np.float32), (t % 64).astype(np.float32)
    inv = (np.float32(10000.0) ** (-np.arange(32, dtype=np.float32) / np.float32(32))).astype(np.float32)
    ang = np.empty((128, 4096), np.float32)
    for d in range(128):
        pos = row if d < 64 else col
        ang[d] = pos * inv[d % 32]
    rotT = np.zeros((128, 128), np.float32)
    for d in range(128):
        if d % 64 < 32:
            rotT[d + 32, d] = -1.0
        else:
            rotT[d - 32, d] = 1.0
    return np.cos(ang).astype(np.float32), np.sin(ang).astype(np.float32), rotT


def _na_bias(rpb):
    kk = np.arange(128)[:, None, None, None]
    var = np.arange(8)[None, :, None, None]
    kb = np.arange(4)[None, None, :, None]
    q = np.arange(64)[None, None, None, :]
    w = 2 * kb + kk // 64
    kcol = kk % 64
    dr = w - var + 7
    dc = np.clip(kcol - q + 15, 0, 30)
    c0 = np.clip(q - 8, 0, 48)
    col_in = (kcol >= c0) & (kcol < c0 + 16)
    dr_b, dc_b, in_b = np.broadcast_arrays(dr, dc, col_in)
    g = rpb[:, dr_b, dc_b]
    g = np.where(in_b[None], g, np.float32(-30000.0)).astype(np.float32)
    return np.ascontiguousarray(g.reshape(16, 128, 8, 256))


def run_na_core(qkv_lat, qkv_ctx, rpb):
    nc = _prog("us_na", build_us_na)
    cos, sin, rotT = _na_tables()
    biasT = _na_bias(rpb)
    maps = []
    for c in range(8):
        b, h0 = c // 4, 4 * (c % 4)
        hs = range(h0, h0 + 4)

        def cols(arr, off):
            return np.ascontiguousarray(np.stack([arr[b][:, off + h * 128:off + (h + 1) * 128].T for h in hs]))
        v = np.stack([np.concatenate([qkv_lat[b][:, 4096 + h * 128:4096 + (h + 1) * 128],
                                      qkv_ctx[b][:, 4096 + h * 128:4096 + (h + 1) * 128]], axis=0) for h in hs])
        maps.append(dict(qT=cols(qkv_lat, 0), kT=cols(qkv_lat, 2048), qcT=cols(qkv_ctx, 0), kcT=cols(qkv_ctx, 2048),
                         v=np.ascontiguousarray(v), biasT=np.ascontiguousarray(biasT[h0:h0 + 4]),
                         cos=cos, sin=sin, rotT=rotT))
    res = _launch(nc, maps)
    o_lat = np.empty((2, 4096, 2048), np.float32)
    o_ctx = np.empty((2, 256, 2048), np.float32)
    for c in range(8):
        b, h0 = c // 4, 4 * (c % 4)
        for u in range(4):
            h = h0 + u
            o_lat[b][:, h * 128:(h + 1) * 128] = res[c]["oT"][u].T
            o_ctx[b][:, h * 128:(h + 1) * 128] = res[c]["ocT"][u].T
    return o_lat, o_ctx


def layer_na(x_lat, x_ctx, com, w_qkv, rpb, w_out):
    xs = to_ts(x_lat, x_ctx)
    nc1 = _prog("ts1", build_ts1, "na")
    res = _launch(nc1, [dict(xT=xs[c], W=w_qkv, **com[c]) for c in range(8)])
    qkv_lat, qkv_ctx = from_ts([r["outT"] for r in res])
    o_lat, o_ctx = run_na_core(qkv_lat, qkv_ctx, rpb)
    os_ = to_ts(o_lat, o_ctx)
    nc2 = _prog("ts2", build_ts2, "na")
    res = _launch(nc2, [dict(xT=xs[c], oT=os_[c], W=w_out, **com[c]) for c in range(8)])
    return from_ts([r["outT"] for r in res])


def run_mixer_layer(i, x_lat, x_ctx, com, p):
    kind, slot = i % 3, i // 3
    if kind == 2:
        return layer_na(x_lat, x_ctx, com, p["na_w_qkv"][slot], p["na_rpb"][slot], p["na_w_out"][slot])
    if kind == 1:
        return layer_hg(i, x_lat, x_ctx, com, p["hg_w_in"][slot], p["hg_lb"], p["hg_norm_g"][slot], p["hg_w_out"][slot])
    return layer_ssd(x_lat, x_ctx, com, p["ssm_w_in"][slot], p["ssm_conv_w"][slot], p["ssm_conv_b"][slot],
                     p["ssm_dt_bias"][slot], p["ssm_a_log"][slot], p["ssm_d"][slot], p["ssm_norm_g"][slot],
                     p["ssm_w_out"][slot])


SEQ_ALL = 4352
NTILE = SEQ_ALL // 128


def build_us_hg(NU=8, ntile=NTILE):
    P = Prog()
    L = ntile * 128
    qT_d = P.dram_in("qT", [NU, 128, L])
    gT_d = P.dram_in("gT", [NU, 128, L])
    gtok_d = P.dram_in("gtok", [NU, L, 128])
    v_d = P.dram_in("v", [NU, L, 128])
    BT = load_small(P, "BT", [128, 128])
    U = load_small(P, "U", [128, 128])
    ind = load_small(P, "ind", [128, 4])
    oT_d = P.dram_out("oT", [NU, 128, L])
    S = [P.sbuf([128, 128], F32, name=f"S{u}") for u in range(NU)]
    Sb = [P.sbuf([128, 128], BF16, name=f"Sb{u}") for u in range(NU)]
    for u in range(NU):
        P.op("vector", lambda e, u=u: e.memset(S[u][:], 0.0), writes=[("S", u)])
        P.op("vector", lambda e, u=u: e.memset(Sb[u][:], 0.0), writes=[("Sb", u)])
    R = lambda n, dt, nm, cols=128: Rot(P, n, [128, cols], dt, nm)
    q_r, g_r, gt_r = R(4, F32, "q"), R(4, F32, "g"), R(4, F32, "gt")
    v_r = R(4, BF16, "v")
    e1_r, e2_r, e3_r = R(3, F32, "e1"), R(3, F32, "e2"), R(3, F32, "e3")
    kT_r, kt_r = R(3, F32, "kT"), R(3, F32, "ktok")
    qt_r, ktl_r = R(3, BF16, "qt"), R(3, BF16, "ktl")
    kh_r = R(3, F32, "kh")
    khm_r = R(3, BF16, "khm", 512)
    at_r = R(3, BF16, "attn")
    o_r = R(3, F32, "o")
    pAB = PsRot(P, 2, "pAB")
    pC = PsRot(P, 2, "pC")
    pD = PsRot(P, 2, "pD")
    pE = PsRot(P, 2, "pE")
    for t in range(ntile):
        ts = slice(t * 128, (t + 1) * 128)
        for u in range(NU):
            qs, qk = q_r.next()
            gs, gk = g_r.next()
            gts, gtk = gt_r.next()
            vs, vk = v_r.next()
            P.dma("sync", qs[:], qT_d[u][:, ts], writes=[qk])
            P.dma("sync", gs[:], gT_d[u][:, ts], writes=[gk])
            P.dma("sync", gts[:], gtok_d[u][ts, :], writes=[gtk])
            P.dma("gpsimd", vs[:], v_d[u][ts, :], writes=[vk])
            ab, abk = pAB.next()
            P.op("tensor", lambda e, ab=ab, gts=gts: e.matmul(ab[:, 0:128], gts[:], BT[:], start=True, stop=True),
                 reads=[gtk, "BT"], writes=[abk])
            P.op("tensor", lambda e, ab=ab, gts=gts: e.matmul(ab[:, 128:256], U[:], gts[:], start=True, stop=True),
                 reads=[gtk, "U"], writes=[abk])
            e1, e1k = e1_r.next()
            e2, e2k = e2_r.next()
            e3, e3k = e3_r.next()
            P.op("scalar", lambda e, e1=e1, ab=ab: e.activation(e1[:], ab[:, 0:128], AF.Exp), reads=[abk], writes=[e1k])
            P.op("scalar", lambda e, e2=e2, ab=ab: e.activation(e2[:], ab[:, 0:128], AF.Exp, scale=-1.0),
                 reads=[abk], writes=[e2k])
            P.op("scalar", lambda e, e3=e3, ab=ab: e.activation(e3[:], ab[:, 128:256], AF.Exp), reads=[abk], writes=[e3k])
            kT, kTk = kT_r.next()
            kt, ktk = kt_r.next()
            P.op("scalar", lambda e, kT=kT, gs=gs: e.activation(kT[:], gs[:], AF.Exp), reads=[gk], writes=[kTk])
            P.op("scalar", lambda e, kt=kt, gts=gts: e.activation(kt[:], gts[:], AF.Exp), reads=[gtk], writes=[ktk])
            P.op("gpsimd", lambda e, kT=kT: e.tensor_scalar(kT[:], kT[:], -1.0, 1.0, ALU.mult, ALU.add),
                 reads=[kTk], writes=[kTk])
            P.op("gpsimd", lambda e, kt=kt: e.tensor_scalar(kt[:], kt[:], -1.0, 1.0, ALU.mult, ALU.add),
                 reads=[ktk], writes=[ktk])
            qt, qtk = qt_r.next()
            ktl, ktlk = ktl_r.next()
            kh, khk = kh_r.next()
            khm, khmk = khm_r.next()
            P.op("vector", lambda e, qt=qt, qs=qs, e1=e1: e.tensor_tensor(qt[:], qs[:], e1[:], ALU.mult),
                 reads=[qk, e1k], writes=[qtk])
            P.op("vector", lambda e, ktl=ktl, kT=kT, e2=e2: e.tensor_tensor(ktl[:], kT[:], e2[:], ALU.mult),
                 reads=[kTk, e2k], writes=[ktlk])
            P.op("gpsimd", lambda e, kh=kh, kt=kt, e3=e3: e.tensor_tensor(kh[:], kt[:], e3[:], ALU.mult),
                 reads=[ktk, e3k], writes=[khk])
            for I in range(4):
                P.op("gpsimd", lambda e, khm=khm, kh=kh, I=I: e.tensor_scalar(
                    khm[:, I * 128:(I + 1) * 128], kh[:], ind[:, I:I + 1], None, ALU.mult),
                    reads=[khk, "ind"], writes=[(khmk, I)])
            pc, pck = pC.next()
            P.op("tensor", lambda e, pc=pc, ktl=ktl, qt=qt: e.matmul(pc[:, 0:128], ktl[:], qt[:], start=True, stop=True),
                 reads=[ktlk, qtk], writes=[pck])
            at, atk = at_r.next()
            P.op("vector", lambda e, at=at, pc=pc: e.tensor_tensor(at[:], pc[:, 0:128], BT[:], ALU.mult),
                 reads=[pck, "BT"], writes=[atk])
            pd, pdk = pD.next()
            P.op("tensor", lambda e, pd=pd, vs=vs, at=at: e.matmul(pd[:, 0:128], vs[:], at[:], start=True, stop=False),
                 reads=[vk, atk], writes=[pdk])
            for I in range(4):
                P.op("tensor", lambda e, pd=pd, u=u, qt=qt, I=I: e.matmul(
                    pd[:, I * 32:(I + 1) * 32], Sb[u][:], qt[:, I * 32:(I + 1) * 32], start=False, stop=(I == 3)),
                    reads=[("Sb", u), qtk], writes=[pdk])
                pe, pek = pE.next()
                P.op("tensor", lambda e, pe=pe, khm=khm, vs=vs, I=I: e.matmul(
                    pe[:, 0:128], khm[:, I * 128:(I + 1) * 128], vs[:], start=True, stop=True),
                    reads=[(khmk, I), vk], writes=[pek])
                P.op("vector", lambda e, u=u, pe=pe, e1=e1, I=I: e.scalar_tensor_tensor(
                    S[u][:], S[u][:], e1[:, I * 32 + 31:I * 32 + 32], pe[:, 0:128], ALU.mult, ALU.add),
                    reads=[("S", u), e1k, pek], writes=[("S", u)])
                P.op("scalar", lambda e, u=u: e.copy(Sb[u][:], S[u][:]), reads=[("S", u)], writes=[("Sb", u)])
            ob, obk = o_r.next()
            P.op("scalar", lambda e, ob=ob, pd=pd: e.copy(ob[:], pd[:, 0:128]), reads=[pdk], writes=[obk])
            P.dma("sync", oT_d[u][:, ts], ob[:], reads=[obk], key=("dma_out", obk))
    return P.build()


def _hg_consts():
    j = np.arange(128)[:, None]
    i = np.arange(128)[None, :]
    same = (j // 32) == (i // 32)
    BT = (same & (j <= i)).astype(np.float32)
    U = (same & (j > i)).astype(np.float32)
    ind = (np.arange(128)[:, None] // 32 == np.arange(4)[None, :]).astype(np.float32)
    return BT, U, ind


def run_hg_core(lat, ctx):
    nc = _prog("us_hg", build_us_hg)
    BT, U, ind = _hg_consts()
    maps = []
    for c in range(8):
        b, h0 = c // 4, 4 * (c % 4)
        qT, gT, gtok, v = [], [], [], []
        for hh in range(4):
            h = h0 + hh
            for d in range(2):
                def seq(off):
                    cc, ll = ctx[b][:, off:off + 128], lat[b][:, off:off + 128]
                    if d == 1:
                        cc, ll = cc[::-1], ll[::-1]
                    return np.concatenate([cc, ll], axis=0)
                q_s, v_s, g_s = seq(h * 128), seq(2048 + h * 128), seq(4096 + d * 2048 + h * 128)
                qT.append(q_s.T)
                gT.append(g_s.T)
                gtok.append(g_s)
                v.append(v_s)
        maps.append(dict(qT=np.ascontiguousarray(np.stack(qT)), gT=np.ascontiguousarray(np.stack(gT)),
                         gtok=np.ascontiguousarray(np.stack(gtok)), v=np.ascontiguousarray(np.stack(v)),
                         BT=BT, U=U, ind=ind))
    res = _launch(nc, maps)
    o_lat = np.empty((2, 2, 4096, 2048), np.float32)
    o_ctx = np.empty((2, 2, 256, 2048), np.float32)
    for c in range(8):
        b, h0 = c // 4, 4 * (c % 4)
        for hh in range(4):
            h = h0 + hh
            for d in range(2):
                o = res[c]["oT"][hh * 2 + d].T
                oc, ol = o[:256], o[256:]
                if d == 1:
                    oc, ol = oc[::-1], ol[::-1]
                o_lat[d, b][:, h * 128:(h + 1) * 128] = ol
                o_ctx[d, b][:, h * 128:(h + 1) * 128] = oc
    return o_lat, o_ctx


def layer_hg(i, x_lat, x_ctx, com, w_in, hg_lb, norm_g_h, w_out):
    xs = to_ts(x_lat, x_ctx)
    nc1 = _prog("ts1", build_ts1, "hg", i)
    lbp = _vec(hg_lb)
    res1 = _launch(nc1, [dict(xT=xs[c], W=w_in, lbp=lbp, **com[c]) for c in range(8)])
    lat, ctx = from_ts([r["outT"] for r in res1])
    o_lat, o_ctx = run_hg_core(lat, ctx)
    oF = to_ts(o_lat[0], o_ctx[0])
    oB = to_ts(o_lat[1], o_ctx[1])
    nc2 = _prog("ts2", build_ts2, "hg")
    hgn = _vec(norm_g_h)
    res = _launch(nc2, [dict(xT=xs[c], oF=oF[c], oB=oB[c], gsT=np.ascontiguousarray(res1[c]["outT"][8192:10240]),
                             hgn=hgn, W=w_out, **com[c]) for c in range(8)])
    return from_ts([r["outT"] for r in res])


SSD_DEBUG = 0


def build_us_ssd(NU=4, ntile=NTILE):
    P = Prog()
    L = ntile * 128
    x_d = P.dram_in("x", [NU, L, 512])
    Btok_d = P.dram_in("Btok", [NU, L, 128])
    BT_d = P.dram_in("BT", [NU, 128, L])
    CT_d = P.dram_in("CT", [NU, 128, L])
    a_d = P.dram_in("a", [NU, L, 8])
    dt_d = P.dram_in("dt", [NU, L, 8])
    T = load_small(P, "T", [128, 128])
    Mneg = load_small(P, "Mneg", [128, 128])
    y_d = P.dram_out("y", [NU, L, 512])
    S = [P.sbuf([128, 512], F32, name=f"S{u}") for u in range(NU)]
    Sb = [P.sbuf([128, 512], BF16, name=f"Sb{u}") for u in range(NU)]
    for u in range(NU):
        P.op("vector", lambda e, u=u: e.memset(S[u][:], 0.0), writes=[("S", u)])
        P.op("vector", lambda e, u=u: e.memset(Sb[u][:], 0.0), writes=[("Sb", u)])
    R = lambda n, dt, nm, cols=128: Rot(P, n, [128, cols], dt, nm)
    x_r = R(3, BF16, "x", 512)
    bt_r, BT_r, CT_r = R(3, BF16, "btok"), R(3, BF16, "BTf"), R(3, BF16, "CTf")
    a_r, dt_r = R(3, F32, "a", 8), R(3, F32, "dt", 8)
    abc_r = R(2, F32, "abc", 1024)
    na_r = R(2, F32, "nacum", 8)
    cb_r = R(2, F32, "cbs")
    tm_r, e_r, e2_r = R(3, F32, "tm"), R(3, F32, "E"), R(3, F32, "E2")
    st_r, ce_r = R(3, BF16, "ST"), R(3, BF16, "Cexp")
    w_r = R(2, F32, "W", 8)
    dec_r = R(2, F32, "dec", 8)
    xw_r = R(2, BF16, "xw", 512)
    y_r = R(2, F32, "ysb", 512)
    pAC = PsRot(P, 1, "pAC")
    pB = PsRot(P, 2, "pB")
    pY = PsRot(P, 2, "pY")
    pS = PsRot(P, 2, "pS")
    for t in range(ntile):
        ts = slice(t * 128, (t + 1) * 128)
        for u in range(NU):
            xs, xk = x_r.next()
            bts, btk = bt_r.next()
            BTs, BTk = BT_r.next()
            CTs, CTk = CT_r.next()
            as_, ak = a_r.next()
            dts, dtk = dt_r.next()
            P.dma("gpsimd", xs[:], x_d[u][ts, :], writes=[xk])
            P.dma("gpsimd", bts[:], Btok_d[u][ts, :], writes=[btk])
            P.dma("gpsimd", BTs[:], BT_d[u][:, ts], writes=[BTk])
            P.dma("gpsimd", CTs[:], CT_d[u][:, ts], writes=[CTk])
            P.dma("sync", as_[:], a_d[u][ts, :], writes=[ak])
            P.dma("sync", dts[:], dt_d[u][ts, :], writes=[dtk])
            pac, pack = pAC.next()
            P.op("tensor", lambda e, pac=pac, BTs=BTs, CTs=CTs: e.matmul(pac[:, 0:128], BTs[:], CTs[:], start=True, stop=True),
                 reads=[BTk, CTk], writes=[pack])
            P.op("tensor", lambda e, pac=pac, as_=as_: e.matmul(pac[:, 128:136], T[:], as_[:], start=True, stop=True),
                 reads=[ak, "T"], writes=[pack])
            abc, abck = abc_r.next()
            P.op("vector", lambda e, abc=abc, as_=as_: e.tensor_copy(
                abc[:].rearrange("p (h m) -> p h m", h=8), as_[:].unsqueeze(2).to_broadcast([128, 8, 128])),
                reads=[ak], writes=[abck])
            pbs = []
            for half in range(2):
                pb, pbk = pB.next()
                for hh in range(4):
                    h = half * 4 + hh
                    P.op("tensor", lambda e, pb=pb, abc=abc, h=h, hh=hh: e.matmul(
                        pb[:, hh * 128:(hh + 1) * 128], abc[:, h * 128:(h + 1) * 128], T[:], start=True, stop=True),
                        reads=[abck, "T"], writes=[pbk], signal=(hh == 3))
                pbs.append((pb, pbk))
            nac, nack = na_r.next()
            lnd, lndk = w_r.next()
            P.op("scalar", lambda e, lnd=lnd, dts=dts: e.activation(lnd[:], dts[:], AF.Ln), reads=[dtk], writes=[lndk])
            P.op("vector", lambda e, nac=nac, pac=pac, lnd=lnd: e.scalar_tensor_tensor(
                nac[:], pac[:, 128:136], -1.0, lnd[:], ALU.mult, ALU.add), reads=[pack, lndk], writes=[nack])
            cbs, cbk = cb_r.next()
            P.op("scalar", lambda e, cbs=cbs, pac=pac: e.copy(cbs[:], pac[:, 0:128]), reads=[pack], writes=[cbk])
            dec, deck = dec_r.next()
            xw, xwk = xw_r.next()
            py, pyk = pY.next()
            if SSD_DEBUG == 1:
                ysb, yk = y_r.next()
                P.op("vector", lambda e, ysb=ysb, cbs=cbs: e.tensor_copy(ysb[:, 0:128], cbs[:]), reads=[cbk], writes=[yk])
                P.op("vector", lambda e, ysb=ysb, nac=nac: e.tensor_copy(ysb[:, 128:136], nac[:]), reads=[nack], writes=[yk])
                P.op("vector", lambda e, ysb=ysb, pb=pbs[0][0]: e.tensor_copy(ysb[:, 256:384], pb[:, 0:128]), reads=[pbs[0][1]], writes=[yk])
                P.op("vector", lambda e, ysb=ysb, pb=pbs[1][0]: e.tensor_copy(ysb[:, 384:512], pb[:, 384:512]), reads=[pbs[1][1]], writes=[yk])
                P.dma("sync", y_d[u][ts, :], ysb[:], reads=[yk], key=("dma_out", yk))
                continue
            for h in range(8):
                pb, pbk = pbs[h // 4]
                bc = pb[:, (h % 4) * 128:(h % 4 + 1) * 128]
                hs = slice(h * 64, (h + 1) * 64)
                tm, tmk = tm_r.next()
                P.op("vector", lambda e, tm=tm, bc=bc: e.tensor_tensor(tm[:], bc, Mneg[:], ALU.add),
                     reads=[pbk, "Mneg"], writes=[tmk])
                E, Ek = e_r.next()
                P.op("scalar", lambda e, E=E, tm=tm, nac=nac, h=h: e.activation(E[:], tm[:], AF.Exp, bias=nac[:, h:h + 1]),
                     reads=[tmk, nack], writes=[Ek])
                ST, STk = st_r.next()
                P.op("gpsimd", lambda e, ST=ST, E=E, cbs=cbs: e.tensor_tensor(ST[:], E[:], cbs[:], ALU.mult),
                     reads=[Ek, cbk], writes=[STk])
                E2, E2k = e2_r.next()
                P.op("scalar", lambda e, E2=E2, bc=bc: e.activation(E2[:], bc, AF.Exp), reads=[pbk], writes=[E2k])
                Ce, Cek = ce_r.next()
                P.op("vector", lambda e, Ce=Ce, CTs=CTs, E2=E2: e.tensor_tensor(Ce[:], CTs[:], E2[:], ALU.mult),
                     reads=[CTk, E2k], writes=[Cek])
                P.op("vector", lambda e, dec=dec, E2=E2, h=h: e.tensor_copy(dec[:, h:h + 1], E2[:, 127:128]),
                     reads=[E2k], writes=[(deck, h)])
                P.op("vector", lambda e, xw=xw, xs=xs, E=E, hs=hs: e.tensor_scalar(
                    xw[:, hs], xs[:, hs], E[:, 127:128], None, ALU.mult), reads=[xk, Ek], writes=[(xwk, h)])
                P.op("tensor", lambda e, py=py, ST=ST, xs=xs, hs=hs: e.matmul(py[:, hs], ST[:], xs[:, hs], start=True, stop=False),
                     reads=[STk, xk], writes=[pyk])
                P.op("tensor", lambda e, py=py, Ce=Ce, u=u, hs=hs: e.matmul(py[:, hs], Ce[:], Sb[u][:, hs], start=False, stop=True),
                     reads=[Cek, ("Sb", u)], writes=[pyk])
            ysb, yk = y_r.next()
            P.op("scalar", lambda e, ysb=ysb, py=py: e.copy(ysb[:], py[:]), reads=[pyk], writes=[yk])
            P.dma("sync", y_d[u][ts, :], ysb[:], reads=[yk], key=("dma_out", yk))
            ps, psk = pS.next()
            P.op("tensor", lambda e, ps=ps, bts=bts, xw=xw: e.matmul(ps[:], bts[:], xw[:], start=True, stop=True),
                 reads=[btk] + [(xwk, h) for h in range(8)], writes=[psk])
            for h in range(8):
                hs = slice(h * 64, (h + 1) * 64)
                P.op("vector", lambda e, u=u, ps=ps, dec=dec, h=h, hs=hs: e.scalar_tensor_tensor(
                    S[u][:, hs], S[u][:, hs], dec[:, h:h + 1], ps[:, hs], ALU.mult, ALU.add),
                    reads=[("S", u), (deck, h), psk], writes=[("S", u)])
            P.op("scalar", lambda e, u=u: e.copy(Sb[u][:], S[u][:]), reads=[("S", u)], writes=[("Sb", u)])
    return P.build()


def _ssd_consts():
    j = np.arange(128)[:, None]
    i = np.arange(128)[None, :]
    T = (j <= i).astype(np.float32)
    Mneg = np.where(j <= i, np.float32(0.0), np.float32(-30000.0)).astype(np.float32)
    return T, Mneg


def run_ssd_core(lat, ctx):
    nc = _prog("us_ssd", build_us_ssd)
    T, Mneg = _ssd_consts()
    maps = []
    for c in range(8):
        b, g0 = c // 4, 2 * (c % 4)
        x, Btok, BT, CT, a, dt = [], [], [], [], [], []
        for gg in range(2):
            g = g0 + gg
            for d in range(2):
                def seq(lo, hi):
                    cc, ll = ctx[b][:, lo:hi], lat[b][:, lo:hi]
                    if d == 1:
                        cc, ll = cc[::-1], ll[::-1]
                    return np.concatenate([cc, ll], axis=0)
                x.append(seq(4096 + g * 512, 4096 + (g + 1) * 512))
                Bs = seq(8192 + g * 128, 8192 + (g + 1) * 128)
                Cs = seq(9216 + g * 128, 9216 + (g + 1) * 128)
                Btok.append(Bs)
                BT.append(Bs.T)
                CT.append(Cs.T)
                dt.append(seq(10240 + d * 64 + g * 8, 10240 + d * 64 + (g + 1) * 8))
                a.append(seq(10368 + d * 64 + g * 8, 10368 + d * 64 + (g + 1) * 8))
        st = lambda l: np.ascontiguousarray(np.stack(l))
        maps.append(dict(x=st(x), Btok=st(Btok), BT=st(BT), CT=st(CT), a=st(a), dt=st(dt), T=T, Mneg=Mneg))
    res = _launch(nc, maps)
    y_lat = np.empty((2, 2, 4096, 4096), np.float32)
    y_ctx = np.empty((2, 2, 256, 4096), np.float32)
    for c in range(8):
        b, g0 = c // 4, 2 * (c % 4)
        for gg in range(2):
            g = g0 + gg
            for d in range(2):
                y = res[c]["y"][gg * 2 + d]
                yc, yl = y[:256], y[256:]
                if d == 1:
                    yc, yl = yc[::-1], yl[::-1]
                y_lat[d, b][:, g * 512:(g + 1) * 512] = yl
                y_ctx[d, b][:, g * 512:(g + 1) * 512] = yc
    return y_lat, y_ctx


def layer_ssd(x_lat, x_ctx, com, w_in, conv_w, conv_b, dt_bias, a_log, d_skip, norm_g_s, w_out):
    xs = to_ts(x_lat, x_ctx)
    nc1 = _prog("ts1", build_ts1, "ssd")
    cw, cb = _vec(conv_w), _vec(conv_b)
    dtb = np.ascontiguousarray(dt_bias.reshape(128, 1).astype(np.float32))
    alog = np.ascontiguousarray(a_log.reshape(128, 1).astype(np.float32))
    res1 = _launch(nc1, [dict(xT=xs[c], W=w_in, cw=cw, cb=cb, dtb=dtb, alog=alog, **com[c]) for c in range(8)])
    lat, ctx = from_ts([r["outT"] for r in res1])
    y_lat, y_ctx = run_ssd_core(lat, ctx)
    yF = to_ts(y_lat[0], y_ctx[0])
    yB = to_ts(y_lat[1], y_ctx[1])
    dsk = _vec(np.repeat(d_skip, 64, axis=1))
    sng = _vec(norm_g_s)
    nc2 = _prog("ts2", build_ts2, "ssd")
    res = _launch(nc2, [dict(xT=xs[c], yF=yF[c], yB=yB[c], xcT=np.ascontiguousarray(res1[c]["outT"][4096:8192]),
                             zsT=np.ascontiguousarray(res1[c]["outT"][0:4096]), dsk=dsk, sng=sng, W=w_out, **com[c])
                      for c in range(8)])
    return from_ts([r["outT"] for r in res])


def kernel(**inputs):
    p = {k: np.asarray(v) for k, v in inputs.items()}
    mod_full = run_mod(p["c"], p["c_ctx"], p["w_mod"], p["b_mod"])
    x_lat = np.asarray(p["x"], np.float32)
    x_ctx = np.asarray(p["ctx"], np.float32)
    for i in range(4):
        com = _common(mod_full, p["norm_g"], i)
        x_lat, x_ctx = run_mixer_layer(i, x_lat, x_ctx, com, p)
        x_lat, x_ctx = run_ffn(x_lat, x_ctx, com, p["ffn_w_up"][i], p["ffn_conv_w"][i], p["ffn_conv_b"][i],
                               p["ffn_w_down"][i])
    return np.ascontiguousarray(x_lat.astype(np.float32))
```

```python
import numpy as np
from contextlib import ExitStack
import concourse.bass as bass
import concourse.mybir as mybir
from concourse.bass_utils import run_bass_kernel_spmd

F32 = mybir.dt.float32
BF16 = mybir.dt.bfloat16
AF = mybir.ActivationFunctionType
ALU = mybir.AluOpType
AX = mybir.AxisListType

SAME_ENGINE_SYNC = True


class Prog:
    ENGINES = ("tensor", "vector", "scalar", "gpsimd", "sync")

    def __init__(self):
        self.nc = bass.Bass("TRN2", target_bir_lowering=False)
        self.ops = []
        self.stack = ExitStack()
        self.stack.enter_context(self.nc.allow_low_precision("bf16 matmul operands, fp32 accumulation"))
        self.n_names = 0

    def dram_in(self, name, shape, dtype=F32):
        return self.nc.dram_tensor(name, list(shape), dtype, kind="ExternalInput").ap()

    def dram_out(self, name, shape, dtype=F32):
        return self.nc.dram_tensor(name, list(shape), dtype, kind="ExternalOutput").ap()

    def sbuf(self, shape, dtype=F32, name=None):
        self.n_names += 1
        name = name or f"sb{self.n_names}"
        return self.stack.enter_context(self.nc.sbuf_tensor(name, list(shape), dtype))

    def psum(self, shape, dtype=F32, name=None):
        self.n_names += 1
        name = name or f"ps{self.n_names}"
        return self.stack.enter_context(self.nc.psum_tensor(name, list(shape), dtype))

    def op(self, engine, fn, reads=(), writes=(), signal=True):
        self.ops.append(dict(engine=engine, fn=fn, reads=list(reads), writes=list(writes),
                             kind="c", signal=signal))

    def dma(self, queue, out, in_, reads=(), writes=(), key=None, **kw):
        if key is None:
            key = ("dma", writes[0] if writes else reads[0])
        self.ops.append(dict(engine=queue, fn=lambda e: e.dma_start(out=out, in_=in_, **kw),
                             reads=list(reads), writes=list(writes), kind="d", key=key, signal=True))

    def build(self):
        nc = self.nc
        ops = self.ops
        eng_count = {e: 0 for e in self.ENGINES}
        dma_count = {}
        per_eng = {e: [] for e in self.ENGINES}
        for i, o in enumerate(ops):
            o["idx"] = i
            per_eng[o["engine"]].append(o)
        for e, lst in per_eng.items():
            c = 0
            pend = []
            for o in lst:
                if o["kind"] == "d":
                    k = o["key"]
                    dma_count[k] = dma_count.get(k, 0) + 16
                    o["token"] = (("D", k), dma_count[k])
                    o["tok_idx"] = o["idx"]
                else:
                    if o["signal"]:
                        c += 1
                        o["token"] = (("E", e), c)
                        o["tok_idx"] = o["idx"]
                        for p in pend:
                            p["token"] = (("E", e), c)
                            p["tok_idx"] = o["idx"]
                        pend = []
                    else:
                        pend.append(o)
            assert not pend, "last op on engine must signal"
        last_w = {}
        readers = {}
        for o in ops:
            deps = {}

            def add(src, same_ok):
                s, v = src["token"]
                if src["engine"] != o["engine"]:
                    assert src["tok_idx"] < o["idx"], ("forward dependency (deadlock)", src["idx"], o["idx"])
                deps[s] = max(deps.get(s, 0), v)

            for k in o["reads"]:
                if k in last_w:
                    add(last_w[k], False)
            for k in o["writes"]:
                if k in last_w:
                    add(last_w[k], True)
                for r in readers.get(k, ()):
                    add(r, True)
            o["deps"] = deps
            for k in o["reads"]:
                readers.setdefault(k, []).append(o)
            for k in o["writes"]:
                last_w[k] = o
                readers[k] = []
        sem_keys = []
        for o in ops:
            s = o["token"][0]
            if s not in sem_keys:
                sem_keys.append(s)
        self.n_sems = len(sem_keys)
        sems = {}
        for i, s in enumerate(sem_keys):
            sems[s] = self.stack.enter_context(nc.semaphore(f"s{i}"))
        block = self.stack.enter_context(nc.Block())
        final_dma = {}
        for o in ops:
            if o["kind"] == "d":
                final_dma.setdefault(o["engine"], {})[o["token"][0]] = o["token"][1]

        def make(e, lst):
            def body(eng):
                seen = {}
                own = ("E", e)
                for o in lst:
                    for s, v in o["deps"].items():
                        if s == own:
                            if not SAME_ENGINE_SYNC or e == "tensor":
                                continue
                            if v >= o["token"][1] and o["kind"] == "c":
                                continue
                        if seen.get(s, 0) >= v:
                            continue
                        eng.wait_ge(sems[s], v)
                        seen[s] = v
                    ins = o["fn"](eng)
                    if o["kind"] == "d":
                        ins.then_inc(sems[o["token"][0]], 16)
                    elif o["signal"]:
                        ins.then_inc(sems[o["token"][0]], 1)
                for s, v in final_dma.get(e, {}).items():
                    if seen.get(s, 0) < v:
                        eng.wait_ge(sems[s], v)
            return body

        for e in self.ENGINES:
            lst = per_eng[e]
            if not lst:
                continue
            getattr(block, e)(make(e, lst))
        self.stack.close()
        return nc


def run(prog_nc, in_maps, n=8, trace=False):
    return run_bass_kernel_spmd(prog_nc, in_maps, core_ids=list(range(n)), trace=trace)


D_MODEL = 2048
KC = D_MODEL // 128
NT = 1092
TTS = [(0, 364), (364, 728), (728, 1092)]
SEGS = [(0, 66), (66, 1092)]
HALO_COLS = [0, 65, 66, 1091]
RMS_EPS = 1e-6


class Rot:
    def __init__(self, P, n, shape, dtype, name):
        self.bufs = [P.sbuf(shape, dtype, name=f"{name}{i}") for i in range(n)]
        self.name = name
        self.i = 0

    def next(self):
        j = self.i % len(self.bufs)
        self.i += 1
        return self.bufs[j], (self.name, j)


class PsRot:
    def __init__(self, P, n, name="pt", cols=512):
        self.bufs = [P.psum([128, cols], F32, name=f"{name}{i}") for i in range(n)]
        self.name = name
        self.i = 0

    def next(self):
        j = self.i % len(self.bufs)
        self.i += 1
        return self.bufs[j], (self.name, j)


def load_small(P, name, shape, queue="sync"):
    d = P.dram_in(name, shape)
    t = P.sbuf(shape, F32, name=name + "_sb")
    P.dma(queue, t[:], d, writes=[name])
    return t


def load_w(P, W, col0, ncols, rot, queue="gpsimd"):
    wt, wk = rot.next()
    P.dma(queue, wt[:, :, 0:ncols], W[:, col0:col0 + ncols].rearrange("(kc p) n -> p kc n", p=128),
          writes=[wk])
    return wt, wk


def make_ones(P):
    ones = P.sbuf([128, 128], BF16, name="ones")
    P.op("vector", lambda e: e.memset(ones[:], 1.0), writes=["ones"])
    return ones


def rms_stats_begin(P, n=3, name="st"):
    return [P.psum([128, 512], F32, name=f"{name}{i}") for i in range(n)]


def rms_accum(P, ones, st, st_name, src, src_key, sqrot, first, last, ncols=NT, tts=TTS):
    sq, sqk = sqrot.next()
    P.op("scalar", lambda e: e.activation(sq[:, 0:ncols], src, AF.Square), reads=[src_key], writes=[sqk])
    for ti, (c0, c1) in enumerate(tts):
        P.op("tensor", lambda e, ti=ti, c0=c0, c1=c1: e.matmul(st[ti][:, 0:c1 - c0], ones[:], sq[:, c0:c1],
                                                                 start=first, stop=last),
             reads=[sqk, "ones"], writes=[(st_name, ti)], signal=True)


def rms_finish(P, st, st_name, rstd, rstd_key, dim, tts=TTS):
    for ti, (c0, c1) in enumerate(tts):
        P.op("vector", lambda e, ti=ti, c0=c0, c1=c1: e.tensor_scalar(
            rstd[:, c0:c1], st[ti][:, 0:c1 - c0], 1.0 / dim, RMS_EPS, ALU.mult, ALU.add),
            reads=[(st_name, ti)], writes=[(rstd_key, ti)])
        P.op("scalar", lambda e, c0=c0, c1=c1: e.activation(rstd[:, c0:c1], rstd[:, c0:c1], AF.Sqrt),
             reads=[(rstd_key, ti)], writes=[(rstd_key, ti)])
        P.op("vector", lambda e, c0=c0, c1=c1: e.reciprocal(rstd[:, c0:c1], rstd[:, c0:c1]),
             reads=[(rstd_key, ti)], writes=[(rstd_key, ti)])


def rstd_keys(rstd_key):
    return [(rstd_key, i) for i in range(3)]


def norm_mod(P, xT, ones, g_ap, sc_t, sh_t, hT, hm, sts, xrot, sqrot, tmprot, rstd):
    s1 = P.sbuf([128, 2, KC], F32, name="nm_s1%d" % P.n_names)
    for s in range(2):
        P.op("vector", lambda e, s=s: e.scalar_tensor_tensor(s1[:, s, :], sc_t[:, s, :], 1.0, g_ap,
                                                            ALU.add, ALU.mult),
             reads=["mod", "ng"], writes=[("s1", id(s1))])
    for kc in range(KC):
        xb, xk = xrot.next()
        P.dma("sync", xb[:], xT[kc * 128:(kc + 1) * 128, :], writes=[xk])
        rms_accum(P, ones, sts, "st", xb[:], xk, sqrot, kc == 0, kc == KC - 1)
    rms_finish(P, sts, "st", rstd, "rstd", D_MODEL)
    for kc in range(KC):
        xb, xk = xrot.next()
        P.dma("sync", xb[:], xT[kc * 128:(kc + 1) * 128, :], writes=[xk])
        tb, tk = tmprot.next()
        P.op("vector", lambda e, xb=xb, tb=tb: e.tensor_tensor(tb[:], xb[:], rstd[:], ALU.mult),
             reads=[xk] + rstd_keys("rstd"), writes=[tk])
        for s, (c0, c1) in enumerate(SEGS):
            P.op("scalar", lambda e, tb=tb, kc=kc, s=s, c0=c0, c1=c1: e.activation(
                hT[:, kc, c0:c1], tb[:, c0:c1], AF.Identity, bias=sh_t[:, s, kc:kc + 1],
                scale=s1[:, s, kc:kc + 1]),
                reads=[tk, ("s1", id(s1)), "mod"], writes=[("hT", kc)])
    for i, col in enumerate(HALO_COLS):
        P.op("vector", lambda e, i=i, col=col: e.tensor_scalar(
            hT[:, :, col:col + 1], hT[:, :, col:col + 1], hm[:, i:i + 1], None, ALU.mult),
            reads=[("hT", kc) for kc in range(KC)] + ["hm"], writes=[("hT", kc) for kc in range(KC)])


def resid_norm(P, fT_d, xT_d, outT_d, rstd2, gate_t, xrot, frot, orot):
    for kc in range(KC):
        xb, xk = xrot.next()
        P.dma("sync", xb[:], xT_d[kc * 128:(kc + 1) * 128, :], writes=[xk])
        fb, fk = frot.next()
        P.dma("sync", fb[:], fT_d[kc * 128:(kc + 1) * 128, :], reads=[("fT_d", kc)], writes=[fk])
        P.op("vector", lambda e, fb=fb: e.tensor_tensor(fb[:], fb[:], rstd2[:], ALU.mult),
             reads=[fk] + rstd_keys("rstd2"), writes=[fk])
        ob, ok = orot.next()
        for s, (c0, c1) in enumerate(SEGS):
            P.op("vector", lambda e, fb=fb, xb=xb, ob=ob, s=s, kc=kc, c0=c0, c1=c1: e.scalar_tensor_tensor(
                ob[:, c0:c1], fb[:, c0:c1], gate_t[:, s, kc:kc + 1], xb[:, c0:c1], ALU.mult, ALU.add),
                reads=[fk, xk, "gate"], writes=[ok])
        P.dma("sync", outT_d[kc * 128:(kc + 1) * 128, :], ob[:], reads=[ok], key=("dma_out", ok))


def make_gate(P, g_ap, gt_t, name):
    gate = P.sbuf([128, 2, KC], F32, name=name)
    for s in range(2):
        P.op("vector", lambda e, s=s: e.tensor_tensor(gate[:, s, :], gt_t[:, s, :], g_ap, ALU.mult),
             reads=["mod", "ng"], writes=["gate"])
    return gate


def load_mod(P):
    mod_d = P.dram_in("mod", [128, 6, 2, KC])
    mod = P.sbuf([128, 6, 2, KC], F32, name="mod_sb")
    P.dma("sync", mod[:], mod_d, writes=["mod"])
    ng = load_small(P, "ng", [128, 4, KC])
    hm = load_small(P, "hm", [128, 4])
    return mod, ng, hm


def build_ffn(FH=5632):
    NJ = FH // 128
    P = Prog()
    xT = P.dram_in("xT", [D_MODEL, NT])
    w_up = P.dram_in("w_up", [D_MODEL, 2 * FH])
    w_down = P.dram_in("w_down", [FH, D_MODEL])
    cw = load_small(P, "cw", [128, 3, 2 * NJ])
    cb = load_small(P, "cb", [128, 2 * NJ])
    mod, ng, hm = load_mod(P)
    outT = P.dram_out("outT", [D_MODEL, NT])
    fT_d = P.dram_out("fT", [D_MODEL, NT])
    ones = make_ones(P)
    hT = P.sbuf([128, KC, NT], BF16, name="hT")
    gT = P.sbuf([128, NJ, NT], BF16, name="gT")
    rstd = P.sbuf([128, NT], F32, name="rstd")
    rstd2 = P.sbuf([128, NT], F32, name="rstd2")
    xrot = Rot(P, 2, [128, NT], F32, "xb")
    frot = xrot
    sqrot = Rot(P, 2, [128, NT], BF16, "sq")
    wuprot = Rot(P, 2, [128, KC, 256], BF16, "wup")
    wdnrot = Rot(P, 2, [128, NJ, 128], BF16, "wdn")
    urot = Rot(P, 2, [128, NT], F32, "u")
    crot = Rot(P, 2, [128, NT], F32, "c")
    sts = rms_stats_begin(P)
    pts = PsRot(P, 5)

    norm_mod(P, xT, ones, ng[:, 2, :], mod[:, 4, :, :], mod[:, 3, :, :], hT, hm, sts, xrot, sqrot, frot, rstd)
    gate = make_gate(P, ng[:, 3, :], mod[:, 5, :, :], "gate3")
    hkeys = [("hT", kc) for kc in range(KC)]

    for j in range(NJ):
        wt, wk = wuprot.next()
        P.dma("gpsimd", wt[:, :, 0:128], w_up[:, j * 128:(j + 1) * 128].rearrange("(kc p) n -> p kc n", p=128),
              writes=[wk], key=("dmaw", wk))
        P.dma("gpsimd", wt[:, :, 128:256],
              w_up[:, FH + j * 128:FH + (j + 1) * 128].rearrange("(kc p) n -> p kc n", p=128),
              writes=[wk], key=("dmaw", wk))
        cs = []
        for half in range(2):
            jj = half * NJ + j
            ub, uk = urot.next()
            for ti, (c0, c1) in enumerate(TTS):
                pt, pk = pts.next()
                for kc in range(KC):
                    P.op("tensor", lambda e, pt=pt, wt=wt, kc=kc, half=half, c0=c0, c1=c1: e.matmul(
                        pt[:, 0:c1 - c0], wt[:, kc, half * 128:(half + 1) * 128], hT[:, kc, c0:c1],
                        start=(kc == 0), stop=(kc == KC - 1)),
                        reads=[wk] + hkeys, writes=[pk], signal=(kc == KC - 1))
                P.op("scalar", lambda e, pt=pt, ub=ub, c0=c0, c1=c1: e.copy(ub[:, c0:c1], pt[:, 0:c1 - c0]),
                     reads=[pk], writes=[(uk, ti)])
            uks = [(uk, ti) for ti in range(3)]
            cbuf, ck = crot.next()
            P.op("scalar", lambda e, ub=ub, cbuf=cbuf, jj=jj: e.activation(
                cbuf[:, 1:NT - 1], ub[:, 1:NT - 1], AF.Identity, bias=cb[:, jj:jj + 1], scale=cw[:, 1, jj:jj + 1]),
                reads=uks + ["cw", "cb"], writes=[ck])
            P.op("vector", lambda e, ub=ub, cbuf=cbuf, jj=jj: e.scalar_tensor_tensor(
                cbuf[:, 1:NT - 1], ub[:, 0:NT - 2], cw[:, 0, jj:jj + 1], cbuf[:, 1:NT - 1], ALU.mult, ALU.add),
                reads=uks + [ck, "cw"], writes=[ck])
            P.op("vector", lambda e, ub=ub, cbuf=cbuf, jj=jj: e.scalar_tensor_tensor(
                cbuf[:, 1:NT - 1], ub[:, 2:NT], cw[:, 2, jj:jj + 1], cbuf[:, 1:NT - 1], ALU.mult, ALU.add),
                reads=uks + [ck, "cw"], writes=[ck])
            cs.append((cbuf, ck))
        (ca, cka), (cv, ckv) = cs
        P.op("scalar", lambda e, ca=ca: e.activation(ca[:, 1:NT - 1], ca[:, 1:NT - 1], AF.Silu),
             reads=[cka], writes=[cka])
        P.op("vector", lambda e, ca=ca, cv=cv, j=j: e.tensor_tensor(
            gT[:, j, 1:NT - 1], ca[:, 1:NT - 1], cv[:, 1:NT - 1], ALU.mult),
            reads=[cka, ckv], writes=[("gT", j)])
    P.op("vector", lambda e: e.memset(gT[:, :, 0:1], 0.0), writes=[("gT", j) for j in range(NJ)],
         reads=[("gT", j) for j in range(NJ)])
    P.op("vector", lambda e: e.memset(gT[:, :, NT - 1:NT], 0.0), writes=[("gT", j) for j in range(NJ)],
         reads=[("gT", j) for j in range(NJ)])
    gkeys = [("gT", j) for j in range(NJ)]

    for d in range(KC):
        wt, wk = wdnrot.next()
        P.dma("gpsimd", wt[:], w_down[:, d * 128:(d + 1) * 128].rearrange("(j p) n -> p j n", p=128),
              writes=[wk], key=("dmaw", wk))
        fb, fk = frot.next()
        for ti, (c0, c1) in enumerate(TTS):
            pt, pk = pts.next()
            for j in range(NJ):
                P.op("tensor", lambda e, pt=pt, wt=wt, j=j, c0=c0, c1=c1: e.matmul(
                    pt[:, 0:c1 - c0], wt[:, j, :], gT[:, j, c0:c1], start=(j == 0), stop=(j == NJ - 1)),
                    reads=[wk] + gkeys, writes=[pk], signal=(j == NJ - 1))
            P.op("scalar", lambda e, pt=pt, fb=fb, c0=c0, c1=c1: e.copy(fb[:, c0:c1], pt[:, 0:c1 - c0]),
                 reads=[pk], writes=[fk])
        P.dma("sync", fT_d[d * 128:(d + 1) * 128, :], fb[:], reads=[fk], writes=[("fT_d", d)],
              key=("dma_out", fk))
        rms_accum(P, ones, sts, "st", fb[:], fk, sqrot, d == 0, d == KC - 1)
    rms_finish(P, sts, "st", rstd2, "rstd2", D_MODEL)
    resid_norm(P, fT_d, xT, outT, rstd2, gate, xrot, frot, urot)
    return P.build()


def build_mod():
    P = Prog()
    wm = P.dram_in("wm", [D_MODEL, 6144])
    bm = load_small(P, "bm", [128, 48])
    cT = load_small(P, "cT", [128, KC, 3])
    modo = P.dram_out("modo", [128, 48, 3])
    sc = P.sbuf([128, KC, 3], BF16, name="silu_c")
    P.op("scalar", lambda e: e.activation(sc[:], cT[:], AF.Silu), reads=["cT"], writes=["sc"])
    res = P.sbuf([128, 48, 3], F32, name="res")
    wrot = Rot(P, 3, [128, KC, 128], BF16, "w")
    pts = PsRot(P, 4)
    for j in range(48):
        wt, wk = load_w(P, wm, j * 128, 128, wrot)
        pt, pk = pts.next()
        for kc in range(KC):
            P.op("tensor", lambda e, pt=pt, wt=wt, kc=kc: e.matmul(pt[:, 0:3], wt[:, kc, :], sc[:, kc, :],
                                                                  start=(kc == 0), stop=(kc == KC - 1)),
                 reads=[wk, "sc"], writes=[pk], signal=(kc == KC - 1))
        P.op("scalar", lambda e, pt=pt, j=j: e.activation(res[:, j, :], pt[:, 0:3], AF.Identity,
                                                         bias=bm[:, j:j + 1]),
             reads=[pk, "bm"], writes=["res"])
    P.dma("sync", modo, res[:], reads=["res"])
    return P.build()


def build_ts1(kind, layer_idx=1):
    N = {"na": 6144, "hg": 10240, "ssd": 10368}[kind]
    NOUT = {"na": 6144, "hg": 10240, "ssd": 10368 + 128}[kind]
    P = Prog()
    xT = P.dram_in("xT", [D_MODEL, NT])
    W = P.dram_in("W", [D_MODEL, N])
    mod, ng, hm = load_mod(P)
    outT = P.dram_out("outT", [NOUT, NT])
    ones = make_ones(P)
    hT = P.sbuf([128, KC, NT], BF16, name="hT")
    rstd = P.sbuf([128, NT], F32, name="rstd")
    xrot = Rot(P, 3, [128, NT], F32, "xb")
    sqrot = Rot(P, 2, [128, NT], BF16, "sq")
    tmprot = Rot(P, 2, [128, NT], F32, "tmp")
    wrot = Rot(P, 3, [128, KC, 128], BF16, "w")
    srot = Rot(P, 4, [128, NT], F32, "stage")
    sts = rms_stats_begin(P)
    pts = PsRot(P, 5)
    if kind == "ssd":
        cw = load_small(P, "cw", [128, 3, 48])
        cb = load_small(P, "cb", [128, 48])
        dtb = load_small(P, "dtb", [128, 1])
        alog = load_small(P, "alog", [128, 1])
        negA = P.sbuf([128, 1], F32, name="negA")
        P.op("scalar", lambda e: e.activation(negA[:], alog[:], AF.Exp), reads=["alog"], writes=["negA"])
        P.op("vector", lambda e: e.tensor_scalar(negA[:], negA[:], -1.0, None, ALU.mult),
             reads=["negA"], writes=["negA"])
    if kind == "hg":
        lbp = load_small(P, "lbp", [128, 2, 4, KC])
        el = P.sbuf([128, 2, 4, KC], F32, name="el")
        P.op("scalar", lambda e: e.activation(el[:], lbp[:], AF.Exp), reads=["lbp"], writes=["el"])
        ssum = P.sbuf([128, 2, KC], F32, name="ssum")
        num = P.sbuf([128, 2, KC], F32, name="num")
        lb = P.sbuf([128, 2, KC], F32, name="lb")
        oml = P.sbuf([128, 2, KC], F32, name="oml")
        P.op("vector", lambda e: e.tensor_tensor(ssum[:], el[:, :, 0, :], el[:, :, 1, :], ALU.add),
             reads=["el"], writes=["ssum"])
        for l in (2, 3):
            P.op("vector", lambda e, l=l: e.tensor_tensor(ssum[:], ssum[:], el[:, :, l, :], ALU.add),
                 reads=["el", "ssum"], writes=["ssum"])
        P.op("vector", lambda e: e.tensor_copy(num[:], el[:, :, 1, :]), reads=["el"], writes=["num"])
        for l in range(2, layer_idx + 1):
            P.op("vector", lambda e, l=l: e.tensor_tensor(num[:], num[:], el[:, :, l, :], ALU.add),
                 reads=["el", "num"], writes=["num"])
        P.op("vector", lambda e: e.reciprocal(ssum[:], ssum[:]), reads=["ssum"], writes=["ssum"])
        P.op("vector", lambda e: e.tensor_tensor(lb[:], num[:], ssum[:], ALU.mult),
             reads=["num", "ssum"], writes=["lb"])
        P.op("vector", lambda e: e.tensor_scalar(oml[:], lb[:], -1.0, 1.0, ALU.mult, ALU.add),
             reads=["lb"], writes=["oml"])

    norm_mod(P, xT, ones, ng[:, 0, :], mod[:, 1, :, :], mod[:, 0, :, :], hT, hm, sts, xrot, sqrot, tmprot, rstd)
    hkeys = [("hT", kc) for kc in range(KC)]

    def out_dma(row0, sb, sk):
        P.dma("sync", outT[row0:row0 + 128, :], sb[:], reads=[sk], key=("dma_out", sk))

    for n in range(N // 128):
        wt, wk = load_w(P, W, n * 128, 128, wrot)
        pks = []
        for ti, (c0, c1) in enumerate(TTS):
            pt, pk = pts.next()
            for kc in range(KC):
                P.op("tensor", lambda e, pt=pt, wt=wt, kc=kc, c0=c0, c1=c1: e.matmul(
                    pt[:, 0:c1 - c0], wt[:, kc, :], hT[:, kc, c0:c1], start=(kc == 0), stop=(kc == KC - 1)),
                    reads=[wk] + hkeys, writes=[pk], signal=(kc == KC - 1))
            pks.append((pt, pk, c0, c1))
        if kind == "na":
            ep = "copy"
        elif kind == "hg":
            ep = "silu" if (n < 16 or n >= 64) else ("copy" if n < 32 else "hgf")
        else:
            ep = "silu" if n < 32 else ("conv" if n < 80 else "dt")
        sb, sk = srot.next()
        if ep in ("copy", "silu"):
            fn = AF.Copy if ep == "copy" else AF.Silu
            for (pt, pk, c0, c1) in pks:
                P.op("scalar", lambda e, pt=pt, sb=sb, c0=c0, c1=c1, fn=fn: e.activation(
                    sb[:, c0:c1], pt[:, 0:c1 - c0], fn), reads=[pk], writes=[sk])
            out_dma(n * 128, sb, sk)
        elif ep == "hgf":
            d, kc_f = (0, n - 32) if n < 48 else (1, n - 48)
            for (pt, pk, c0, c1) in pks:
                P.op("scalar", lambda e, pt=pt, sb=sb, c0=c0, c1=c1: e.activation(
                    sb[:, c0:c1], pt[:, 0:c1 - c0], AF.Sigmoid), reads=[pk], writes=[sk])
            P.op("vector", lambda e, sb=sb, d=d, kc_f=kc_f: e.tensor_scalar(
                sb[:], sb[:], oml[:, d, kc_f:kc_f + 1], lb[:, d, kc_f:kc_f + 1], ALU.mult, ALU.add),
                reads=[sk, "oml", "lb"], writes=[sk])
            P.op("scalar", lambda e, sb=sb: e.activation(sb[:], sb[:], AF.Ln), reads=[sk], writes=[sk])
            out_dma(n * 128, sb, sk)
        elif ep == "conv":
            jj = n - 32
            for (pt, pk, c0, c1) in pks:
                P.op("scalar", lambda e, pt=pt, sb=sb, c0=c0, c1=c1: e.copy(sb[:, c0:c1], pt[:, 0:c1 - c0]),
                     reads=[pk], writes=[sk])
            cbuf, ck = srot.next()
            P.op("scalar", lambda e, sb=sb, cbuf=cbuf, jj=jj: e.activation(
                cbuf[:, 1:NT - 1], sb[:, 1:NT - 1], AF.Identity, bias=cb[:, jj:jj + 1], scale=cw[:, 1, jj:jj + 1]),
                reads=[sk, "cw", "cb"], writes=[ck])
            P.op("vector", lambda e, sb=sb, cbuf=cbuf, jj=jj: e.scalar_tensor_tensor(
                cbuf[:, 1:NT - 1], sb[:, 0:NT - 2], cw[:, 0, jj:jj + 1], cbuf[:, 1:NT - 1], ALU.mult, ALU.add),
                reads=[sk, ck, "cw"], writes=[ck])
            P.op("vector", lambda e, sb=sb, cbuf=cbuf, jj=jj: e.scalar_tensor_tensor(
                cbuf[:, 1:NT - 1], sb[:, 2:NT], cw[:, 2, jj:jj + 1], cbuf[:, 1:NT - 1], ALU.mult, ALU.add),
                reads=[sk, ck, "cw"], writes=[ck])
            P.op("scalar", lambda e, cbuf=cbuf: e.activation(cbuf[:, 1:NT - 1], cbuf[:, 1:NT - 1], AF.Silu),
                 reads=[ck], writes=[ck])
            P.op("vector", lambda e, cbuf=cbuf: e.memset(cbuf[:, 0:1], 0.0), reads=[ck], writes=[ck])
            P.op("vector", lambda e, cbuf=cbuf: e.memset(cbuf[:, NT - 1:NT], 0.0), reads=[ck], writes=[ck])
            out_dma(n * 128, cbuf, ck)
        else:
            for (pt, pk, c0, c1) in pks:
                P.op("scalar", lambda e, pt=pt, sb=sb, c0=c0, c1=c1: e.activation(
                    sb[:, c0:c1], pt[:, 0:c1 - c0], AF.Exp, bias=dtb[:, 0:1]), reads=[pk, "dtb"], writes=[sk])
            P.op("scalar", lambda e, sb=sb: e.activation(sb[:], sb[:], AF.Ln, bias=1.0), reads=[sk], writes=[sk])
            out_dma(n * 128, sb, sk)
            ab, ak = srot.next()
            P.op("vector", lambda e, sb=sb, ab=ab: e.tensor_scalar(ab[:], sb[:], negA[:, 0:1], None, ALU.mult),
                 reads=[sk, "negA"], writes=[ak])
            out_dma(n * 128 + 128, ab, ak)
    return P.build()


def build_ts2(kind):
    KCI = 32 if kind == "ssd" else 16
    P = Prog()
    xT = P.dram_in("xT", [D_MODEL, NT])
    W = P.dram_in("W", [KCI * 128, D_MODEL])
    mod, ng, hm = load_mod(P)
    outT = P.dram_out("outT", [D_MODEL, NT])
    fT_d = P.nc.dram_tensor("fT_scratch", [D_MODEL, NT], F32).ap()
    ones = make_ones(P)
    actT = P.sbuf([128, KCI, NT], BF16, name="actT")
    rstd2 = P.sbuf([128, NT], F32, name="rstd2")
    arot = Rot(P, 4, [128, NT], F32, "a")
    xrot = Rot(P, 3, [128, NT], F32, "xb")
    sqrot = Rot(P, 2, [128, NT], BF16, "sq")
    wrot = Rot(P, 2, [128, KCI, 128], BF16, "w")
    sts = rms_stats_begin(P)
    pts = PsRot(P, 5)
    akeys = [("actT", kc) for kc in range(KCI)]
    rstd_s = None
    if kind == "na":
        oT = P.dram_in("oT", [D_MODEL, NT])
        for kc in range(KCI):
            ab, ak = arot.next()
            P.dma("sync", ab[:], oT[kc * 128:(kc + 1) * 128, :], writes=[ak])
            P.op("scalar", lambda e, ab=ab, kc=kc: e.copy(actT[:, kc, :], ab[:]), reads=[ak], writes=[("actT", kc)])
    elif kind == "hg":
        oF = P.dram_in("oF", [D_MODEL, NT])
        oB = P.dram_in("oB", [D_MODEL, NT])
        gsT = P.dram_in("gsT", [D_MODEL, NT])
        hgn = load_small(P, "hgn", [128, KC])
        rs = P.sbuf([128, NT], F32, name="rs_h")
        for kc in range(KCI):
            a1, k1 = arot.next()
            a2, k2 = arot.next()
            a3, k3 = arot.next()
            P.dma("sync", a1[:], oF[kc * 128:(kc + 1) * 128, :], writes=[k1])
            P.dma("sync", a2[:], oB[kc * 128:(kc + 1) * 128, :], writes=[k2])
            P.dma("sync", a3[:], gsT[kc * 128:(kc + 1) * 128, :], writes=[k3])
            P.op("vector", lambda e, a1=a1, a2=a2: e.tensor_tensor(a1[:], a1[:], a2[:], ALU.add),
                 reads=[k1, k2], writes=[k1])
            rms_accum(P, ones, sts, "st", a1[:], k1, sqrot, True, True)
            rms_finish(P, sts, "st", rs, "rs", 128)
            P.op("vector", lambda e, a1=a1: e.tensor_tensor(a1[:], a1[:], rs[:], ALU.mult),
                 reads=[k1] + rstd_keys("rs"), writes=[k1])
            P.op("vector", lambda e, a1=a1, a3=a3, kc=kc: e.scalar_tensor_tensor(
                actT[:, kc, :], a1[:], hgn[:, kc:kc + 1], a3[:], ALU.mult, ALU.mult),
                reads=[k1, k3, "hgn"], writes=[("actT", kc)])
    else:
        yF = P.dram_in("yF", [4096, NT])
        yB = P.dram_in("yB", [4096, NT])
        xcT = P.dram_in("xcT", [4096, NT])
        zsT = P.dram_in("zsT", [4096, NT])
        dsk = load_small(P, "dsk", [128, 2, 32])
        sng = load_small(P, "sng", [128, 32])
        dsum = P.sbuf([128, 32], F32, name="dsum")
        P.op("vector", lambda e: e.tensor_tensor(dsum[:], dsk[:, 0, :], dsk[:, 1, :], ALU.add),
             reads=["dsk"], writes=["dsum"])
        rstd_s = P.sbuf([128, NT], F32, name="rstd_s")
        for kc in range(KCI):
            a1, k1 = arot.next()
            a2, k2 = arot.next()
            a3, k3 = arot.next()
            a4, k4 = arot.next()
            P.dma("sync", a1[:], yF[kc * 128:(kc + 1) * 128, :], writes=[k1])
            P.dma("sync", a2[:], yB[kc * 128:(kc + 1) * 128, :], writes=[k2])
            P.dma("sync", a3[:], xcT[kc * 128:(kc + 1) * 128, :], writes=[k3])
            P.dma("sync", a4[:], zsT[kc * 128:(kc + 1) * 128, :], writes=[k4])
            P.op("vector", lambda e, a1=a1, a2=a2: e.tensor_tensor(a1[:], a1[:], a2[:], ALU.add),
                 reads=[k1, k2], writes=[k1])
            P.op("vector", lambda e, a1=a1, a3=a3, kc=kc: e.scalar_tensor_tensor(
                a1[:], a3[:], dsum[:, kc:kc + 1], a1[:], ALU.mult, ALU.add), reads=[k1, k3, "dsum"], writes=[k1])
            P.op("vector", lambda e, a1=a1, a4=a4: e.tensor_tensor(a1[:], a1[:], a4[:], ALU.mult),
                 reads=[k1, k4], writes=[k1])
            rms_accum(P, ones, sts, "st", a1[:], k1, sqrot, kc == 0, kc == KCI - 1)
            P.op("vector", lambda e, a1=a1, kc=kc: e.tensor_scalar(
                actT[:, kc, :], a1[:], sng[:, kc:kc + 1], None, ALU.mult), reads=[k1, "sng"], writes=[("actT", kc)])
        rms_finish(P, sts, "st", rstd_s, "rstd_s", 4096)

    gate = make_gate(P, ng[:, 1, :], mod[:, 2, :, :], "gate1")
    for d in range(KC):
        wt, wk = wrot.next()
        P.dma("gpsimd", wt[:], W[:, d * 128:(d + 1) * 128].rearrange("(j p) n -> p j n", p=128),
              writes=[wk], key=("dmaw", wk))
        fb, fk = xrot.next()
        for ti, (c0, c1) in enumerate(TTS):
            pt, pk = pts.next()
            for j in range(KCI):
                P.op("tensor", lambda e, pt=pt, wt=wt, j=j, c0=c0, c1=c1: e.matmul(
                    pt[:, 0:c1 - c0], wt[:, j, :], actT[:, j, c0:c1], start=(j == 0), stop=(j == KCI - 1)),
                    reads=[wk] + akeys, writes=[pk], signal=(j == KCI - 1))
            if rstd_s is None:
                P.op("scalar", lambda e, pt=pt, fb=fb, c0=c0, c1=c1: e.copy(fb[:, c0:c1], pt[:, 0:c1 - c0]),
                     reads=[pk], writes=[fk])
            else:
                P.op("vector", lambda e, pt=pt, fb=fb, c0=c0, c1=c1: e.tensor_tensor(
                    fb[:, c0:c1], pt[:, 0:c1 - c0], rstd_s[:, c0:c1], ALU.mult),
                    reads=[pk] + rstd_keys("rstd_s"), writes=[fk])
        P.dma("sync", fT_d[d * 128:(d + 1) * 128, :], fb[:], reads=[fk], writes=[("fT_d", d)],
              key=("dma_out", fk))
        rms_accum(P, ones, sts, "st", fb[:], fk, sqrot, d == 0, d == KC - 1)
    rms_finish(P, sts, "st", rstd2, "rstd2", D_MODEL)
    resid_norm(P, fT_d, xT, outT, rstd2, gate, xrot, arot, arot)
    return P.build()


NA_SCALE = 128 ** -0.5


def build_us_na(NU=4):
    P = Prog()
    qT_d = P.dram_in("qT", [NU, 128, 4096])
    kT_d = P.dram_in("kT", [NU, 128, 4096])
    qcT_d = P.dram_in("qcT", [NU, 128, 256])
    kcT_d = P.dram_in("kcT", [NU, 128, 256])
    v_d = P.dram_in("v", [NU, 4352, 128])
    bias_d = P.dram_in("biasT", [NU, 128, 8, 256])
    cos = load_small(P, "cos", [128, 4096])
    sin = load_small(P, "sin", [128, 4096])
    rot_d = P.dram_in("rotT", [128, 128])
    oT_d = P.dram_out("oT", [NU, 128, 4096])
    ocT_d = P.dram_out("ocT", [NU, 128, 256])
    rotb = P.sbuf([128, 128], BF16, name="rotb")
    P.dma("gpsimd", rotb[:], rot_d, writes=["rotb"])
    ones = make_ones(P)
    qrot = Rot(P, 2, [128, 4096], F32, "qf")
    qbrot = Rot(P, 2, [128, 512], BF16, "qb")
    t1rot = Rot(P, 2, [128, 512], F32, "t1")
    t2rot = Rot(P, 2, [128, 512], F32, "t2")
    qr = P.sbuf([128, 4096], BF16, name="qr")
    kr = P.sbuf([128, 4096], BF16, name="kr")
    qc = P.sbuf([128, 256], BF16, name="qc")
    kc = P.sbuf([128, 256], BF16, name="kc")
    ve = P.sbuf([128, 34, 128], BF16, name="ve")
    vo = P.sbuf([128, 32, 128], BF16, name="vo")
    bias = P.sbuf([128, 8, 256], F32, name="bias")
    oT = P.sbuf([128, 4096], F32, name="oT_sb")
    ocT = P.sbuf([128, 256], F32, name="ocT_sb")
    trot = Rot(P, 3, [128, 256], F32, "t")
    erot = Rot(P, 3, [128, 512], BF16, "e")
    rrot = Rot(P, 2, [128, 256], F32, "rec")
    ps_s = PsRot(P, 3, "pS")
    ps_nd = PsRot(P, 3, "pND")
    ps_r = PsRot(P, 2, "pR")

    for u in range(NU):
        P.dma("gpsimd", ve[:], v_d[u].rearrange("(t p) d -> p t d", p=128), writes=["ve"])
        P.dma("gpsimd", vo[:], v_d[u, 64:64 + 4096, :].rearrange("(t p) d -> p t d", p=128), writes=["vo"])
        P.dma("gpsimd", qc[:], qcT_d[u], writes=["qc"])
        P.dma("gpsimd", kc[:], kcT_d[u], writes=["kc"])
        P.dma("sync", bias[:], bias_d[u], writes=["bias"])
        for (src_d, dst, dk) in ((qT_d, qr, "qr"), (kT_d, kr, "kr")):
            qf, qk = qrot.next()
            P.dma("sync", qf[:], src_d[u], writes=[qk])
            for t in range(8):
                sl = slice(t * 512, (t + 1) * 512)
                qb, qbk = qbrot.next()
                P.op("scalar", lambda e, qb=qb, qf=qf, sl=sl: e.copy(qb[:], qf[:, sl]), reads=[qk], writes=[qbk])
                pt, pk = ps_r.next()
                P.op("tensor", lambda e, pt=pt, qb=qb: e.matmul(pt[:], rotb[:], qb[:], start=True, stop=True),
                     reads=[qbk, "rotb"], writes=[pk])
                t1, t1k = t1rot.next()
                t2, t2k = t2rot.next()
                P.op("vector", lambda e, t1=t1, qf=qf, sl=sl: e.tensor_tensor(t1[:], qf[:, sl], cos[:, sl], ALU.mult),
                     reads=[qk, "cos"], writes=[t1k])
                P.op("vector", lambda e, t2=t2, pt=pt, sl=sl: e.tensor_tensor(t2[:], pt[:], sin[:, sl], ALU.mult),
                     reads=[pk, "sin"], writes=[t2k])
                P.op("gpsimd", lambda e, t1=t1, t2=t2, dst=dst, sl=sl: e.tensor_tensor(dst[:, sl], t1[:], t2[:], ALU.add),
                     reads=[t1k, t2k], writes=[(dk, t)])
        qrk = [("qr", t) for t in range(8)]
        krk = [("kr", t) for t in range(8)]
        for r in range(64):
            r0 = min(max(r - 4, 0), 56)
            var = r - r0
            qs = slice(r * 64, (r + 1) * 64)
            ps, psk = ps_s.next()
            for kb in range(4):
                tok0 = (r0 + 2 * kb) * 64
                P.op("tensor", lambda e, ps=ps, kb=kb, tok0=tok0, qs=qs: e.matmul(
                    ps[:, kb * 64:(kb + 1) * 64], kr[:, tok0:tok0 + 128], qr[:, qs], start=True, stop=True),
                    reads=qrk + krk, writes=[psk], signal=False)
            for cb in range(2):
                P.op("tensor", lambda e, ps=ps, cb=cb, qs=qs: e.matmul(
                    ps[:, 256 + cb * 64:256 + (cb + 1) * 64], kc[:, cb * 128:(cb + 1) * 128], qr[:, qs],
                    start=True, stop=True), reads=qrk + ["kc"], writes=[psk], signal=(cb == 1))
            tb, tk = trot.next()
            P.op("vector", lambda e, tb=tb, ps=ps, var=var: e.scalar_tensor_tensor(
                tb[:], ps[:, 0:256], NA_SCALE, bias[:, var, :], ALU.mult, ALU.add),
                reads=[psk, "bias"], writes=[tk])
            eb, ek = erot.next()
            P.op("scalar", lambda e, eb=eb, tb=tb: e.activation(eb[:, 0:256], tb[:], AF.Exp),
                 reads=[tk], writes=[(ek, 0)])
            P.op("scalar", lambda e, eb=eb, ps=ps: e.activation(eb[:, 256:384], ps[:, 256:384], AF.Exp, scale=NA_SCALE),
                 reads=[psk], writes=[(ek, 1)])
            nd, ndk = ps_nd.next()
            for which in range(2):
                for blk in range(6):
                    if which == 0:
                        if blk < 4:
                            tok0 = (r0 + 2 * blk) * 64
                            lhs = ve[:, tok0 // 128, :] if tok0 % 128 == 0 else vo[:, (tok0 - 64) // 128, :]
                        else:
                            lhs = ve[:, 32 + blk - 4, :]
                    else:
                        lhs = ones[:]
                    P.op("tensor", lambda e, nd=nd, lhs=lhs, eb=eb, blk=blk, which=which: e.matmul(
                        nd[:, which * 64:(which + 1) * 64], lhs, eb[:, blk * 64:(blk + 1) * 64],
                        start=(blk == 0), stop=(blk == 5)),
                        reads=[(ek, 0), (ek, 1), "ve", "vo", "ones"], writes=[ndk],
                        signal=(which == 1 and blk == 5))
            rb, rk = rrot.next()
            P.op("vector", lambda e, rb=rb, nd=nd: e.reciprocal(rb[:, 0:64], nd[:, 64:128]), reads=[ndk], writes=[rk])
            P.op("vector", lambda e, rb=rb, nd=nd, qs=qs: e.tensor_tensor(oT[:, qs], nd[:, 0:64], rb[:, 0:64], ALU.mult),
                 reads=[ndk, rk], writes=["oT"])
        P.dma("sync", oT_d[u], oT[:], reads=["oT"], key=("dma_out", "oT"))
        ps, psk = ps_s.next()
        for cb in range(2):
            P.op("tensor", lambda e, ps=ps, cb=cb: e.matmul(
                ps[:, cb * 256:(cb + 1) * 256], kc[:, cb * 128:(cb + 1) * 128], qc[:], start=True, stop=True),
                reads=["qc", "kc"], writes=[psk], signal=(cb == 1))
        eb, ek = erot.next()
        P.op("scalar", lambda e, eb=eb, ps=ps: e.activation(eb[:], ps[:], AF.Exp, scale=NA_SCALE),
             reads=[psk], writes=[(ek, 0), (ek, 1)])
        nd, ndk = ps_nd.next()
        for which in range(2):
            for cb in range(2):
                lhs = ve[:, 32 + cb, :] if which == 0 else ones[:]
                P.op("tensor", lambda e, nd=nd, lhs=lhs, eb=eb, cb=cb, which=which: e.matmul(
                    nd[:, which * 256:(which + 1) * 256], lhs, eb[:, cb * 256:(cb + 1) * 256],
                    start=(cb == 0), stop=(cb == 1)),
                    reads=[(ek, 0), (ek, 1), "ve", "ones"], writes=[ndk], signal=(which == 1 and cb == 1))
        rb, rk = rrot.next()
        P.op("vector", lambda e, rb=rb, nd=nd: e.reciprocal(rb[:], nd[:, 256:512]), reads=[ndk], writes=[rk])
        P.op("vector", lambda e, rb=rb, nd=nd: e.tensor_tensor(ocT[:], nd[:, 0:256], rb[:], ALU.mult),
             reads=[ndk, rk], writes=["ocT"])
        P.dma("sync", ocT_d[u], ocT[:], reads=["ocT"], key=("dma_out", "ocT"))
    return P.build()


_PROGS = {}


def _prog(name, builder, *a):
    key = (name,) + a
    if key not in _PROGS:
        _PROGS[key] = builder(*a)
    return _PROGS[key]


_TRACE = False
_TIMES = []


def _launch(nc, in_maps):
    if _TRACE:
        res = run_bass_kernel_spmd(nc, in_maps, core_ids=list(range(8)), trace=True)
        _TIMES.append(res.exec_time_ns)
        print("LAUNCH exec_time_ns", res.exec_time_ns, flush=True)
    else:
        res = run_bass_kernel_spmd(nc, in_maps, core_ids=list(range(8)))
    return res.results


def _vec(v):
    v = np.asarray(v, np.float32)
    F = v.shape[-1]
    return np.ascontiguousarray(np.moveaxis(v.reshape(v.shape[:-1] + (F // 128, 128)), -1, 0))


def to_ts(lat, ctx):
    out = []
    F = lat.shape[-1]
    for c in range(8):
        b, q = c // 4, c % 4
        a = np.zeros((NT, F), np.float32)
        lo, hi = q * 64 - 1, q * 64 + 65
        s0, s1 = max(lo, 0), min(hi, 256)
        a[s0 - lo:s1 - lo] = ctx[b, s0:s1]
        lo, hi = q * 1024 - 1, q * 1024 + 1025
        s0, s1 = max(lo, 0), min(hi, 4096)
        a[66 + s0 - lo:66 + s1 - lo] = lat[b, s0:s1]
        out.append(np.ascontiguousarray(a.T))
    return out


def from_ts(outs, rows=None):
    F = outs[0].shape[0] if rows is None else rows[1] - rows[0]
    lat = np.empty((2, 4096, F), np.float32)
    ctx = np.empty((2, 256, F), np.float32)
    for c in range(8):
        b, q = c // 4, c % 4
        a = outs[c] if rows is None else outs[c][rows[0]:rows[1]]
        a = a.T
        ctx[b, q * 64:(q + 1) * 64] = a[1:65]
        lat[b, q * 1024:(q + 1) * 1024] = a[67:1091]
    return lat, ctx


def _hm(c):
    q = c % 4
    v = np.array([q > 0, q < 3, q > 0, q < 3], np.float32)
    return np.ascontiguousarray(np.tile(v, (128, 1)))


def run_mod(c, c_ctx, w_mod, b_mod):
    nc = _prog("mod", build_mod)
    cT = np.ascontiguousarray(np.transpose(_vec(np.stack([c[0], c[1], c_ctx])), (0, 2, 1)))
    maps = []
    for core in range(8):
        l, half = core // 2, core % 2
        maps.append(dict(wm=np.ascontiguousarray(w_mod[l][:, half * 6144:(half + 1) * 6144]),
                         bm=_vec(b_mod[l][half * 6144:(half + 1) * 6144]), cT=cT))
    res = _launch(nc, maps)
    mod_full = np.empty((4, 3, 12288), np.float32)
    for core in range(8):
        l, half = core // 2, core % 2
        mo = res[core]["modo"]
        mod_full[l][:, half * 6144:(half + 1) * 6144] = np.transpose(mo, (2, 1, 0)).reshape(3, 6144)
    return mod_full


def _mod_in(mod_full, i, core):
    b = core // 4
    m = mod_full[i].reshape(3, 6, 2048)
    arr = np.stack([m[2], m[b]], axis=1)
    return _vec(arr)


def _common(mod_full, norm_g, i):
    return [dict(mod=_mod_in(mod_full, i, c), ng=_vec(norm_g[i]), hm=_hm(c)) for c in range(8)]


def run_ffn(x_lat, x_ctx, com, w_up, conv_w, conv_b, w_down):
    nc = _prog("ffn", build_ffn)
    xs = to_ts(x_lat, x_ctx)
    cw, cb = _vec(conv_w), _vec(conv_b)
    maps = [dict(xT=xs[c], w_up=w_up, w_down=w_down, cw=cw, cb=cb, **com[c]) for c in range(8)]
    res = _launch(nc, maps)
    return from_ts([r["outT"] for r in res])


def _na_tables():
    t = np.arange(4096)
    row, col = (t // 64).astype(np.float32), (t % 64).astype(np.float32)
    inv = (np.float32(10000.0) ** (-np.arange(32, dtype=np.float32) / np.float32(32))).astype(np.float32)
    ang = np.empty((128, 4096), np.float32)
    for d in range(128):
        pos = row if d < 64 else col
        ang[d] = pos * inv[d % 32]
    rotT = np.zeros((128, 128), np.float32)
    for d in range(128):
        if d % 64 < 32:
            rotT[d + 32, d] = -1.0
        else:
            rotT[d - 32, d] = 1.0
    return np.cos(ang).astype(np.float32), np.sin(ang).astype(np.float32), rotT


def _na_bias(rpb):
    kk = np.arange(128)[:, None, None, None]
    var = np.arange(8)[None, :, None, None]
    kb = np.arange(4)[None, None, :, None]
    q = np.arange(64)[None, None, None, :]
    w = 2 * kb + kk // 64
    kcol = kk % 64
    dr = w - var + 7
    dc = np.clip(kcol - q + 15, 0, 30)
    c0 = np.clip(q - 8, 0, 48)
    col_in = (kcol >= c0) & (kcol < c0 + 16)
    dr_b, dc_b, in_b = np.broadcast_arrays(dr, dc, col_in)
    g = rpb[:, dr_b, dc_b]
    g = np.where(in_b[None], g, np.float32(-30000.0)).astype(np.float32)
    return np.ascontiguousarray(g.reshape(16, 128, 8, 256))


def run_na_core(qkv_lat, qkv_ctx, rpb):
    nc = _prog("us_na", build_us_na)
    cos, sin, rotT = _na_tables()
    biasT = _na_bias(rpb)
    maps = []
    for c in range(8):
        b, h0 = c // 4, 4 * (c % 4)
        hs = range(h0, h0 + 4)

        def cols(arr, off):
            return np.ascontiguousarray(np.stack([arr[b][:, off + h * 128:off + (h + 1) * 128].T for h in hs]))
        v = np.stack([np.concatenate([qkv_lat[b][:, 4096 + h * 128:4096 + (h + 1) * 128],
                                      qkv_ctx[b][:, 4096 + h * 128:4096 + (h + 1) * 128]], axis=0) for h in hs])
        maps.append(dict(qT=cols(qkv_lat, 0), kT=cols(qkv_lat, 2048), qcT=cols(qkv_ctx, 0), kcT=cols(qkv_ctx, 2048),
                         v=np.ascontiguousarray(v), biasT=np.ascontiguousarray(biasT[h0:h0 + 4]),
                         cos=cos, sin=sin, rotT=rotT))
    res = _launch(nc, maps)
    o_lat = np.empty((2, 4096, 2048), np.float32)
    o_ctx = np.empty((2, 256, 2048), np.float32)
    for c in range(8):
        b, h0 = c // 4, 4 * (c % 4)
        for u in range(4):
            h = h0 + u
            o_lat[b][:, h * 128:(h + 1) * 128] = res[c]["oT"][u].T
            o_ctx[b][:, h * 128:(h + 1) * 128] = res[c]["ocT"][u].T
    return o_lat, o_ctx


def layer_na(x_lat, x_ctx, com, w_qkv, rpb, w_out):
    xs = to_ts(x_lat, x_ctx)
    nc1 = _prog("ts1", build_ts1, "na")
    res = _launch(nc1, [dict(xT=xs[c], W=w_qkv, **com[c]) for c in range(8)])
    qkv_lat, qkv_ctx = from_ts([r["outT"] for r in res])
    o_lat, o_ctx = run_na_core(qkv_lat, qkv_ctx, rpb)
    os_ = to_ts(o_lat, o_ctx)
    nc2 = _prog("ts2", build_ts2, "na")
    res = _launch(nc2, [dict(xT=xs[c], oT=os_[c], W=w_out, **com[c]) for c in range(8)])
    return from_ts([r["outT"] for r in res])


def run_mixer_layer(i, x_lat, x_ctx, com, p):
    kind, slot = i % 3, i // 3
    if kind == 2:
        return layer_na(x_lat, x_ctx, com, p["na_w_qkv"][slot], p["na_rpb"][slot], p["na_w_out"][slot])
    if kind == 1:
        return layer_hg(i, x_lat, x_ctx, com, p["hg_w_in"][slot], p["hg_lb"], p["hg_norm_g"][slot], p["hg_w_out"][slot])
    return layer_ssd(x_lat, x_ctx, com, p["ssm_w_in"][slot], p["ssm_conv_w"][slot], p["ssm_conv_b"][slot],
                     p["ssm_dt_bias"][slot], p["ssm_a_log"][slot], p["ssm_d"][slot], p["ssm_norm_g"][slot],
                     p["ssm_w_out"][slot])


SEQ_ALL = 4352
NTILE = SEQ_ALL // 128


def build_us_hg(NU=8, ntile=NTILE):
    P = Prog()
    L = ntile * 128
    qT_d = P.dram_in("qT", [NU, 128, L])
    gT_d = P.dram_in("gT", [NU, 128, L])
    gtok_d = P.dram_in("gtok", [NU, L, 128])
    v_d = P.dram_in("v", [NU, L, 128])
    BT = load_small(P, "BT", [128, 128])
    U = load_small(P, "U", [128, 128])
    ind = load_small(P, "ind", [128, 4])
    oT_d = P.dram_out("oT", [NU, 128, L])
    S = [P.sbuf([128, 128], F32, name=f"S{u}") for u in range(NU)]
    Sb = [P.sbuf([128, 128], BF16, name=f"Sb{u}") for u in range(NU)]
    for u in range(NU):
        P.op("vector", lambda e, u=u: e.memset(S[u][:], 0.0), writes=[("S", u)])
        P.op("vector", lambda e, u=u: e.memset(Sb[u][:], 0.0), writes=[("Sb", u)])
    R = lambda n, dt, nm, cols=128: Rot(P, n, [128, cols], dt, nm)
    q_r, g_r, gt_r = R(4, F32, "q"), R(4, F32, "g"), R(6, F32, "gt")
    v_r = R(6, BF16, "v")
    e1_r, e2_r, e3_r = R(6, F32, "e1"), R(3, F32, "e2"), R(3, F32, "e3")
    kT_r, kt_r = R(3, F32, "kT"), R(3, F32, "ktok")
    qt_r, ktl_r = R(6, BF16, "qt"), R(3, BF16, "ktl")
    kh_r = R(3, F32, "kh")
    khm_r = R(6, BF16, "khm", 512)
    at_r = R(6, BF16, "attn")
    o_r = R(4, F32, "o")
    pAB = PsRot(P, 2, "pAB")
    pC = PsRot(P, 2, "pC")
    pD = PsRot(P, 2, "pD")
    pE = PsRot(P, 2, "pE")

    def stage_a(t, u):
        ts = slice(t * 128, (t + 1) * 128)
        qs, qk = q_r.next()
        gs, gk = g_r.next()
        gts, gtk = gt_r.next()
        vs, vk = v_r.next()
        P.dma("sync", qs[:], qT_d[u][:, ts], writes=[qk])
        P.dma("sync", gs[:], gT_d[u][:, ts], writes=[gk])
        P.dma("sync", gts[:], gtok_d[u][ts, :], writes=[gtk])
        P.dma("gpsimd", vs[:], v_d[u][ts, :], writes=[vk])
        ab, abk = pAB.next()
        P.op("tensor", lambda e: e.matmul(ab[:, 0:128], gts[:], BT[:], start=True, stop=True),
             reads=[gtk, "BT"], writes=[abk])
        P.op("tensor", lambda e: e.matmul(ab[:, 128:256], U[:], gts[:], start=True, stop=True),
             reads=[gtk, "U"], writes=[abk])
        e1, e1k = e1_r.next()
        e2, e2k = e2_r.next()
        e3, e3k = e3_r.next()
        P.op("scalar", lambda e: e.activation(e1[:], ab[:, 0:128], AF.Exp), reads=[abk], writes=[e1k])
        P.op("scalar", lambda e: e.activation(e2[:], ab[:, 0:128], AF.Exp, scale=-1.0), reads=[abk], writes=[e2k])
        P.op("scalar", lambda e: e.activation(e3[:], ab[:, 128:256], AF.Exp), reads=[abk], writes=[e3k])
        kT, kTk = kT_r.next()
        kt, ktk = kt_r.next()
        P.op("scalar", lambda e: e.activation(kT[:], gs[:], AF.Exp), reads=[gk], writes=[kTk])
        P.op("scalar", lambda e: e.activation(kt[:], gts[:], AF.Exp), reads=[gtk], writes=[ktk])
        P.op("gpsimd", lambda e: e.tensor_scalar(kT[:], kT[:], -1.0, 1.0, ALU.mult, ALU.add), reads=[kTk], writes=[kTk])
        P.op("gpsimd", lambda e: e.tensor_scalar(kt[:], kt[:], -1.0, 1.0, ALU.mult, ALU.add), reads=[ktk], writes=[ktk])
        qt, qtk = qt_r.next()
        ktl, ktlk = ktl_r.next()
        kh, khk = kh_r.next()
        khm, khmk = khm_r.next()
        P.op("vector", lambda e: e.tensor_tensor(qt[:], qs[:], e1[:], ALU.mult), reads=[qk, e1k], writes=[qtk])
        P.op("vector", lambda e: e.tensor_tensor(ktl[:], kT[:], e2[:], ALU.mult), reads=[kTk, e2k], writes=[ktlk])
        P.op("gpsimd", lambda e: e.tensor_tensor(kh[:], kt[:], e3[:], ALU.mult), reads=[ktk, e3k], writes=[khk])
        for I in range(4):
            P.op("gpsimd", lambda e, I=I: e.tensor_scalar(khm[:, I * 128:(I + 1) * 128], kh[:], ind[:, I:I + 1], None, ALU.mult),
                 reads=[khk, "ind"], writes=[(khmk, I)])
        pc, pck = pC.next()
        P.op("tensor", lambda e: e.matmul(pc[:, 0:128], ktl[:], qt[:], start=True, stop=True),
             reads=[ktlk, qtk], writes=[pck])
        at, atk = at_r.next()
        P.op("vector", lambda e: e.tensor_tensor(at[:], pc[:, 0:128], BT[:], ALU.mult), reads=[pck, "BT"], writes=[atk])
        return dict(t=t, u=u, ts=ts, vs=vs, vk=vk, e1=e1, e1k=e1k, qt=qt, qtk=qtk, khm=khm, khmk=khmk, at=at, atk=atk)

    def stage_b(group):
        for c in group:
            c["pd"], c["pdk"] = pD.next()
            P.op("tensor", lambda e, c=c: e.matmul(c["pd"][:, 0:128], c["vs"][:], c["at"][:], start=True, stop=False),
                 reads=[c["vk"], c["atk"]], writes=[c["pdk"]])
        for I in range(4):
            for c in group:
                u = c["u"]
                P.op("tensor", lambda e, c=c, u=u, I=I: e.matmul(
                    c["pd"][:, I * 32:(I + 1) * 32], Sb[u][:], c["qt"][:, I * 32:(I + 1) * 32], start=False, stop=(I == 3)),
                    reads=[("Sb", u), c["qtk"]], writes=[c["pdk"]])
                pe, pek = pE.next()
                P.op("tensor", lambda e, c=c, pe=pe, I=I: e.matmul(
                    pe[:, 0:128], c["khm"][:, I * 128:(I + 1) * 128], c["vs"][:], start=True, stop=True),
                    reads=[(c["khmk"], I), c["vk"]], writes=[pek])
                P.op("vector", lambda e, c=c, u=u, pe=pe, I=I: e.scalar_tensor_tensor(
                    S[u][:], S[u][:], c["e1"][:, I * 32 + 31:I * 32 + 32], pe[:, 0:128], ALU.mult, ALU.add),
                    reads=[("S", u), c["e1k"], pek], writes=[("S", u)])
                P.op("scalar", lambda e, u=u: e.copy(Sb[u][:], S[u][:]), reads=[("S", u)], writes=[("Sb", u)])
        for c in group:
            ob, obk = o_r.next()
            P.op("scalar", lambda e, c=c, ob=ob: e.copy(ob[:], c["pd"][:, 0:128]), reads=[c["pdk"]], writes=[obk])
            P.dma("sync", oT_d[c["u"]][:, c["ts"]], ob[:], reads=[obk], key=("dma_out", obk))

    G = 2
    groups = [[(t, u) for u in range(g0, min(g0 + G, NU))] for t in range(ntile) for g0 in range(0, NU, G)]
    prev = None
    for grp in groups:
        cur = [stage_a(t, u) for (t, u) in grp]
        if prev is not None:
            stage_b(prev)
        prev = cur
    stage_b(prev)
    return P.build()


def _hg_consts():
    j = np.arange(128)[:, None]
    i = np.arange(128)[None, :]
    same = (j // 32) == (i // 32)
    BT = (same & (j <= i)).astype(np.float32)
    U = (same & (j > i)).astype(np.float32)
    ind = (np.arange(128)[:, None] // 32 == np.arange(4)[None, :]).astype(np.float32)
    return BT, U, ind


def run_hg_core(lat, ctx):
    nc = _prog("us_hg", build_us_hg)
    BT, U, ind = _hg_consts()
    maps = []
    for c in range(8):
        b, h0 = c // 4, 4 * (c % 4)
        qT, gT, gtok, v = [], [], [], []
        for hh in range(4):
            h = h0 + hh
            for d in range(2):
                def seq(off):
                    cc, ll = ctx[b][:, off:off + 128], lat[b][:, off:off + 128]
                    if d == 1:
                        cc, ll = cc[::-1], ll[::-1]
                    return np.concatenate([cc, ll], axis=0)
                q_s, v_s, g_s = seq(h * 128), seq(2048 + h * 128), seq(4096 + d * 2048 + h * 128)
                qT.append(q_s.T)
                gT.append(g_s.T)
                gtok.append(g_s)
                v.append(v_s)
        maps.append(dict(qT=np.ascontiguousarray(np.stack(qT)), gT=np.ascontiguousarray(np.stack(gT)),
                         gtok=np.ascontiguousarray(np.stack(gtok)), v=np.ascontiguousarray(np.stack(v)),
                         BT=BT, U=U, ind=ind))
    res = _launch(nc, maps)
    o_lat = np.empty((2, 2, 4096, 2048), np.float32)
    o_ctx = np.empty((2, 2, 256, 2048), np.float32)
    for c in range(8):
        b, h0 = c // 4, 4 * (c % 4)
        for hh in range(4):
            h = h0 + hh
            for d in range(2):
                o = res[c]["oT"][hh * 2 + d].T
                oc, ol = o[:256], o[256:]
                if d == 1:
                    oc, ol = oc[::-1], ol[::-1]
                o_lat[d, b][:, h * 128:(h + 1) * 128] = ol
                o_ctx[d, b][:, h * 128:(h + 1) * 128] = oc
    return o_lat, o_ctx


def layer_hg(i, x_lat, x_ctx, com, w_in, hg_lb, norm_g_h, w_out):
    xs = to_ts(x_lat, x_ctx)
    nc1 = _prog("ts1", build_ts1, "hg", i)
    lbp = _vec(hg_lb)
    res1 = _launch(nc1, [dict(xT=xs[c], W=w_in, lbp=lbp, **com[c]) for c in range(8)])
    lat, ctx = from_ts([r["outT"] for r in res1])
    o_lat, o_ctx = run_hg_core(lat, ctx)
    oF = to_ts(o_lat[0], o_ctx[0])
    oB = to_ts(o_lat[1], o_ctx[1])
    nc2 = _prog("ts2", build_ts2, "hg")
    hgn = _vec(norm_g_h)
    res = _launch(nc2, [dict(xT=xs[c], oF=oF[c], oB=oB[c], gsT=np.ascontiguousarray(res1[c]["outT"][8192:10240]),
                             hgn=hgn, W=w_out, **com[c]) for c in range(8)])
    return from_ts([r["outT"] for r in res])


SSD_POOL = "gpsimd"


def build_us_ssd(NU=4, ntile=NTILE):
    P = Prog()
    L = ntile * 128
    x_d = P.dram_in("x", [NU, L, 512])
    Btok_d = P.dram_in("Btok", [NU, L, 128])
    BT_d = P.dram_in("BT", [NU, 128, L])
    CT_d = P.dram_in("CT", [NU, 128, L])
    a_d = P.dram_in("a", [NU, L, 8])
    dt_d = P.dram_in("dt", [NU, L, 8])
    T = load_small(P, "T", [128, 128])
    Mneg4 = load_small(P, "Mneg4", [128, 512])
    y_d = P.dram_out("y", [NU, L, 512])
    S = [P.sbuf([128, 512], F32, name=f"S{u}") for u in range(NU)]
    Sb = [P.sbuf([128, 512], BF16, name=f"Sb{u}") for u in range(NU)]
    for u in range(NU):
        P.op("vector", lambda e, u=u: e.memset(S[u][:], 0.0), writes=[("S", u)])
        P.op("vector", lambda e, u=u: e.memset(Sb[u][:], 0.0), writes=[("Sb", u)])
    R = lambda n, dt, nm, cols=128: Rot(P, n, [128, cols], dt, nm)
    x_r = R(3, BF16, "x", 512)
    bt_r, BT_r, CT_r = R(3, BF16, "btok"), R(3, BF16, "BTf"), R(3, BF16, "CTf")
    a_r, dt_r = R(3, F32, "a", 8), R(3, F32, "dt", 8)
    abc_r = R(2, F32, "abc", 1024)
    na_r = R(3, F32, "nacum", 8)
    ln_r = R(2, F32, "lnd", 8)
    cb_r = R(3, F32, "cbs4", 512)
    tm_r, e_r, e2_r = R(3, F32, "tm", 512), R(3, F32, "E", 512), R(4, F32, "E2", 512)
    st_r, ce_r = R(3, BF16, "ST", 512), R(3, BF16, "Cexp", 512)
    ts_r = R(2, F32, "tmpS", 256)
    xw_r = R(2, BF16, "xw", 512)
    y_r = R(2, F32, "ysb", 512)
    pAC = PsRot(P, 1, "pAC")
    pB = PsRot(P, 4, "pB")
    pY = PsRot(P, 2, "pY")
    pS = PsRot(P, 1, "pS")
    v4 = lambda ap, m: ap.rearrange("p (h m) -> p h m", h=4)

    def stage_a(t, u):
        ts = slice(t * 128, (t + 1) * 128)
        xs, xk = x_r.next()
        bts, btk = bt_r.next()
        BTs, BTk = BT_r.next()
        CTs, CTk = CT_r.next()
        as_, ak = a_r.next()
        dts, dtk = dt_r.next()
        P.dma("gpsimd", xs[:], x_d[u][ts, :], writes=[xk])
        P.dma("gpsimd", bts[:], Btok_d[u][ts, :], writes=[btk])
        P.dma("gpsimd", BTs[:], BT_d[u][:, ts], writes=[BTk])
        P.dma("gpsimd", CTs[:], CT_d[u][:, ts], writes=[CTk])
        P.dma("sync", as_[:], a_d[u][ts, :], writes=[ak])
        P.dma("sync", dts[:], dt_d[u][ts, :], writes=[dtk])
        pac, pack = pAC.next()
        P.op("tensor", lambda e, pac=pac, BTs=BTs, CTs=CTs: e.matmul(pac[:, 0:128], BTs[:], CTs[:], start=True, stop=True),
             reads=[BTk, CTk], writes=[pack])
        P.op("tensor", lambda e, pac=pac, as_=as_: e.matmul(pac[:, 128:136], T[:], as_[:], start=True, stop=True),
             reads=[ak, "T"], writes=[pack])
        abc, abck = abc_r.next()
        P.op("vector", lambda e, abc=abc, as_=as_: e.tensor_copy(
            abc[:].rearrange("p (h m) -> p h m", h=8), as_[:].unsqueeze(2).to_broadcast([128, 8, 128])),
            reads=[ak], writes=[abck])
        pbs = []
        for half in range(2):
            pb, pbk = pB.next()
            for hh in range(4):
                h = half * 4 + hh
                P.op("tensor", lambda e, pb=pb, abc=abc, h=h, hh=hh: e.matmul(
                    pb[:, hh * 128:(hh + 1) * 128], abc[:, h * 128:(h + 1) * 128], T[:], start=True, stop=True),
                    reads=[abck, "T"], writes=[pbk], signal=(hh == 3))
            pbs.append((pb, pbk))
        lnd, lndk = ln_r.next()
        P.op("scalar", lambda e, lnd=lnd, dts=dts: e.activation(lnd[:], dts[:], AF.Ln), reads=[dtk], writes=[lndk])
        nac, nack = na_r.next()
        P.op("vector", lambda e, nac=nac, pac=pac, lnd=lnd: e.scalar_tensor_tensor(
            nac[:], pac[:, 128:136], -1.0, lnd[:], ALU.mult, ALU.add), reads=[pack, lndk], writes=[nack])
        cbs, cbk = cb_r.next()
        for hh in range(4):
            P.op("scalar", lambda e, cbs=cbs, pac=pac, hh=hh: e.copy(cbs[:, hh * 128:(hh + 1) * 128], pac[:, 0:128]),
                 reads=[pack], writes=[cbk])
        return dict(t=t, u=u, ts=ts, xs=xs, xk=xk, bts=bts, btk=btk, CTs=CTs, CTk=CTk, pbs=pbs, nac=nac, nack=nack,
                    cbs=cbs, cbk=cbk)

    def stage_b(c):
        t, u, ts, xs, xk, bts, btk, CTs, CTk, pbs, nac, nack, cbs, cbk = (
            c[k] for k in ("t", "u", "ts", "xs", "xk", "bts", "btk", "CTs", "CTk", "pbs", "nac", "nack", "cbs", "cbk"))
        xw, xwk = xw_r.next()
        py, pyk = pY.next()
        e2s = []
        for half in range(2):
            pb, pbk = pbs[half]
            hsl = slice(half * 256, (half + 1) * 256)
            tm, tmk = tm_r.next()
            P.op("vector", lambda e, tm=tm, pb=pb, nac=nac, half=half: e.tensor_tensor(
                v4(tm[:], 128), v4(pb[:], 128),
                nac[:, 4 * half:4 * half + 4].unsqueeze(2).to_broadcast([128, 4, 128]), ALU.add),
                reads=[pbk, nack], writes=[tmk])
            P.op(SSD_POOL, lambda e, tm=tm: e.tensor_tensor(tm[:], tm[:], Mneg4[:], ALU.add),
                 reads=[tmk, "Mneg4"], writes=[tmk])
            E, Ek = e_r.next()
            P.op("scalar", lambda e, E=E, tm=tm: e.activation(E[:], tm[:], AF.Exp), reads=[tmk], writes=[Ek])
            ST, STk = st_r.next()
            P.op(SSD_POOL, lambda e, ST=ST, E=E, cbs=cbs: e.tensor_tensor(ST[:], E[:], cbs[:], ALU.mult),
                 reads=[Ek, cbk], writes=[STk])
            E2, E2k = e2_r.next()
            P.op("scalar", lambda e, E2=E2, pb=pb: e.activation(E2[:], pb[:], AF.Exp), reads=[pbk], writes=[E2k])
            e2s.append((E2, E2k))
            Ce, Cek = ce_r.next()
            P.op("vector", lambda e, Ce=Ce, CTs=CTs, E2=E2: e.tensor_tensor(
                v4(Ce[:], 128), v4(E2[:], 128), CTs[:].unsqueeze(1).to_broadcast([128, 4, 128]), ALU.mult),
                reads=[CTk, E2k], writes=[Cek])
            P.op("vector", lambda e, xw=xw, xs=xs, E=E, hsl=hsl: e.tensor_tensor(
                xw[:, hsl].rearrange("p (h m) -> p h m", h=4), xs[:, hsl].rearrange("p (h m) -> p h m", h=4),
                v4(E[:], 128)[:, :, 127:128].to_broadcast([128, 4, 64]), ALU.mult),
                reads=[xk, Ek], writes=[(xwk, half)])
            for hh in range(4):
                h = half * 4 + hh
                hs = slice(h * 64, (h + 1) * 64)
                P.op("tensor", lambda e, py=py, ST=ST, xs=xs, hs=hs, hh=hh: e.matmul(
                    py[:, hs], ST[:, hh * 128:(hh + 1) * 128], xs[:, hs], start=True, stop=False),
                    reads=[STk, xk], writes=[pyk], signal=False)
                P.op("tensor", lambda e, py=py, Ce=Ce, u=u, hs=hs, hh=hh: e.matmul(
                    py[:, hs], Ce[:, hh * 128:(hh + 1) * 128], Sb[u][:, hs], start=False, stop=True),
                    reads=[Cek, ("Sb", u)], writes=[pyk], signal=(hh == 3))
        ysb, yk = y_r.next()
        P.op("scalar", lambda e, ysb=ysb, py=py: e.copy(ysb[:], py[:]), reads=[pyk], writes=[yk])
        P.dma("sync", y_d[u][ts, :], ysb[:], reads=[yk], key=("dma_out", yk))
        ps, psk = pS.next()
        P.op("tensor", lambda e, ps=ps, bts=bts, xw=xw: e.matmul(ps[:], bts[:], xw[:], start=True, stop=True),
             reads=[btk, (xwk, 0), (xwk, 1)], writes=[psk])
        for half in range(2):
            hsl = slice(half * 256, (half + 1) * 256)
            E2, E2k = e2s[half]
            tS, tSk = ts_r.next()
            P.op("vector", lambda e, tS=tS, u=u, E2=E2, hsl=hsl: e.tensor_tensor(
                tS[:].rearrange("p (h m) -> p h m", h=4), S[u][:, hsl].rearrange("p (h m) -> p h m", h=4),
                v4(E2[:], 128)[:, :, 127:128].to_broadcast([128, 4, 64]), ALU.mult),
                reads=[("S", u), E2k], writes=[tSk])
            P.op("vector", lambda e, tS=tS, u=u, ps=ps, hsl=hsl: e.tensor_tensor(
                S[u][:, hsl], tS[:], ps[:, hsl], ALU.add), reads=[tSk, psk], writes=[("S", u)])
        P.op("scalar", lambda e, u=u: e.copy(Sb[u][:], S[u][:]), reads=[("S", u)], writes=[("Sb", u)])

    steps = [(t, u) for t in range(ntile) for u in range(NU)]
    prev = None
    for (t, u) in steps:
        cur = stage_a(t, u)
        if prev is not None:
            stage_b(prev)
        prev = cur
    stage_b(prev)
    return P.build()


def _ssd_consts():
    j = np.arange(128)[:, None]
    i = np.arange(128)[None, :]
    T = (j <= i).astype(np.float32)
    Mneg = np.where(j <= i, np.float32(0.0), np.float32(-30000.0)).astype(np.float32)
    return T, np.ascontiguousarray(np.tile(Mneg, (1, 4)))


def run_ssd_core(lat, ctx):
    nc = _prog("us_ssd", build_us_ssd)
    T, Mneg4 = _ssd_consts()
    maps = []
    for c in range(8):
        b, g0 = c // 4, 2 * (c % 4)
        x, Btok, BT, CT, a, dt = [], [], [], [], [], []
        for gg in range(2):
            g = g0 + gg
            for d in range(2):
                def seq(lo, hi):
                    cc, ll = ctx[b][:, lo:hi], lat[b][:, lo:hi]
                    if d == 1:
                        cc, ll = cc[::-1], ll[::-1]
                    return np.concatenate([cc, ll], axis=0)
                x.append(seq(4096 + g * 512, 4096 + (g + 1) * 512))
                Bs = seq(8192 + g * 128, 8192 + (g + 1) * 128)
                Cs = seq(9216 + g * 128, 9216 + (g + 1) * 128)
                Btok.append(Bs)
                BT.append(Bs.T)
                CT.append(Cs.T)
                dt.append(seq(10240 + d * 64 + g * 8, 10240 + d * 64 + (g + 1) * 8))
                a.append(seq(10368 + d * 64 + g * 8, 10368 + d * 64 + (g + 1) * 8))
        st = lambda l: np.ascontiguousarray(np.stack(l))
        maps.append(dict(x=st(x), Btok=st(Btok), BT=st(BT), CT=st(CT), a=st(a), dt=st(dt), T=T, Mneg4=Mneg4))
    res = _launch(nc, maps)
    y_lat = np.empty((2, 2, 4096, 4096), np.float32)
    y_ctx = np.empty((2, 2, 256, 4096), np.float32)
    for c in range(8):
        b, g0 = c // 4, 2 * (c % 4)
        for gg in range(2):
            g = g0 + gg
            for d in range(2):
                y = res[c]["y"][gg * 2 + d]
                yc, yl = y[:256], y[256:]
                if d == 1:
                    yc, yl = yc[::-1], yl[::-1]
                y_lat[d, b][:, g * 512:(g + 1) * 512] = yl
                y_ctx[d, b][:, g * 512:(g + 1) * 512] = yc
    return y_lat, y_ctx


def layer_ssd(x_lat, x_ctx, com, w_in, conv_w, conv_b, dt_bias, a_log, d_skip, norm_g_s, w_out):
    xs = to_ts(x_lat, x_ctx)
    nc1 = _prog("ts1", build_ts1, "ssd")
    cw, cb = _vec(conv_w), _vec(conv_b)
    dtb = np.ascontiguousarray(dt_bias.reshape(128, 1).astype(np.float32))
    alog = np.ascontiguousarray(a_log.reshape(128, 1).astype(np.float32))
    res1 = _launch(nc1, [dict(xT=xs[c], W=w_in, cw=cw, cb=cb, dtb=dtb, alog=alog, **com[c]) for c in range(8)])
    lat, ctx = from_ts([r["outT"] for r in res1])
    y_lat, y_ctx = run_ssd_core(lat, ctx)
    yF = to_ts(y_lat[0], y_ctx[0])
    yB = to_ts(y_lat[1], y_ctx[1])
    dsk = _vec(np.repeat(d_skip, 64, axis=1))
    sng = _vec(norm_g_s)
    nc2 = _prog("ts2", build_ts2, "ssd")
    res = _launch(nc2, [dict(xT=xs[c], yF=yF[c], yB=yB[c], xcT=np.ascontiguousarray(res1[c]["outT"][4096:8192]),
                             zsT=np.ascontiguousarray(res1[c]["outT"][0:4096]), dsk=dsk, sng=sng, W=w_out, **com[c])
                      for c in range(8)])
    return from_ts([r["outT"] for r in res])


def kernel(**inputs):
    p = {k: np.asarray(v) for k, v in inputs.items()}
    mod_full = run_mod(p["c"], p["c_ctx"], p["w_mod"], p["b_mod"])
    x_lat = np.asarray(p["x"], np.float32)
    x_ctx = np.asarray(p["ctx"], np.float32)
    for i in range(4):
        com = _common(mod_full, p["norm_g"], i)
        x_lat, x_ctx = run_mixer_layer(i, x_lat, x_ctx, com, p)
        x_lat, x_ctx = run_ffn(x_lat, x_ctx, com, p["ffn_w_up"][i], p["ffn_conv_w"][i], p["ffn_conv_b"][i],
                               p["ffn_w_down"][i])
    return np.ascontiguousarray(x_lat.astype(np.float32))
```

```python
import numpy as np
from contextlib import ExitStack
import concourse.bass as bass
import concourse.mybir as mybir
from concourse.bass_utils import run_bass_kernel_spmd

F32 = mybir.dt.float32
BF16 = mybir.dt.bfloat16
AF = mybir.ActivationFunctionType
ALU = mybir.AluOpType
AX = mybir.AxisListType

SAME_ENGINE_SYNC = True


class Prog:
    ENGINES = ("tensor", "vector", "scalar", "gpsimd", "sync")

    def __init__(self):
        self.nc = bass.Bass("TRN2", target_bir_lowering=False)
        self.ops = []
        self.stack = ExitStack()
        self.stack.enter_context(self.nc.allow_low_precision("bf16 matmul operands, fp32 accumulation"))
        self.n_names = 0

    def dram_in(self, name, shape, dtype=F32):
        return self.nc.dram_tensor(name, list(shape), dtype, kind="ExternalInput").ap()

    def dram_out(self, name, shape, dtype=F32):
        return self.nc.dram_tensor(name, list(shape), dtype, kind="ExternalOutput").ap()

    def sbuf(self, shape, dtype=F32, name=None):
        self.n_names += 1
        name = name or f"sb{self.n_names}"
        return self.stack.enter_context(self.nc.sbuf_tensor(name, list(shape), dtype))

    def psum(self, shape, dtype=F32, name=None):
        self.n_names += 1
        name = name or f"ps{self.n_names}"
        return self.stack.enter_context(self.nc.psum_tensor(name, list(shape), dtype))

    def op(self, engine, fn, reads=(), writes=(), signal=True):
        self.ops.append(dict(engine=engine, fn=fn, reads=list(reads), writes=list(writes),
                             kind="c", signal=signal))

    def dma(self, queue, out, in_, reads=(), writes=(), key=None, **kw):
        if key is None:
            key = ("dma", writes[0] if writes else reads[0])
        self.ops.append(dict(engine=queue, fn=lambda e: e.dma_start(out=out, in_=in_, **kw),
                             reads=list(reads), writes=list(writes), kind="d", key=key, signal=True))

    def build(self):
        nc = self.nc
        ops = self.ops
        eng_count = {e: 0 for e in self.ENGINES}
        dma_count = {}
        per_eng = {e: [] for e in self.ENGINES}
        for i, o in enumerate(ops):
            o["idx"] = i
            per_eng[o["engine"]].append(o)
        for e, lst in per_eng.items():
            c = 0
            pend = []
            for o in lst:
                if o["kind"] == "d":
                    k = o["key"]
                    dma_count[k] = dma_count.get(k, 0) + 16
                    o["token"] = (("D", k), dma_count[k])
                    o["tok_idx"] = o["idx"]
                else:
                    if o["signal"]:
                        c += 1
                        o["token"] = (("E", e), c)
                        o["tok_idx"] = o["idx"]
                        for p in pend:
                            p["token"] = (("E", e), c)
                            p["tok_idx"] = o["idx"]
                        pend = []
                    else:
                        pend.append(o)
            assert not pend, "last op on engine must signal"
        last_w = {}
        readers = {}
        for o in ops:
            deps = {}

            def add(src, same_ok):
                s, v = src["token"]
                if src["engine"] != o["engine"]:
                    assert src["tok_idx"] < o["idx"], ("forward dependency (deadlock)", src["idx"], o["idx"])
                deps[s] = max(deps.get(s, 0), v)

            for k in o["reads"]:
                if k in last_w:
                    add(last_w[k], False)
            for k in o["writes"]:
                if k in last_w:
                    add(last_w[k], True)
                for r in readers.get(k, ()):
                    add(r, True)
            o["deps"] = deps
            for k in o["reads"]:
                readers.setdefault(k, []).append(o)
            for k in o["writes"]:
                last_w[k] = o
                readers[k] = []
        sem_keys = []
        for o in ops:
            s = o["token"][0]
            if s not in sem_keys:
                sem_keys.append(s)
        self.n_sems = len(sem_keys)
        sems = {}
        for i, s in enumerate(sem_keys):
            sems[s] = self.stack.enter_context(nc.semaphore(f"s{i}"))
        block = self.stack.enter_context(nc.Block())
        final_dma = {}
        for o in ops:
            if o["kind"] == "d":
                final_dma.setdefault(o["engine"], {})[o["token"][0]] = o["token"][1]

        def make(e, lst):
            def body(eng):
                seen = {}
                own = ("E", e)
                for o in lst:
                    for s, v in o["deps"].items():
                        if s == own:
                            if not SAME_ENGINE_SYNC or e == "tensor":
                                continue
                            if v >= o["token"][1] and o["kind"] == "c":
                                continue
                        if seen.get(s, 0) >= v:
                            continue
                        eng.wait_ge(sems[s], v)
                        seen[s] = v
                    ins = o["fn"](eng)
                    if o["kind"] == "d":
                        ins.then_inc(sems[o["token"][0]], 16)
                    elif o["signal"]:
                        ins.then_inc(sems[o["token"][0]], 1)
                for s, v in final_dma.get(e, {}).items():
                    if seen.get(s, 0) < v:
                        eng.wait_ge(sems[s], v)
            return body

        for e in self.ENGINES:
            lst = per_eng[e]
            if not lst:
                continue
            getattr(block, e)(make(e, lst))
        self.stack.close()
        return nc


def run(prog_nc, in_maps, n=8, trace=False):
    return run_bass_kernel_spmd(prog_nc, in_maps, core_ids=list(range(n)), trace=trace)


D_MODEL = 2048
KC = D_MODEL // 128
NT = 1092
TTS = [(0, 364), (364, 728), (728, 1092)]
SEGS = [(0, 66), (66, 1092)]
HALO_COLS = [0, 65, 66, 1091]
RMS_EPS = 1e-6


class Rot:
    def __init__(self, P, n, shape, dtype, name):
        self.bufs = [P.sbuf(shape, dtype, name=f"{name}{i}") for i in range(n)]
        self.name = name
        self.i = 0

    def next(self):
        j = self.i % len(self.bufs)
        self.i += 1
        return self.bufs[j], (self.name, j)


class PsRot:
    def __init__(self, P, n, name="pt", cols=512):
        self.bufs = [P.psum([128, cols], F32, name=f"{name}{i}") for i in range(n)]
        self.name = name
        self.i = 0

    def next(self):
        j = self.i % len(self.bufs)
        self.i += 1
        return self.bufs[j], (self.name, j)


def load_small(P, name, shape, queue="sync"):
    d = P.dram_in(name, shape)
    t = P.sbuf(shape, F32, name=name + "_sb")
    P.dma(queue, t[:], d, writes=[name])
    return t


def load_w(P, W, col0, ncols, rot, queue="gpsimd"):
    wt, wk = rot.next()
    P.dma(queue, wt[:, :, 0:ncols], W[:, col0:col0 + ncols].rearrange("(kc p) n -> p kc n", p=128),
          writes=[wk])
    return wt, wk


def make_ones(P):
    ones = P.sbuf([128, 128], BF16, name="ones")
    P.op("vector", lambda e: e.memset(ones[:], 1.0), writes=["ones"])
    return ones


def rms_stats_begin(P, n=3, name="st"):
    return [P.psum([128, 512], F32, name=f"{name}{i}") for i in range(n)]


def rms_accum(P, ones, st, st_name, src, src_key, sqrot, first, last, ncols=NT, tts=TTS):
    sq, sqk = sqrot.next()
    P.op("scalar", lambda e: e.activation(sq[:, 0:ncols], src, AF.Square), reads=[src_key], writes=[sqk])
    for ti, (c0, c1) in enumerate(tts):
        P.op("tensor", lambda e, ti=ti, c0=c0, c1=c1: e.matmul(st[ti][:, 0:c1 - c0], ones[:], sq[:, c0:c1],
                                                                 start=first, stop=last),
             reads=[sqk, "ones"], writes=[(st_name, ti)], signal=True)


def rms_finish(P, st, st_name, rstd, rstd_key, dim, tts=TTS):
    for ti, (c0, c1) in enumerate(tts):
        P.op("vector", lambda e, ti=ti, c0=c0, c1=c1: e.tensor_scalar(
            rstd[:, c0:c1], st[ti][:, 0:c1 - c0], 1.0 / dim, RMS_EPS, ALU.mult, ALU.add),
            reads=[(st_name, ti)], writes=[(rstd_key, ti)])
        P.op("scalar", lambda e, c0=c0, c1=c1: e.activation(rstd[:, c0:c1], rstd[:, c0:c1], AF.Sqrt),
             reads=[(rstd_key, ti)], writes=[(rstd_key, ti)])
        P.op("vector", lambda e, c0=c0, c1=c1: e.reciprocal(rstd[:, c0:c1], rstd[:, c0:c1]),
             reads=[(rstd_key, ti)], writes=[(rstd_key, ti)])


def rstd_keys(rstd_key):
    return [(rstd_key, i) for i in range(3)]


def norm_mod(P, xT, ones, g_ap, sc_t, sh_t, hT, hm, sts, xrot, sqrot, tmprot, rstd):
    s1 = P.sbuf([128, 2, KC], F32, name="nm_s1%d" % P.n_names)
    for s in range(2):
        P.op("vector", lambda e, s=s: e.scalar_tensor_tensor(s1[:, s, :], sc_t[:, s, :], 1.0, g_ap,
                                                            ALU.add, ALU.mult),
             reads=["mod", "ng"], writes=[("s1", id(s1))])
    for kc in range(KC):
        xb, xk = xrot.next()
        P.dma("sync", xb[:], xT[kc * 128:(kc + 1) * 128, :], writes=[xk])
        rms_accum(P, ones, sts, "st", xb[:], xk, sqrot, kc == 0, kc == KC - 1)
    rms_finish(P, sts, "st", rstd, "rstd", D_MODEL)
    for kc in range(KC):
        xb, xk = xrot.next()
        P.dma("sync", xb[:], xT[kc * 128:(kc + 1) * 128, :], writes=[xk])
        tb, tk = tmprot.next()
        P.op("vector", lambda e, xb=xb, tb=tb: e.tensor_tensor(tb[:], xb[:], rstd[:], ALU.mult),
             reads=[xk] + rstd_keys("rstd"), writes=[tk])
        for s, (c0, c1) in enumerate(SEGS):
            P.op("scalar", lambda e, tb=tb, kc=kc, s=s, c0=c0, c1=c1: e.activation(
                hT[:, kc, c0:c1], tb[:, c0:c1], AF.Identity, bias=sh_t[:, s, kc:kc + 1],
                scale=s1[:, s, kc:kc + 1]),
                reads=[tk, ("s1", id(s1)), "mod"], writes=[("hT", kc)])
    for i, col in enumerate(HALO_COLS):
        P.op("vector", lambda e, i=i, col=col: e.tensor_scalar(
            hT[:, :, col:col + 1], hT[:, :, col:col + 1], hm[:, i:i + 1], None, ALU.mult),
            reads=[("hT", kc) for kc in range(KC)] + ["hm"], writes=[("hT", kc) for kc in range(KC)])


def resid_norm(P, fT_d, xT_d, outT_d, rstd2, gate_t, xrot, frot, orot):
    for kc in range(KC):
        xb, xk = xrot.next()
        P.dma("sync", xb[:], xT_d[kc * 128:(kc + 1) * 128, :], writes=[xk])
        fb, fk = frot.next()
        P.dma("sync", fb[:], fT_d[kc * 128:(kc + 1) * 128, :], reads=[("fT_d", kc)], writes=[fk])
        P.op("vector", lambda e, fb=fb: e.tensor_tensor(fb[:], fb[:], rstd2[:], ALU.mult),
             reads=[fk] + rstd_keys("rstd2"), writes=[fk])
        ob, ok = orot.next()
        for s, (c0, c1) in enumerate(SEGS):
            P.op("vector", lambda e, fb=fb, xb=xb, ob=ob, s=s, kc=kc, c0=c0, c1=c1: e.scalar_tensor_tensor(
                ob[:, c0:c1], fb[:, c0:c1], gate_t[:, s, kc:kc + 1], xb[:, c0:c1], ALU.mult, ALU.add),
                reads=[fk, xk, "gate"], writes=[ok])
        P.dma("sync", outT_d[kc * 128:(kc + 1) * 128, :], ob[:], reads=[ok], key=("dma_out", ok))


def make_gate(P, g_ap, gt_t, name):
    gate = P.sbuf([128, 2, KC], F32, name=name)
    for s in range(2):
        P.op("vector", lambda e, s=s: e.tensor_tensor(gate[:, s, :], gt_t[:, s, :], g_ap, ALU.mult),
             reads=["mod", "ng"], writes=["gate"])
    return gate


def load_mod(P):
    mod_d = P.dram_in("mod", [128, 6, 2, KC])
    mod = P.sbuf([128, 6, 2, KC], F32, name="mod_sb")
    P.dma("sync", mod[:], mod_d, writes=["mod"])
    ng = load_small(P, "ng", [128, 4, KC])
    hm = load_small(P, "hm", [128, 4])
    return mod, ng, hm


def build_ffn(FH=5632):
    NJ = FH // 128
    P = Prog()
    xT = P.dram_in("xT", [D_MODEL, NT])
    w_up = P.dram_in("w_up", [D_MODEL, 2 * FH])
    w_down = P.dram_in("w_down", [FH, D_MODEL])
    cw = load_small(P, "cw", [128, 3, 2 * NJ])
    cb = load_small(P, "cb", [128, 2 * NJ])
    mod, ng, hm = load_mod(P)
    outT = P.dram_out("outT", [D_MODEL, NT])
    fT_d = P.dram_out("fT", [D_MODEL, NT])
    ones = make_ones(P)
    hT = P.sbuf([128, KC, NT], BF16, name="hT")
    gT = P.sbuf([128, NJ, NT], BF16, name="gT")
    rstd = P.sbuf([128, NT], F32, name="rstd")
    rstd2 = P.sbuf([128, NT], F32, name="rstd2")
    xrot = Rot(P, 2, [128, NT], F32, "xb")
    frot = xrot
    sqrot = Rot(P, 2, [128, NT], BF16, "sq")
    wuprot = Rot(P, 2, [128, KC, 256], BF16, "wup")
    wdnrot = Rot(P, 2, [128, NJ, 128], BF16, "wdn")
    urot = Rot(P, 2, [128, NT], F32, "u")
    crot = Rot(P, 2, [128, NT], F32, "c")
    sts = rms_stats_begin(P)
    pts = PsRot(P, 5)

    norm_mod(P, xT, ones, ng[:, 2, :], mod[:, 4, :, :], mod[:, 3, :, :], hT, hm, sts, xrot, sqrot, frot, rstd)
    gate = make_gate(P, ng[:, 3, :], mod[:, 5, :, :], "gate3")
    hkeys = [("hT", kc) for kc in range(KC)]

    for j in range(NJ):
        wt, wk = wuprot.next()
        P.dma("gpsimd", wt[:, :, 0:128], w_up[:, j * 128:(j + 1) * 128].rearrange("(kc p) n -> p kc n", p=128),
              writes=[wk], key=("dmaw", wk))
        P.dma("gpsimd", wt[:, :, 128:256],
              w_up[:, FH + j * 128:FH + (j + 1) * 128].rearrange("(kc p) n -> p kc n", p=128),
              writes=[wk], key=("dmaw", wk))
        cs = []
        for half in range(2):
            jj = half * NJ + j
            ub, uk = urot.next()
            for ti, (c0, c1) in enumerate(TTS):
                pt, pk = pts.next()
                for kc in range(KC):
                    P.op("tensor", lambda e, pt=pt, wt=wt, kc=kc, half=half, c0=c0, c1=c1: e.matmul(
                        pt[:, 0:c1 - c0], wt[:, kc, half * 128:(half + 1) * 128], hT[:, kc, c0:c1],
                        start=(kc == 0), stop=(kc == KC - 1)),
                        reads=[wk] + hkeys, writes=[pk], signal=(kc == KC - 1))
                P.op("scalar", lambda e, pt=pt, ub=ub, c0=c0, c1=c1: e.copy(ub[:, c0:c1], pt[:, 0:c1 - c0]),
                     reads=[pk], writes=[(uk, ti)])
            uks = [(uk, ti) for ti in range(3)]
            cbuf, ck = crot.next()
            P.op("scalar", lambda e, ub=ub, cbuf=cbuf, jj=jj: e.activation(
                cbuf[:, 1:NT - 1], ub[:, 1:NT - 1], AF.Identity, bias=cb[:, jj:jj + 1], scale=cw[:, 1, jj:jj + 1]),
                reads=uks + ["cw", "cb"], writes=[ck])
            P.op("vector", lambda e, ub=ub, cbuf=cbuf, jj=jj: e.scalar_tensor_tensor(
                cbuf[:, 1:NT - 1], ub[:, 0:NT - 2], cw[:, 0, jj:jj + 1], cbuf[:, 1:NT - 1], ALU.mult, ALU.add),
                reads=uks + [ck, "cw"], writes=[ck])
            P.op("vector", lambda e, ub=ub, cbuf=cbuf, jj=jj: e.scalar_tensor_tensor(
                cbuf[:, 1:NT - 1], ub[:, 2:NT], cw[:, 2, jj:jj + 1], cbuf[:, 1:NT - 1], ALU.mult, ALU.add),
                reads=uks + [ck, "cw"], writes=[ck])
            cs.append((cbuf, ck))
        (ca, cka), (cv, ckv) = cs
        P.op("scalar", lambda e, ca=ca: e.activation(ca[:, 1:NT - 1], ca[:, 1:NT - 1], AF.Silu),
             reads=[cka], writes=[cka])
        P.op("vector", lambda e, ca=ca, cv=cv, j=j: e.tensor_tensor(
            gT[:, j, 1:NT - 1], ca[:, 1:NT - 1], cv[:, 1:NT - 1], ALU.mult),
            reads=[cka, ckv], writes=[("gT", j)])
    P.op("vector", lambda e: e.memset(gT[:, :, 0:1], 0.0), writes=[("gT", j) for j in range(NJ)],
         reads=[("gT", j) for j in range(NJ)])
    P.op("vector", lambda e: e.memset(gT[:, :, NT - 1:NT], 0.0), writes=[("gT", j) for j in range(NJ)],
         reads=[("gT", j) for j in range(NJ)])
    gkeys = [("gT", j) for j in range(NJ)]

    for d in range(KC):
        wt, wk = wdnrot.next()
        P.dma("gpsimd", wt[:], w_down[:, d * 128:(d + 1) * 128].rearrange("(j p) n -> p j n", p=128),
              writes=[wk], key=("dmaw", wk))
        fb, fk = frot.next()
        for ti, (c0, c1) in enumerate(TTS):
            pt, pk = pts.next()
            for j in range(NJ):
                P.op("tensor", lambda e, pt=pt, wt=wt, j=j, c0=c0, c1=c1: e.matmul(
                    pt[:, 0:c1 - c0], wt[:, j, :], gT[:, j, c0:c1], start=(j == 0), stop=(j == NJ - 1)),
                    reads=[wk] + gkeys, writes=[pk], signal=(j == NJ - 1))
            P.op("scalar", lambda e, pt=pt, fb=fb, c0=c0, c1=c1: e.copy(fb[:, c0:c1], pt[:, 0:c1 - c0]),
                 reads=[pk], writes=[fk])
        P.dma("sync", fT_d[d * 128:(d + 1) * 128, :], fb[:], reads=[fk], writes=[("fT_d", d)],
              key=("dma_out", fk))
        rms_accum(P, ones, sts, "st", fb[:], fk, sqrot, d == 0, d == KC - 1)
    rms_finish(P, sts, "st", rstd2, "rstd2", D_MODEL)
    resid_norm(P, fT_d, xT, outT, rstd2, gate, xrot, frot, urot)
    return P.build()


def build_mod():
    P = Prog()
    wm = P.dram_in("wm", [D_MODEL, 6144])
    bm = load_small(P, "bm", [128, 48])
    cT = load_small(P, "cT", [128, KC, 3])
    modo = P.dram_out("modo", [128, 48, 3])
    sc = P.sbuf([128, KC, 3], BF16, name="silu_c")
    P.op("scalar", lambda e: e.activation(sc[:], cT[:], AF.Silu), reads=["cT"], writes=["sc"])
    res = P.sbuf([128, 48, 3], F32, name="res")
    wrot = Rot(P, 3, [128, KC, 128], BF16, "w")
    pts = PsRot(P, 4)
    for j in range(48):
        wt, wk = load_w(P, wm, j * 128, 128, wrot)
        pt, pk = pts.next()
        for kc in range(KC):
            P.op("tensor", lambda e, pt=pt, wt=wt, kc=kc: e.matmul(pt[:, 0:3], wt[:, kc, :], sc[:, kc, :],
                                                                  start=(kc == 0), stop=(kc == KC - 1)),
                 reads=[wk, "sc"], writes=[pk], signal=(kc == KC - 1))
        P.op("scalar", lambda e, pt=pt, j=j: e.activation(res[:, j, :], pt[:, 0:3], AF.Identity,
                                                         bias=bm[:, j:j + 1]),
             reads=[pk, "bm"], writes=["res"])
    P.dma("sync", modo, res[:], reads=["res"])
    return P.build()


def build_ts1(kind, layer_idx=1):
    N = {"na": 6144, "hg": 10240, "ssd": 10368}[kind]
    NOUT = {"na": 6144, "hg": 10240, "ssd": 10368 + 128}[kind]
    P = Prog()
    xT = P.dram_in("xT", [D_MODEL, NT])
    W = P.dram_in("W", [D_MODEL, N])
    mod, ng, hm = load_mod(P)
    outT = P.dram_out("outT", [NOUT, NT])
    ones = make_ones(P)
    hT = P.sbuf([128, KC, NT], BF16, name="hT")
    rstd = P.sbuf([128, NT], F32, name="rstd")
    xrot = Rot(P, 3, [128, NT], F32, "xb")
    sqrot = Rot(P, 2, [128, NT], BF16, "sq")
    tmprot = Rot(P, 2, [128, NT], F32, "tmp")
    wrot = Rot(P, 3, [128, KC, 128], BF16, "w")
    srot = Rot(P, 4, [128, NT], F32, "stage")
    sts = rms_stats_begin(P)
    pts = PsRot(P, 5)
    if kind == "ssd":
        cw = load_small(P, "cw", [128, 3, 48])
        cb = load_small(P, "cb", [128, 48])
        dtb = load_small(P, "dtb", [128, 1])
        alog = load_small(P, "alog", [128, 1])
        negA = P.sbuf([128, 1], F32, name="negA")
        P.op("scalar", lambda e: e.activation(negA[:], alog[:], AF.Exp), reads=["alog"], writes=["negA"])
        P.op("vector", lambda e: e.tensor_scalar(negA[:], negA[:], -1.0, None, ALU.mult),
             reads=["negA"], writes=["negA"])
    if kind == "hg":
        lbp = load_small(P, "lbp", [128, 2, 4, KC])
        el = P.sbuf([128, 2, 4, KC], F32, name="el")
        P.op("scalar", lambda e: e.activation(el[:], lbp[:], AF.Exp), reads=["lbp"], writes=["el"])
        ssum = P.sbuf([128, 2, KC], F32, name="ssum")
        num = P.sbuf([128, 2, KC], F32, name="num")
        lb = P.sbuf([128, 2, KC], F32, name="lb")
        oml = P.sbuf([128, 2, KC], F32, name="oml")
        P.op("vector", lambda e: e.tensor_tensor(ssum[:], el[:, :, 0, :], el[:, :, 1, :], ALU.add),
             reads=["el"], writes=["ssum"])
        for l in (2, 3):
            P.op("vector", lambda e, l=l: e.tensor_tensor(ssum[:], ssum[:], el[:, :, l, :], ALU.add),
                 reads=["el", "ssum"], writes=["ssum"])
        P.op("vector", lambda e: e.tensor_copy(num[:], el[:, :, 1, :]), reads=["el"], writes=["num"])
        for l in range(2, layer_idx + 1):
            P.op("vector", lambda e, l=l: e.tensor_tensor(num[:], num[:], el[:, :, l, :], ALU.add),
                 reads=["el", "num"], writes=["num"])
        P.op("vector", lambda e: e.reciprocal(ssum[:], ssum[:]), reads=["ssum"], writes=["ssum"])
        P.op("vector", lambda e: e.tensor_tensor(lb[:], num[:], ssum[:], ALU.mult),
             reads=["num", "ssum"], writes=["lb"])
        P.op("vector", lambda e: e.tensor_scalar(oml[:], lb[:], -1.0, 1.0, ALU.mult, ALU.add),
             reads=["lb"], writes=["oml"])

    norm_mod(P, xT, ones, ng[:, 0, :], mod[:, 1, :, :], mod[:, 0, :, :], hT, hm, sts, xrot, sqrot, tmprot, rstd)
    hkeys = [("hT", kc) for kc in range(KC)]

    def out_dma(row0, sb, sk):
        P.dma("sync", outT[row0:row0 + 128, :], sb[:], reads=[sk], key=("dma_out", sk))

    for n in range(N // 128):
        wt, wk = load_w(P, W, n * 128, 128, wrot)
        pks = []
        for ti, (c0, c1) in enumerate(TTS):
            pt, pk = pts.next()
            for kc in range(KC):
                P.op("tensor", lambda e, pt=pt, wt=wt, kc=kc, c0=c0, c1=c1: e.matmul(
                    pt[:, 0:c1 - c0], wt[:, kc, :], hT[:, kc, c0:c1], start=(kc == 0), stop=(kc == KC - 1)),
                    reads=[wk] + hkeys, writes=[pk], signal=(kc == KC - 1))
            pks.append((pt, pk, c0, c1))
        if kind == "na":
            ep = "copy"
        elif kind == "hg":
            ep = "silu" if (n < 16 or n >= 64) else ("copy" if n < 32 else "hgf")
        else:
            ep = "silu" if n < 32 else ("conv" if n < 80 else "dt")
        sb, sk = srot.next()
        if ep in ("copy", "silu"):
            fn = AF.Copy if ep == "copy" else AF.Silu
            for (pt, pk, c0, c1) in pks:
                P.op("scalar", lambda e, pt=pt, sb=sb, c0=c0, c1=c1, fn=fn: e.activation(
                    sb[:, c0:c1], pt[:, 0:c1 - c0], fn), reads=[pk], writes=[sk])
            out_dma(n * 128, sb, sk)
        elif ep == "hgf":
            d, kc_f = (0, n - 32) if n < 48 else (1, n - 48)
            for (pt, pk, c0, c1) in pks:
                P.op("scalar", lambda e, pt=pt, sb=sb, c0=c0, c1=c1: e.activation(
                    sb[:, c0:c1], pt[:, 0:c1 - c0], AF.Sigmoid), reads=[pk], writes=[sk])
            P.op("vector", lambda e, sb=sb, d=d, kc_f=kc_f: e.tensor_scalar(
                sb[:], sb[:], oml[:, d, kc_f:kc_f + 1], lb[:, d, kc_f:kc_f + 1], ALU.mult, ALU.add),
                reads=[sk, "oml", "lb"], writes=[sk])
            P.op("scalar", lambda e, sb=sb: e.activation(sb[:], sb[:], AF.Ln), reads=[sk], writes=[sk])
            out_dma(n * 128, sb, sk)
        elif ep == "conv":
            jj = n - 32
            for (pt, pk, c0, c1) in pks:
                P.op("scalar", lambda e, pt=pt, sb=sb, c0=c0, c1=c1: e.copy(sb[:, c0:c1], pt[:, 0:c1 - c0]),
                     reads=[pk], writes=[sk])
            cbuf, ck = srot.next()
            P.op("scalar", lambda e, sb=sb, cbuf=cbuf, jj=jj: e.activation(
                cbuf[:, 1:NT - 1], sb[:, 1:NT - 1], AF.Identity, bias=cb[:, jj:jj + 1], scale=cw[:, 1, jj:jj + 1]),
                reads=[sk, "cw", "cb"], writes=[ck])
            P.op("vector", lambda e, sb=sb, cbuf=cbuf, jj=jj: e.scalar_tensor_tensor(
                cbuf[:, 1:NT - 1], sb[:, 0:NT - 2], cw[:, 0, jj:jj + 1], cbuf[:, 1:NT - 1], ALU.mult, ALU.add),
                reads=[sk, ck, "cw"], writes=[ck])
            P.op("vector", lambda e, sb=sb, cbuf=cbuf, jj=jj: e.scalar_tensor_tensor(
                cbuf[:, 1:NT - 1], sb[:, 2:NT], cw[:, 2, jj:jj + 1], cbuf[:, 1:NT - 1], ALU.mult, ALU.add),
                reads=[sk, ck, "cw"], writes=[ck])
            P.op("scalar", lambda e, cbuf=cbuf: e.activation(cbuf[:, 1:NT - 1], cbuf[:, 1:NT - 1], AF.Silu),
                 reads=[ck], writes=[ck])
            P.op("vector", lambda e, cbuf=cbuf: e.memset(cbuf[:, 0:1], 0.0), reads=[ck], writes=[ck])
            P.op("vector", lambda e, cbuf=cbuf: e.memset(cbuf[:, NT - 1:NT], 0.0), reads=[ck], writes=[ck])
            out_dma(n * 128, cbuf, ck)
        else:
            for (pt, pk, c0, c1) in pks:
                P.op("scalar", lambda e, pt=pt, sb=sb, c0=c0, c1=c1: e.activation(
                    sb[:, c0:c1], pt[:, 0:c1 - c0], AF.Exp, bias=dtb[:, 0:1]), reads=[pk, "dtb"], writes=[sk])
            P.op("scalar", lambda e, sb=sb: e.activation(sb[:], sb[:], AF.Ln, bias=1.0), reads=[sk], writes=[sk])
            out_dma(n * 128, sb, sk)
            ab, ak = srot.next()
            P.op("vector", lambda e, sb=sb, ab=ab: e.tensor_scalar(ab[:], sb[:], negA[:, 0:1], None, ALU.mult),
                 reads=[sk, "negA"], writes=[ak])
            out_dma(n * 128 + 128, ab, ak)
    return P.build()


def build_ts2(kind):
    KCI = 32 if kind == "ssd" else 16
    P = Prog()
    xT = P.dram_in("xT", [D_MODEL, NT])
    W = P.dram_in("W", [KCI * 128, D_MODEL])
    mod, ng, hm = load_mod(P)
    outT = P.dram_out("outT", [D_MODEL, NT])
    fT_d = P.nc.dram_tensor("fT_scratch", [D_MODEL, NT], F32).ap()
    ones = make_ones(P)
    actT = P.sbuf([128, KCI, NT], BF16, name="actT")
    rstd2 = P.sbuf([128, NT], F32, name="rstd2")
    arot = Rot(P, 4, [128, NT], F32, "a")
    xrot = Rot(P, 3, [128, NT], F32, "xb")
    sqrot = Rot(P, 2, [128, NT], BF16, "sq")
    wrot = Rot(P, 2, [128, KCI, 128], BF16, "w")
    sts = rms_stats_begin(P)
    pts = PsRot(P, 5)
    akeys = [("actT", kc) for kc in range(KCI)]
    rstd_s = None
    if kind == "na":
        oT = P.dram_in("oT", [D_MODEL, NT])
        for kc in range(KCI):
            ab, ak = arot.next()
            P.dma("sync", ab[:], oT[kc * 128:(kc + 1) * 128, :], writes=[ak])
            P.op("scalar", lambda e, ab=ab, kc=kc: e.copy(actT[:, kc, :], ab[:]), reads=[ak], writes=[("actT", kc)])
    elif kind == "hg":
        oF = P.dram_in("oF", [D_MODEL, NT])
        oB = P.dram_in("oB", [D_MODEL, NT])
        gsT = P.dram_in("gsT", [D_MODEL, NT])
        hgn = load_small(P, "hgn", [128, KC])
        rs = P.sbuf([128, NT], F32, name="rs_h")
        for kc in range(KCI):
            a1, k1 = arot.next()
            a2, k2 = arot.next()
            a3, k3 = arot.next()
            P.dma("sync", a1[:], oF[kc * 128:(kc + 1) * 128, :], writes=[k1])
            P.dma("sync", a2[:], oB[kc * 128:(kc + 1) * 128, :], writes=[k2])
            P.dma("sync", a3[:], gsT[kc * 128:(kc + 1) * 128, :], writes=[k3])
            P.op("vector", lambda e, a1=a1, a2=a2: e.tensor_tensor(a1[:], a1[:], a2[:], ALU.add),
                 reads=[k1, k2], writes=[k1])
            rms_accum(P, ones, sts, "st", a1[:], k1, sqrot, True, True)
            rms_finish(P, sts, "st", rs, "rs", 128)
            P.op("vector", lambda e, a1=a1: e.tensor_tensor(a1[:], a1[:], rs[:], ALU.mult),
                 reads=[k1] + rstd_keys("rs"), writes=[k1])
            P.op("vector", lambda e, a1=a1, a3=a3, kc=kc: e.scalar_tensor_tensor(
                actT[:, kc, :], a1[:], hgn[:, kc:kc + 1], a3[:], ALU.mult, ALU.mult),
                reads=[k1, k3, "hgn"], writes=[("actT", kc)])
    else:
        yF = P.dram_in("yF", [4096, NT])
        yB = P.dram_in("yB", [4096, NT])
        xcT = P.dram_in("xcT", [4096, NT])
        zsT = P.dram_in("zsT", [4096, NT])
        dsk = load_small(P, "dsk", [128, 2, 32])
        sng = load_small(P, "sng", [128, 32])
        dsum = P.sbuf([128, 32], F32, name="dsum")
        P.op("vector", lambda e: e.tensor_tensor(dsum[:], dsk[:, 0, :], dsk[:, 1, :], ALU.add),
             reads=["dsk"], writes=["dsum"])
        rstd_s = P.sbuf([128, NT], F32, name="rstd_s")
        for kc in range(KCI):
            a1, k1 = arot.next()
            a2, k2 = arot.next()
            a3, k3 = arot.next()
            a4, k4 = arot.next()
            P.dma("sync", a1[:], yF[kc * 128:(kc + 1) * 128, :], writes=[k1])
            P.dma("sync", a2[:], yB[kc * 128:(kc + 1) * 128, :], writes=[k2])
            P.dma("sync", a3[:], xcT[kc * 128:(kc + 1) * 128, :], writes=[k3])
            P.dma("sync", a4[:], zsT[kc * 128:(kc + 1) * 128, :], writes=[k4])
            P.op("vector", lambda e, a1=a1, a2=a2: e.tensor_tensor(a1[:], a1[:], a2[:], ALU.add),
                 reads=[k1, k2], writes=[k1])
            P.op("vector", lambda e, a1=a1, a3=a3, kc=kc: e.scalar_tensor_tensor(
                a1[:], a3[:], dsum[:, kc:kc + 1], a1[:], ALU.mult, ALU.add), reads=[k1, k3, "dsum"], writes=[k1])
            P.op("vector", lambda e, a1=a1, a4=a4: e.tensor_tensor(a1[:], a1[:], a4[:], ALU.mult),
                 reads=[k1, k4], writes=[k1])
            rms_accum(P, ones, sts, "st", a1[:], k1, sqrot, kc == 0, kc == KCI - 1)
            P.op("vector", lambda e, a1=a1, kc=kc: e.tensor_scalar(
                actT[:, kc, :], a1[:], sng[:, kc:kc + 1], None, ALU.mult), reads=[k1, "sng"], writes=[("actT", kc)])
        rms_finish(P, sts, "st", rstd_s, "rstd_s", 4096)

    gate = make_gate(P, ng[:, 1, :], mod[:, 2, :, :], "gate1")
    for d in range(KC):
        wt, wk = wrot.next()
        P.dma("gpsimd", wt[:], W[:, d * 128:(d + 1) * 128].rearrange("(j p) n -> p j n", p=128),
              writes=[wk], key=("dmaw", wk))
        fb, fk = xrot.next()
        for ti, (c0, c1) in enumerate(TTS):
            pt, pk = pts.next()
            for j in range(KCI):
                P.op("tensor", lambda e, pt=pt, wt=wt, j=j, c0=c0, c1=c1: e.matmul(
                    pt[:, 0:c1 - c0], wt[:, j, :], actT[:, j, c0:c1], start=(j == 0), stop=(j == KCI - 1)),
                    reads=[wk] + akeys, writes=[pk], signal=(j == KCI - 1))
            if rstd_s is None:
                P.op("scalar", lambda e, pt=pt, fb=fb, c0=c0, c1=c1: e.copy(fb[:, c0:c1], pt[:, 0:c1 - c0]),
                     reads=[pk], writes=[fk])
            else:
                P.op("vector", lambda e, pt=pt, fb=fb, c0=c0, c1=c1: e.tensor_tensor(
                    fb[:, c0:c1], pt[:, 0:c1 - c0], rstd_s[:, c0:c1], ALU.mult),
                    reads=[pk] + rstd_keys("rstd_s"), writes=[fk])
        P.dma("sync", fT_d[d * 128:(d + 1) * 128, :], fb[:], reads=[fk], writes=[("fT_d", d)],
              key=("dma_out", fk))
        rms_accum(P, ones, sts, "st", fb[:], fk, sqrot, d == 0, d == KC - 1)
    rms_finish(P, sts, "st", rstd2, "rstd2", D_MODEL)
    resid_norm(P, fT_d, xT, outT, rstd2, gate, xrot, arot, arot)
    return P.build()


NA_SCALE = 128 ** -0.5


def build_us_na(NU=4):
    P = Prog()
    qT_d = P.dram_in("qT", [NU, 128, 4096])
    kT_d = P.dram_in("kT", [NU, 128, 4096])
    qcT_d = P.dram_in("qcT", [NU, 128, 256])
    kcT_d = P.dram_in("kcT", [NU, 128, 256])
    v_d = P.dram_in("v", [NU, 4352, 128])
    bias_d = P.dram_in("biasT", [NU, 128, 8, 256])
    cos = load_small(P, "cos", [128, 4096])
    sin = load_small(P, "sin", [128, 4096])
    rot_d = P.dram_in("rotT", [128, 128])
    oT_d = P.dram_out("oT", [NU, 128, 4096])
    ocT_d = P.dram_out("ocT", [NU, 128, 256])
    rotb = P.sbuf([128, 128], BF16, name="rotb")
    P.dma("gpsimd", rotb[:], rot_d, writes=["rotb"])
    ones = make_ones(P)
    qrot = Rot(P, 2, [128, 4096], F32, "qf")
    qbrot = Rot(P, 2, [128, 512], BF16, "qb")
    t1rot = Rot(P, 2, [128, 512], F32, "t1")
    t2rot = Rot(P, 2, [128, 512], F32, "t2")
    qr = P.sbuf([128, 4096], BF16, name="qr")
    kr = P.sbuf([128, 4096], BF16, name="kr")
    qc = P.sbuf([128, 256], BF16, name="qc")
    kc = P.sbuf([128, 256], BF16, name="kc")
    ve = P.sbuf([128, 34, 128], BF16, name="ve")
    vo = P.sbuf([128, 32, 128], BF16, name="vo")
    bias = P.sbuf([128, 8, 256], F32, name="bias")
    oT = P.sbuf([128, 4096], F32, name="oT_sb")
    ocT = P.sbuf([128, 256], F32, name="ocT_sb")
    trot = Rot(P, 3, [128, 256], F32, "t")
    erot = Rot(P, 3, [128, 512], BF16, "e")
    rrot = Rot(P, 2, [128, 256], F32, "rec")
    ps_s = PsRot(P, 3, "pS")
    ps_nd = PsRot(P, 3, "pND")
    ps_r = PsRot(P, 2, "pR")

    for u in range(NU):
        P.dma("gpsimd", ve[:], v_d[u].rearrange("(t p) d -> p t d", p=128), writes=["ve"])
        P.dma("gpsimd", vo[:], v_d[u, 64:64 + 4096, :].rearrange("(t p) d -> p t d", p=128), writes=["vo"])
        P.dma("gpsimd", qc[:], qcT_d[u], writes=["qc"])
        P.dma("gpsimd", kc[:], kcT_d[u], writes=["kc"])
        P.dma("sync", bias[:], bias_d[u], writes=["bias"])
        for (src_d, dst, dk) in ((qT_d, qr, "qr"), (kT_d, kr, "kr")):
            qf, qk = qrot.next()
            P.dma("sync", qf[:], src_d[u], writes=[qk])
            for t in range(8):
                sl = slice(t * 512, (t + 1) * 512)
                qb, qbk = qbrot.next()
                P.op("scalar", lambda e, qb=qb, qf=qf, sl=sl: e.copy(qb[:], qf[:, sl]), reads=[qk], writes=[qbk])
                pt, pk = ps_r.next()
                P.op("tensor", lambda e, pt=pt, qb=qb: e.matmul(pt[:], rotb[:], qb[:], start=True, stop=True),
                     reads=[qbk, "rotb"], writes=[pk])
                t1, t1k = t1rot.next()
                t2, t2k = t2rot.next()
                P.op("vector", lambda e, t1=t1, qf=qf, sl=sl: e.tensor_tensor(t1[:], qf[:, sl], cos[:, sl], ALU.mult),
                     reads=[qk, "cos"], writes=[t1k])
                P.op("vector", lambda e, t2=t2, pt=pt, sl=sl: e.tensor_tensor(t2[:], pt[:], sin[:, sl], ALU.mult),
                     reads=[pk, "sin"], writes=[t2k])
                P.op("gpsimd", lambda e, t1=t1, t2=t2, dst=dst, sl=sl: e.tensor_tensor(dst[:, sl], t1[:], t2[:], ALU.add),
                     reads=[t1k, t2k], writes=[(dk, t)])
        qrk = [("qr", t) for t in range(8)]
        krk = [("kr", t) for t in range(8)]
        for r in range(64):
            r0 = min(max(r - 4, 0), 56)
            var = r - r0
            qs = slice(r * 64, (r + 1) * 64)
            ps, psk = ps_s.next()
            for kb in range(4):
                tok0 = (r0 + 2 * kb) * 64
                P.op("tensor", lambda e, ps=ps, kb=kb, tok0=tok0, qs=qs: e.matmul(
                    ps[:, kb * 64:(kb + 1) * 64], kr[:, tok0:tok0 + 128], qr[:, qs], start=True, stop=True),
                    reads=qrk + krk, writes=[psk], signal=False)
            for cb in range(2):
                P.op("tensor", lambda e, ps=ps, cb=cb, qs=qs: e.matmul(
                    ps[:, 256 + cb * 64:256 + (cb + 1) * 64], kc[:, cb * 128:(cb + 1) * 128], qr[:, qs],
                    start=True, stop=True), reads=qrk + ["kc"], writes=[psk], signal=(cb == 1))
            tb, tk = trot.next()
            P.op("vector", lambda e, tb=tb, ps=ps, var=var: e.scalar_tensor_tensor(
                tb[:], ps[:, 0:256], NA_SCALE, bias[:, var, :], ALU.mult, ALU.add),
                reads=[psk, "bias"], writes=[tk])
            eb, ek = erot.next()
            P.op("scalar", lambda e, eb=eb, tb=tb: e.activation(eb[:, 0:256], tb[:], AF.Exp),
                 reads=[tk], writes=[(ek, 0)])
            P.op("scalar", lambda e, eb=eb, ps=ps: e.activation(eb[:, 256:384], ps[:, 256:384], AF.Exp, scale=NA_SCALE),
                 reads=[psk], writes=[(ek, 1)])
            nd, ndk = ps_nd.next()
            for which in range(2):
                for blk in range(6):
                    if which == 0:
                        if blk < 4:
                            tok0 = (r0 + 2 * blk) * 64
                            lhs = ve[:, tok0 // 128, :] if tok0 % 128 == 0 else vo[:, (tok0 - 64) // 128, :]
                        else:
                            lhs = ve[:, 32 + blk - 4, :]
                    else:
                        lhs = ones[:]
                    P.op("tensor", lambda e, nd=nd, lhs=lhs, eb=eb, blk=blk, which=which: e.matmul(
                        nd[:, which * 64:(which + 1) * 64], lhs, eb[:, blk * 64:(blk + 1) * 64],
                        start=(blk == 0), stop=(blk == 5)),
                        reads=[(ek, 0), (ek, 1), "ve", "vo", "ones"], writes=[ndk],
                        signal=(which == 1 and blk == 5))
            rb, rk = rrot.next()
            P.op("vector", lambda e, rb=rb, nd=nd: e.reciprocal(rb[:, 0:64], nd[:, 64:128]), reads=[ndk], writes=[rk])
            P.op("vector", lambda e, rb=rb, nd=nd, qs=qs: e.tensor_tensor(oT[:, qs], nd[:, 0:64], rb[:, 0:64], ALU.mult),
                 reads=[ndk, rk], writes=["oT"])
        P.dma("sync", oT_d[u], oT[:], reads=["oT"], key=("dma_out", "oT"))
        ps, psk = ps_s.next()
        for cb in range(2):
            P.op("tensor", lambda e, ps=ps, cb=cb: e.matmul(
                ps[:, cb * 256:(cb + 1) * 256], kc[:, cb * 128:(cb + 1) * 128], qc[:], start=True, stop=True),
                reads=["qc", "kc"], writes=[psk], signal=(cb == 1))
        eb, ek = erot.next()
        P.op("scalar", lambda e, eb=eb, ps=ps: e.activation(eb[:], ps[:], AF.Exp, scale=NA_SCALE),
             reads=[psk], writes=[(ek, 0), (ek, 1)])
        nd, ndk = ps_nd.next()
        for which in range(2):
            for cb in range(2):
                lhs = ve[:, 32 + cb, :] if which == 0 else ones[:]
                P.op("tensor", lambda e, nd=nd, lhs=lhs, eb=eb, cb=cb, which=which: e.matmul(
                    nd[:, which * 256:(which + 1) * 256], lhs, eb[:, cb * 256:(cb + 1) * 256],
                    start=(cb == 0), stop=(cb == 1)),
                    reads=[(ek, 0), (ek, 1), "ve", "ones"], writes=[ndk], signal=(which == 1 and cb == 1))
        rb, rk = rrot.next()
        P.op("vector", lambda e, rb=rb, nd=nd: e.reciprocal(rb[:], nd[:, 256:512]), reads=[ndk], writes=[rk])
        P.op("vector", lambda e, rb=rb, nd=nd: e.tensor_tensor(ocT[:], nd[:, 0:256], rb[:], ALU.mult),
             reads=[ndk, rk], writes=["ocT"])
        P.dma("sync", ocT_d[u], ocT[:], reads=["ocT"], key=("dma_out", "ocT"))
    return P.build()


_PROGS = {}


def _prog(name, builder, *a):
    key = (name,) + a
    if key not in _PROGS:
        _PROGS[key] = builder(*a)
    return _PROGS[key]


_TRACE = False
_TIMES = []


def _launch(nc, in_maps):
    if _TRACE:
        res = run_bass_kernel_spmd(nc, in_maps, core_ids=list(range(8)), trace=True)
        _TIMES.append(res.exec_time_ns)
        print("LAUNCH exec_time_ns", res.exec_time_ns, flush=True)
    else:
        res = run_bass_kernel_spmd(nc, in_maps, core_ids=list(range(8)))
    return res.results


def _vec(v):
    v = np.asarray(v, np.float32)
    F = v.shape[-1]
    return np.ascontiguousarray(np.moveaxis(v.reshape(v.shape[:-1] + (F // 128, 128)), -1, 0))


def to_ts(lat, ctx):
    out = []
    F = lat.shape[-1]
    for c in range(8):
        b, q = c // 4, c % 4
        a = np.zeros((NT, F), np.float32)
        lo, hi = q * 64 - 1, q * 64 + 65
        s0, s1 = max(lo, 0), min(hi, 256)
        a[s0 - lo:s1 - lo] = ctx[b, s0:s1]
        lo, hi = q * 1024 - 1, q * 1024 + 1025
        s0, s1 = max(lo, 0), min(hi, 4096)
        a[66 + s0 - lo:66 + s1 - lo] = lat[b, s0:s1]
        out.append(np.ascontiguousarray(a.T))
    return out


def from_ts(outs, rows=None):
    F = outs[0].shape[0] if rows is None else rows[1] - rows[0]
    lat = np.empty((2, 4096, F), np.float32)
    ctx = np.empty((2, 256, F), np.float32)
    for c in range(8):
        b, q = c // 4, c % 4
        a = outs[c] if rows is None else outs[c][rows[0]:rows[1]]
        a = a.T
        ctx[b, q * 64:(q + 1) * 64] = a[1:65]
        lat[b, q * 1024:(q + 1) * 1024] = a[67:1091]
    return lat, ctx


def _hm(c):
    q = c % 4
    v = np.array([q > 0, q < 3, q > 0, q < 3], np.float32)
    return np.ascontiguousarray(np.tile(v, (128, 1)))


def run_mod(c, c_ctx, w_mod, b_mod):
    nc = _prog("mod", build_mod)
    cT = np.ascontiguousarray(np.transpose(_vec(np.stack([c[0], c[1], c_ctx])), (0, 2, 1)))
    maps = []
    for core in range(8):
        l, half = core // 2, core % 2
        maps.append(dict(wm=np.ascontiguousarray(w_mod[l][:, half * 6144:(half + 1) * 6144]),
                         bm=_vec(b_mod[l][half * 6144:(half + 1) * 6144]), cT=cT))
    res = _launch(nc, maps)
    mod_full = np.empty((4, 3, 12288), np.float32)
    for core in range(8):
        l, half = core // 2, core % 2
        mo = res[core]["modo"]
        mod_full[l][:, half * 6144:(half + 1) * 6144] = np.transpose(mo, (2, 1, 0)).reshape(3, 6144)
    return mod_full


def _mod_in(mod_full, i, core):
    b = core // 4
    m = mod_full[i].reshape(3, 6, 2048)
    arr = np.stack([m[2], m[b]], axis=1)
    return _vec(arr)


def _common(mod_full, norm_g, i):
    return [dict(mod=_mod_in(mod_full, i, c), ng=_vec(norm_g[i]), hm=_hm(c)) for c in range(8)]


def run_ffn(x_lat, x_ctx, com, w_up, conv_w, conv_b, w_down):
    nc = _prog("ffn", build_ffn)
    xs = to_ts(x_lat, x_ctx)
    cw, cb = _vec(conv_w), _vec(conv_b)
    maps = [dict(xT=xs[c], w_up=w_up, w_down=w_down, cw=cw, cb=cb, **com[c]) for c in range(8)]
    res = _launch(nc, maps)
    return from_ts([r["outT"] for r in res])


def _na_tables():
    t = np.arange(4096)
    row, col = (t // 64).astype(np.float32), (t % 64).astype(np.float32)
    inv = (np.float32(10000.0) ** (-np.arange(32, dtype=np.float32) / np.float32(32))).astype(np.float32)
    ang = np.empty((128, 4096), np.float32)
    for d in range(128):
        pos = row if d < 64 else col
        ang[d] = pos * inv[d % 32]
    rotT = np.zeros((128, 128), np.float32)
    for d in range(128):
        if d % 64 < 32:
            rotT[d + 32, d] = -1.0
        else:
            rotT[d - 32, d] = 1.0
    return np.cos(ang).astype(np.float32), np.sin(ang).astype(np.float32), rotT


def _na_bias(rpb):
    kk = np.arange(128)[:, None, None, None]
    var = np.arange(8)[None, :, None, None]
    kb = np.arange(4)[None, None, :, None]
    q = np.arange(64)[None, None, None, :]
    w = 2 * kb + kk // 64
    kcol = kk % 64
    dr = w - var + 7
    dc = np.clip(kcol - q + 15, 0, 30)
    c0 = np.clip(q - 8, 0, 48)
    col_in = (kcol >= c0) & (kcol < c0 + 16)
    dr_b, dc_b, in_b = np.broadcast_arrays(dr, dc, col_in)
    g = rpb[:, dr_b, dc_b]
    g = np.where(in_b[None], g, np.float32(-30000.0)).astype(np.float32)
    return np.ascontiguousarray(g.reshape(16, 128, 8, 256))


def run_na_core(qkv_lat, qkv_ctx, rpb):
    nc = _prog("us_na", build_us_na)
    cos, sin, rotT = _na_tables()
    biasT = _na_bias(rpb)
    maps = []
    for c in range(8):
        b, h0 = c // 4, 4 * (c % 4)
        hs = range(h0, h0 + 4)

        def cols(arr, off):
            return np.ascontiguousarray(np.stack([arr[b][:, off + h * 128:off + (h + 1) * 128].T for h in hs]))
        v = np.stack([np.concatenate([qkv_lat[b][:, 4096 + h * 128:4096 + (h + 1) * 128],
                                      qkv_ctx[b][:, 4096 + h * 128:4096 + (h + 1) * 128]], axis=0) for h in hs])
        maps.append(dict(qT=cols(qkv_lat, 0), kT=cols(qkv_lat, 2048), qcT=cols(qkv_ctx, 0), kcT=cols(qkv_ctx, 2048),
                         v=np.ascontiguousarray(v), biasT=np.ascontiguousarray(biasT[h0:h0 + 4]),
                         cos=cos, sin=sin, rotT=rotT))
    res = _launch(nc, maps)
    o_lat = np.empty((2, 4096, 2048), np.float32)
    o_ctx = np.empty((2, 256, 2048), np.float32)
    for c in range(8):
        b, h0 = c // 4, 4 * (c % 4)
        for u in range(4):
            h = h0 + u
            o_lat[b][:, h * 128:(h + 1) * 128] = res[c]["oT"][u].T
            o_ctx[b][:, h * 128:(h + 1) * 128] = res[c]["ocT"][u].T
    return o_lat, o_ctx


def layer_na(x_lat, x_ctx, com, w_qkv, rpb, w_out):
    xs = to_ts(x_lat, x_ctx)
    nc1 = _prog("ts1", build_ts1, "na")
    res = _launch(nc1, [dict(xT=xs[c], W=w_qkv, **com[c]) for c in range(8)])
    qkv_lat, qkv_ctx = from_ts([r["outT"] for r in res])
    o_lat, o_ctx = run_na_core(qkv_lat, qkv_ctx, rpb)
    os_ = to_ts(o_lat, o_ctx)
    nc2 = _prog("ts2", build_ts2, "na")
    res = _launch(nc2, [dict(xT=xs[c], oT=os_[c], W=w_out, **com[c]) for c in range(8)])
    return from_ts([r["outT"] for r in res])


def run_mixer_layer(i, x_lat, x_ctx, com, p):
    kind, slot = i % 3, i // 3
    if kind == 2:
        return layer_na(x_lat, x_ctx, com, p["na_w_qkv"][slot], p["na_rpb"][slot], p["na_w_out"][slot])
    if kind == 1:
        return layer_hg(i, x_lat, x_ctx, com, p["hg_w_in"][slot], p["hg_lb"], p["hg_norm_g"][slot], p["hg_w_out"][slot])
    return layer_ssd(x_lat, x_ctx, com, p["ssm_w_in"][slot], p["ssm_conv_w"][slot], p["ssm_conv_b"][slot],
                     p["ssm_dt_bias"][slot], p["ssm_a_log"][slot], p["ssm_d"][slot], p["ssm_norm_g"][slot],
                     p["ssm_w_out"][slot])


SEQ_ALL = 4352
NTILE = SEQ_ALL // 128


def build_us_hg(NU=8, ntile=NTILE):
    P = Prog()
    L = ntile * 128
    qT_d = P.dram_in("qT", [NU, 128, L])
    gT_d = P.dram_in("gT", [NU, 128, L])
    gtok_d = P.dram_in("gtok", [NU, L, 128])
    v_d = P.dram_in("v", [NU, L, 128])
    BT = load_small(P, "BT", [128, 128])
    U = load_small(P, "U", [128, 128])
    ind = load_small(P, "ind", [128, 4])
    oT_d = P.dram_out("oT", [NU, 128, L])
    S = [P.sbuf([128, 128], F32, name=f"S{u}") for u in range(NU)]
    Sb = [P.sbuf([128, 128], BF16, name=f"Sb{u}") for u in range(NU)]
    for u in range(NU):
        P.op("vector", lambda e, u=u: e.memset(S[u][:], 0.0), writes=[("S", u)])
        P.op("vector", lambda e, u=u: e.memset(Sb[u][:], 0.0), writes=[("Sb", u)])
    R = lambda n, dt, nm, cols=128: Rot(P, n, [128, cols], dt, nm)
    q_r, g_r, gt_r = R(4, F32, "q"), R(4, F32, "g"), R(4, F32, "gt")
    v_r = R(10, BF16, "v")
    e1_r, e2_r, e3_r = R(10, F32, "e1"), R(3, F32, "e2"), R(3, F32, "e3")
    kT_r, kt_r = R(3, F32, "kT"), R(3, F32, "ktok")
    qt_r, ktl_r = R(10, BF16, "qt"), R(3, BF16, "ktl")
    kh_r = R(3, F32, "kh")
    khm_r = R(10, BF16, "khm", 512)
    at_r = R(10, BF16, "attn")
    o_r = R(6, F32, "o")
    pAB = PsRot(P, 2, "pAB")
    pD = PsRot(P, 4, "pD")
    pE = PsRot(P, 2, "pE")

    def stage_a(t, u):
        ts = slice(t * 128, (t + 1) * 128)
        qs, qk = q_r.next()
        gs, gk = g_r.next()
        gts, gtk = gt_r.next()
        vs, vk = v_r.next()
        P.dma("sync", qs[:], qT_d[u][:, ts], writes=[qk])
        P.dma("sync", gs[:], gT_d[u][:, ts], writes=[gk])
        P.dma("sync", gts[:], gtok_d[u][ts, :], writes=[gtk])
        P.dma("gpsimd", vs[:], v_d[u][ts, :], writes=[vk])
        ab, abk = pAB.next()
        P.op("tensor", lambda e: e.matmul(ab[:, 0:128], gts[:], BT[:], start=True, stop=True),
             reads=[gtk, "BT"], writes=[abk])
        P.op("tensor", lambda e: e.matmul(ab[:, 128:256], U[:], gts[:], start=True, stop=True),
             reads=[gtk, "U"], writes=[abk])
        e1, e1k = e1_r.next()
        e2, e2k = e2_r.next()
        e3, e3k = e3_r.next()
        P.op("scalar", lambda e: e.activation(e1[:], ab[:, 0:128], AF.Exp), reads=[abk], writes=[e1k])
        P.op("scalar", lambda e: e.activation(e2[:], ab[:, 0:128], AF.Exp, scale=-1.0), reads=[abk], writes=[e2k])
        P.op("scalar", lambda e: e.activation(e3[:], ab[:, 128:256], AF.Exp), reads=[abk], writes=[e3k])
        kT, kTk = kT_r.next()
        kt, ktk = kt_r.next()
        P.op("scalar", lambda e: e.activation(kT[:], gs[:], AF.Exp), reads=[gk], writes=[kTk])
        P.op("scalar", lambda e: e.activation(kt[:], gts[:], AF.Exp), reads=[gtk], writes=[ktk])
        P.op("gpsimd", lambda e: e.tensor_scalar(kT[:], kT[:], -1.0, 1.0, ALU.mult, ALU.add), reads=[kTk], writes=[kTk])
        P.op("gpsimd", lambda e: e.tensor_scalar(kt[:], kt[:], -1.0, 1.0, ALU.mult, ALU.add), reads=[ktk], writes=[ktk])
        qt, qtk = qt_r.next()
        ktl, ktlk = ktl_r.next()
        kh, khk = kh_r.next()
        khm, khmk = khm_r.next()
        P.op("vector", lambda e: e.tensor_tensor(qt[:], qs[:], e1[:], ALU.mult), reads=[qk, e1k], writes=[qtk])
        P.op("vector", lambda e: e.tensor_tensor(ktl[:], kT[:], e2[:], ALU.mult), reads=[kTk, e2k], writes=[ktlk])
        P.op("gpsimd", lambda e: e.tensor_tensor(kh[:], kt[:], e3[:], ALU.mult), reads=[ktk, e3k], writes=[khk])
        for I in range(4):
            P.op("gpsimd", lambda e, I=I: e.tensor_scalar(khm[:, I * 128:(I + 1) * 128], kh[:], ind[:, I:I + 1], None, ALU.mult),
                 reads=[khk, "ind"], writes=[(khmk, I)])
        P.op("tensor", lambda e: e.matmul(ab[:, 256:384], ktl[:], qt[:], start=True, stop=True),
             reads=[ktlk, qtk, e1k, e2k, e3k], writes=[abk])
        at, atk = at_r.next()
        P.op("vector", lambda e: e.tensor_tensor(at[:], ab[:, 256:384], BT[:], ALU.mult), reads=[abk, "BT"], writes=[atk])
        return dict(t=t, u=u, ts=ts, vs=vs, vk=vk, e1=e1, e1k=e1k, qt=qt, qtk=qtk, khm=khm, khmk=khmk, at=at, atk=atk)

    def stage_b(group):
        for c in group:
            c["pd"], c["pdk"] = pD.next()
            P.op("tensor", lambda e, c=c: e.matmul(c["pd"][:, 0:128], c["vs"][:], c["at"][:], start=True, stop=False),
                 reads=[c["vk"], c["atk"]], writes=[c["pdk"]])
        for I in range(4):
            for c in group:
                u = c["u"]
                P.op("tensor", lambda e, c=c, u=u, I=I: e.matmul(
                    c["pd"][:, I * 32:(I + 1) * 32], Sb[u][:], c["qt"][:, I * 32:(I + 1) * 32], start=False, stop=(I == 3)),
                    reads=[("Sb", u), c["qtk"]], writes=[c["pdk"]])
                pe, pek = pE.next()
                P.op("tensor", lambda e, c=c, pe=pe, I=I: e.matmul(
                    pe[:, 0:128], c["khm"][:, I * 128:(I + 1) * 128], c["vs"][:], start=True, stop=True),
                    reads=[(c["khmk"], I), c["vk"]], writes=[pek])
                P.op("vector", lambda e, c=c, u=u, pe=pe, I=I: e.scalar_tensor_tensor(
                    S[u][:], S[u][:], c["e1"][:, I * 32 + 31:I * 32 + 32], pe[:, 0:128], ALU.mult, ALU.add),
                    reads=[("S", u), c["e1k"], pek], writes=[("S", u)])
                P.op("scalar", lambda e, u=u: e.copy(Sb[u][:], S[u][:]), reads=[("S", u)], writes=[("Sb", u)])
        for c in group:
            ob, obk = o_r.next()
            P.op("scalar", lambda e, c=c, ob=ob: e.copy(ob[:], c["pd"][:, 0:128]), reads=[c["pdk"]], writes=[obk])
            P.dma("sync", oT_d[c["u"]][:, c["ts"]], ob[:], reads=[obk], key=("dma_out", obk))

    G = 4
    groups = [[(t, u) for u in range(g0, min(g0 + G, NU))] for t in range(ntile) for g0 in range(0, NU, G)]
    prev = None
    for grp in groups:
        cur = [stage_a(t, u) for (t, u) in grp]
        if prev is not None:
            stage_b(prev)
        prev = cur
    stage_b(prev)
    return P.build()


def _hg_consts():
    j = np.arange(128)[:, None]
    i = np.arange(128)[None, :]
    same = (j // 32) == (i // 32)
    BT = (same & (j <= i)).astype(np.float32)
    U = (same & (j > i)).astype(np.float32)
    ind = (np.arange(128)[:, None] // 32 == np.arange(4)[None, :]).astype(np.float32)
    return BT, U, ind


def run_hg_core(lat, ctx):
    nc = _prog("us_hg", build_us_hg)
    BT, U, ind = _hg_consts()
    maps = []
    for c in range(8):
        b, h0 = c // 4, 4 * (c % 4)
        qT, gT, gtok, v = [], [], [], []
        for hh in range(4):
            h = h0 + hh
            for d in range(2):
                def seq(off):
                    cc, ll = ctx[b][:, off:off + 128], lat[b][:, off:off + 128]
                    if d == 1:
                        cc, ll = cc[::-1], ll[::-1]
                    return np.concatenate([cc, ll], axis=0)
                q_s, v_s, g_s = seq(h * 128), seq(2048 + h * 128), seq(4096 + d * 2048 + h * 128)
                qT.append(q_s.T)
                gT.append(g_s.T)
                gtok.append(g_s)
                v.append(v_s)
        maps.append(dict(qT=np.ascontiguousarray(np.stack(qT)), gT=np.ascontiguousarray(np.stack(gT)),
                         gtok=np.ascontiguousarray(np.stack(gtok)), v=np.ascontiguousarray(np.stack(v)),
                         BT=BT, U=U, ind=ind))
    res = _launch(nc, maps)
    o_lat = np.empty((2, 2, 4096, 2048), np.float32)
    o_ctx = np.empty((2, 2, 256, 2048), np.float32)
    for c in range(8):
        b, h0 = c // 4, 4 * (c % 4)
        for hh in range(4):
            h = h0 + hh
            for d in range(2):
                o = res[c]["oT"][hh * 2 + d].T
                oc, ol = o[:256], o[256:]
                if d == 1:
                    oc, ol = oc[::-1], ol[::-1]
                o_lat[d, b][:, h * 128:(h + 1) * 128] = ol
                o_ctx[d, b][:, h * 128:(h + 1) * 128] = oc
    return o_lat, o_ctx


def layer_hg(i, x_lat, x_ctx, com, w_in, hg_lb, norm_g_h, w_out):
    xs = to_ts(x_lat, x_ctx)
    nc1 = _prog("ts1", build_ts1, "hg", i)
    lbp = _vec(hg_lb)
    res1 = _launch(nc1, [dict(xT=xs[c], W=w_in, lbp=lbp, **com[c]) for c in range(8)])
    lat, ctx = from_ts([r["outT"] for r in res1])
    o_lat, o_ctx = run_hg_core(lat, ctx)
    oF = to_ts(o_lat[0], o_ctx[0])
    oB = to_ts(o_lat[1], o_ctx[1])
    nc2 = _prog("ts2", build_ts2, "hg")
    hgn = _vec(norm_g_h)
    res = _launch(nc2, [dict(xT=xs[c], oF=oF[c], oB=oB[c], gsT=np.ascontiguousarray(res1[c]["outT"][8192:10240]),
                             hgn=hgn, W=w_out, **com[c]) for c in range(8)])
    return from_ts([r["outT"] for r in res])


SSD_POOL = "gpsimd"


def build_us_ssd(NU=4, ntile=NTILE):
    P = Prog()
    L = ntile * 128
    x_d = P.dram_in("x", [NU, L, 512])
    Btok_d = P.dram_in("Btok", [NU, L, 128])
    BT_d = P.dram_in("BT", [NU, 128, L])
    CT_d = P.dram_in("CT", [NU, 128, L])
    a_d = P.dram_in("a", [NU, L, 8])
    dt_d = P.dram_in("dt", [NU, L, 8])
    T = load_small(P, "T", [128, 128])
    Mneg4 = load_small(P, "Mneg4", [128, 512])
    y_d = P.dram_out("y", [NU, L, 512])
    S = [P.sbuf([128, 512], F32, name=f"S{u}") for u in range(NU)]
    Sb = [P.sbuf([128, 512], BF16, name=f"Sb{u}") for u in range(NU)]
    for u in range(NU):
        P.op("vector", lambda e, u=u: e.memset(S[u][:], 0.0), writes=[("S", u)])
        P.op("vector", lambda e, u=u: e.memset(Sb[u][:], 0.0), writes=[("Sb", u)])
    R = lambda n, dt, nm, cols=128: Rot(P, n, [128, cols], dt, nm)
    x_r = R(3, BF16, "x", 512)
    bt_r, BT_r, CT_r = R(3, BF16, "btok"), R(3, BF16, "BTf"), R(3, BF16, "CTf")
    a_r, dt_r = R(3, F32, "a", 8), R(3, F32, "dt", 8)
    abc_r = R(2, F32, "abc", 1024)
    na_r = R(3, F32, "nacum", 8)
    ln_r = R(2, F32, "lnd", 8)
    cb_r = R(3, F32, "cbs4", 512)
    tm_r, e_r, e2_r = R(3, F32, "tm", 512), R(3, F32, "E", 512), R(4, F32, "E2", 512)
    st_r, ce_r = R(3, BF16, "ST", 512), R(3, BF16, "Cexp", 512)
    ts_r = R(2, F32, "tmpS", 256)
    xw_r = R(2, BF16, "xw", 512)
    y_r = R(2, F32, "ysb", 512)
    pAC = PsRot(P, 1, "pAC")
    pB = PsRot(P, 4, "pB")
    pY = PsRot(P, 2, "pY")
    pS = PsRot(P, 1, "pS")
    v4 = lambda ap, m: ap.rearrange("p (h m) -> p h m", h=4)

    def stage_a(t, u):
        ts = slice(t * 128, (t + 1) * 128)
        xs, xk = x_r.next()
        bts, btk = bt_r.next()
        BTs, BTk = BT_r.next()
        CTs, CTk = CT_r.next()
        as_, ak = a_r.next()
        dts, dtk = dt_r.next()
        P.dma("gpsimd", xs[:], x_d[u][ts, :], writes=[xk])
        P.dma("gpsimd", bts[:], Btok_d[u][ts, :], writes=[btk])
        P.dma("gpsimd", BTs[:], BT_d[u][:, ts], writes=[BTk])
        P.dma("gpsimd", CTs[:], CT_d[u][:, ts], writes=[CTk])
        P.dma("sync", as_[:], a_d[u][ts, :], writes=[ak])
        P.dma("sync", dts[:], dt_d[u][ts, :], writes=[dtk])
        pac, pack = pAC.next()
        P.op("tensor", lambda e, pac=pac, BTs=BTs, CTs=CTs: e.matmul(pac[:, 0:128], BTs[:], CTs[:], start=True, stop=True),
             reads=[BTk, CTk], writes=[pack])
        P.op("tensor", lambda e, pac=pac, as_=as_: e.matmul(pac[:, 128:136], T[:], as_[:], start=True, stop=True),
             reads=[ak, "T"], writes=[pack])
        abc, abck = abc_r.next()
        P.op("vector", lambda e, abc=abc, as_=as_: e.tensor_copy(
            abc[:].rearrange("p (h m) -> p h m", h=8), as_[:].unsqueeze(2).to_broadcast([128, 8, 128])),
            reads=[ak], writes=[abck])
        pbs = []
        for half in range(2):
            pb, pbk = pB.next()
            for hh in range(4):
                h = half * 4 + hh
                P.op("tensor", lambda e, pb=pb, abc=abc, h=h, hh=hh: e.matmul(
                    pb[:, hh * 128:(hh + 1) * 128], abc[:, h * 128:(h + 1) * 128], T[:], start=True, stop=True),
                    reads=[abck, "T"], writes=[pbk], signal=(hh == 3))
            pbs.append((pb, pbk))
        lnd, lndk = ln_r.next()
        P.op("scalar", lambda e, lnd=lnd, dts=dts: e.activation(lnd[:], dts[:], AF.Ln), reads=[dtk], writes=[lndk])
        nac, nack = na_r.next()
        P.op("vector", lambda e, nac=nac, pac=pac, lnd=lnd: e.scalar_tensor_tensor(
            nac[:], pac[:, 128:136], -1.0, lnd[:], ALU.mult, ALU.add), reads=[pack, lndk], writes=[nack])
        cbs, cbk = cb_r.next()
        for hh in range(4):
            P.op("scalar", lambda e, cbs=cbs, pac=pac, hh=hh: e.copy(cbs[:, hh * 128:(hh + 1) * 128], pac[:, 0:128]),
                 reads=[pack], writes=[cbk])
        return dict(t=t, u=u, ts=ts, xs=xs, xk=xk, bts=bts, btk=btk, CTs=CTs, CTk=CTk, pbs=pbs, nac=nac, nack=nack,
                    cbs=cbs, cbk=cbk)

    def stage_b(c):
        t, u, ts, xs, xk, bts, btk, CTs, CTk, pbs, nac, nack, cbs, cbk = (
            c[k] for k in ("t", "u", "ts", "xs", "xk", "bts", "btk", "CTs", "CTk", "pbs", "nac", "nack", "cbs", "cbk"))
        xw, xwk = xw_r.next()
        py, pyk = pY.next()
        e2s = []
        for half in range(2):
            pb, pbk = pbs[half]
            hsl = slice(half * 256, (half + 1) * 256)
            tm, tmk = tm_r.next()
            P.op("vector", lambda e, tm=tm, pb=pb, nac=nac, half=half: e.tensor_tensor(
                v4(tm[:], 128), v4(pb[:], 128),
                nac[:, 4 * half:4 * half + 4].unsqueeze(2).to_broadcast([128, 4, 128]), ALU.add),
                reads=[pbk, nack], writes=[tmk])
            P.op(SSD_POOL, lambda e, tm=tm: e.tensor_tensor(tm[:], tm[:], Mneg4[:], ALU.add),
                 reads=[tmk, "Mneg4"], writes=[tmk])
            E, Ek = e_r.next()
            P.op("scalar", lambda e, E=E, tm=tm: e.activation(E[:], tm[:], AF.Exp), reads=[tmk], writes=[Ek])
            ST, STk = st_r.next()
            P.op(SSD_POOL, lambda e, ST=ST, E=E, cbs=cbs: e.tensor_tensor(ST[:], E[:], cbs[:], ALU.mult),
                 reads=[Ek, cbk], writes=[STk])
            E2, E2k = e2_r.next()
            P.op("scalar", lambda e, E2=E2, pb=pb: e.activation(E2[:], pb[:], AF.Exp), reads=[pbk], writes=[E2k])
            e2s.append((E2, E2k))
            Ce, Cek = ce_r.next()
            P.op("vector", lambda e, Ce=Ce, CTs=CTs, E2=E2: e.tensor_tensor(
                v4(Ce[:], 128), v4(E2[:], 128), CTs[:].unsqueeze(1).to_broadcast([128, 4, 128]), ALU.mult),
                reads=[CTk, E2k], writes=[Cek])
            P.op("vector", lambda e, xw=xw, xs=xs, E=E, hsl=hsl: e.tensor_tensor(
                xw[:, hsl].rearrange("p (h m) -> p h m", h=4), xs[:, hsl].rearrange("p (h m) -> p h m", h=4),
                v4(E[:], 128)[:, :, 127:128].to_broadcast([128, 4, 64]), ALU.mult),
                reads=[xk, Ek], writes=[(xwk, half)])
            for hh in range(4):
                h = half * 4 + hh
                hs = slice(h * 64, (h + 1) * 64)
                P.op("tensor", lambda e, py=py, ST=ST, xs=xs, hs=hs, hh=hh: e.matmul(
                    py[:, hs], ST[:, hh * 128:(hh + 1) * 128], xs[:, hs], start=True, stop=False),
                    reads=[STk, xk], writes=[pyk], signal=False)
                P.op("tensor", lambda e, py=py, Ce=Ce, u=u, hs=hs, hh=hh: e.matmul(
                    py[:, hs], Ce[:, hh * 128:(hh + 1) * 128], Sb[u][:, hs], start=False, stop=True),
                    reads=[Cek, ("Sb", u)], writes=[pyk], signal=(hh == 3))
        ysb, yk = y_r.next()
        P.op("scalar", lambda e, ysb=ysb, py=py: e.copy(ysb[:], py[:]), reads=[pyk], writes=[yk])
        P.dma("sync", y_d[u][ts, :], ysb[:], reads=[yk], key=("dma_out", yk))
        ps, psk = pS.next()
        P.op("tensor", lambda e, ps=ps, bts=bts, xw=xw: e.matmul(ps[:], bts[:], xw[:], start=True, stop=True),
             reads=[btk, (xwk, 0), (xwk, 1)], writes=[psk])
        for half in range(2):
            hsl = slice(half * 256, (half + 1) * 256)
            E2, E2k = e2s[half]
            tS, tSk = ts_r.next()
            P.op("vector", lambda e, tS=tS, u=u, E2=E2, hsl=hsl: e.tensor_tensor(
                tS[:].rearrange("p (h m) -> p h m", h=4), S[u][:, hsl].rearrange("p (h m) -> p h m", h=4),
                v4(E2[:], 128)[:, :, 127:128].to_broadcast([128, 4, 64]), ALU.mult),
                reads=[("S", u), E2k], writes=[tSk])
            P.op("vector", lambda e, tS=tS, u=u, ps=ps, hsl=hsl: e.tensor_tensor(
                S[u][:, hsl], tS[:], ps[:, hsl], ALU.add), reads=[tSk, psk], writes=[("S", u)])
        P.op("scalar", lambda e, u=u: e.copy(Sb[u][:], S[u][:]), reads=[("S", u)], writes=[("Sb", u)])

    steps = [(t, u) for t in range(ntile) for u in range(NU)]
    prev = None
    for (t, u) in steps:
        cur = stage_a(t, u)
        if prev is not None:
            stage_b(prev)
        prev = cur
    stage_b(prev)
    return P.build()


def _ssd_consts():
    j = np.arange(128)[:, None]
    i = np.arange(128)[None, :]
    T = (j <= i).astype(np.float32)
    Mneg = np.where(j <= i, np.float32(0.0), np.float32(-30000.0)).astype(np.float32)
    return T, np.ascontiguousarray(np.tile(Mneg, (1, 4)))


def run_ssd_core(lat, ctx):
    nc = _prog("us_ssd", build_us_ssd)
    T, Mneg4 = _ssd_consts()
    maps = []
    for c in range(8):
        b, g0 = c // 4, 2 * (c % 4)
        x, Btok, BT, CT, a, dt = [], [], [], [], [], []
        for gg in range(2):
            g = g0 + gg
            for d in range(2):
                def seq(lo, hi):
                    cc, ll = ctx[b][:, lo:hi], lat[b][:, lo:hi]
                    if d == 1:
                        cc, ll = cc[::-1], ll[::-1]
                    return np.concatenate([cc, ll], axis=0)
                x.append(seq(4096 + g * 512, 4096 + (g + 1) * 512))
                Bs = seq(8192 + g * 128, 8192 + (g + 1) * 128)
                Cs = seq(9216 + g * 128, 9216 + (g + 1) * 128)
                Btok.append(Bs)
                BT.append(Bs.T)
                CT.append(Cs.T)
                dt.append(seq(10240 + d * 64 + g * 8, 10240 + d * 64 + (g + 1) * 8))
                a.append(seq(10368 + d * 64 + g * 8, 10368 + d * 64 + (g + 1) * 8))
        st = lambda l: np.ascontiguousarray(np.stack(l))
        maps.append(dict(x=st(x), Btok=st(Btok), BT=st(BT), CT=st(CT), a=st(a), dt=st(dt), T=T, Mneg4=Mneg4))
    res = _launch(nc, maps)
    y_lat = np.empty((2, 2, 4096, 4096), np.float32)
    y_ctx = np.empty((2, 2, 256, 4096), np.float32)
    for c in range(8):
        b, g0 = c // 4, 2 * (c % 4)
        for gg in range(2):
            g = g0 + gg
            for d in range(2):
                y = res[c]["y"][gg * 2 + d]
                yc, yl = y[:256], y[256:]
                if d == 1:
                    yc, yl = yc[::-1], yl[::-1]
                y_lat[d, b][:, g * 512:(g + 1) * 512] = yl
                y_ctx[d, b][:, g * 512:(g + 1) * 512] = yc
    return y_lat, y_ctx


def layer_ssd(x_lat, x_ctx, com, w_in, conv_w, conv_b, dt_bias, a_log, d_skip, norm_g_s, w_out):
    xs = to_ts(x_lat, x_ctx)
    nc1 = _prog("ts1", build_ts1, "ssd")
    cw, cb = _vec(conv_w), _vec(conv_b)
    dtb = np.ascontiguousarray(dt_bias.reshape(128, 1).astype(np.float32))
    alog = np.ascontiguousarray(a_log.reshape(128, 1).astype(np.float32))
    res1 = _launch(nc1, [dict(xT=xs[c], W=w_in, cw=cw, cb=cb, dtb=dtb, alog=alog, **com[c]) for c in range(8)])
    lat, ctx = from_ts([r["outT"] for r in res1])
    y_lat, y_ctx = run_ssd_core(lat, ctx)
    yF = to_ts(y_lat[0], y_ctx[0])
    yB = to_ts(y_lat[1], y_ctx[1])
    dsk = _vec(np.repeat(d_skip, 64, axis=1))
    sng = _vec(norm_g_s)
    nc2 = _prog("ts2", build_ts2, "ssd")
    res = _launch(nc2, [dict(xT=xs[c], yF=yF[c], yB=yB[c], xcT=np.ascontiguousarray(res1[c]["outT"][4096:8192]),
                             zsT=np.ascontiguousarray(res1[c]["outT"][0:4096]), dsk=dsk, sng=sng, W=w_out, **com[c])
                      for c in range(8)])
    return from_ts([r["outT"] for r in res])


def kernel(**inputs):
    p = {k: np.asarray(v) for k, v in inputs.items()}
    mod_full = run_mod(p["c"], p["c_ctx"], p["w_mod"], p["b_mod"])
    x_lat = np.asarray(p["x"], np.float32)
    x_ctx = np.asarray(p["ctx"], np.float32)
    for i in range(4):
        com = _common(mod_full, p["norm_g"], i)
        x_lat, x_ctx = run_mixer_layer(i, x_lat, x_ctx, com, p)
        x_lat, x_ctx = run_ffn(x_lat, x_ctx, com, p["ffn_w_up"][i], p["ffn_conv_w"][i], p["ffn_conv_b"][i],
                               p["ffn_w_down"][i])
    return np.ascontiguousarray(x_lat.astype(np.float32))
```
